# Optimizing a Trainium2 kernel written in Bass

```python
import jax, jax.numpy as jnp
from jax import lax
import numpy as np

D_MODEL = 2048
BATCH = 2
SEQ = 4096
DEPTH = 1
DEC_BATCH = 16
DEC_SEQ = 32
PAST_LEN = 4096

CHUNK = 64
Q_BLOCK = 128
EPS = 1e-6
POOL_WINDOWS = (2, 4, 8, 16)
N_POOL_GROUPS = 4
POOL_W = D_MODEL // 2
POOL_GW = POOL_W // N_POOL_GROUPS
POOL_OUT_GW = D_MODEL // N_POOL_GROUPS
POOL_STATE = max(POOL_WINDOWS) - 1
MLA_HEADS = 16
Q_LORA = 512
KV_LORA = 512
QK_NOPE = 128
QK_ROPE = 64
QK_HEAD = QK_NOPE + QK_ROPE
V_HEAD = 128
ROPE_THETA = 10000.0
MLA_SCALE = QK_HEAD ** -0.5
N_MEM = 256
MEM_HEADS = 4
MEM_HEAD_DIM = 256
MEM_W = MEM_HEADS * MEM_HEAD_DIM
MEM_SCALE = MEM_HEAD_DIM ** -0.5
N_BRANCH = 3
D_FF = 4 * D_MODEL
D_IN = POOL_W + Q_LORA + KV_LORA + QK_ROPE + MEM_W + N_BRANCH * D_MODEL

kernel_name = "hybrid_pool_mla_memory_stream_step"


def rmsnorm(x, g):
    xf = x.astype(jnp.float32)
    y = xf * lax.rsqrt(jnp.mean(xf * xf, axis=-1, keepdims=True) + EPS)
    return (y * g.astype(jnp.float32)).astype(x.dtype)


def rope(x, pos):
    half = QK_ROPE // 2
    inv = 1.0 / (ROPE_THETA ** (jnp.arange(half, dtype=jnp.float32) * (2.0 / QK_ROPE)))
    ang = pos.astype(jnp.float32)[:, None] * inv[None, :]
    cos = jnp.cos(ang)[:, None, :]
    sin = jnp.sin(ang)[:, None, :]
    xf = x.astype(jnp.float32)
    x1, x2 = xf[..., :half], xf[..., half:]
    out = jnp.concatenate([x1 * cos - x2 * sin, x2 * cos + x1 * sin], axis=-1)
    return out.astype(x.dtype)


def attend(q, k, v, scale, mask=None):
    s = jnp.einsum('bqhd,bkhd->bhqk', q, k).astype(jnp.float32) * scale
    if mask is not None:
        s = jnp.where(mask, s, -1e30)
    p = jax.nn.softmax(s, axis=-1).astype(v.dtype)
    return jnp.einsum('bhqk,bkhd->bqhd', p, v)


def chunk_attend(q, q_pos, k, v, k_pos):
    mask = (q_pos // CHUNK)[:, None] >= (k_pos // CHUNK)[None, :]
    return attend(q, k, v, MLA_SCALE, mask)


def split_in(z):
    sizes = (POOL_W, Q_LORA, KV_LORA, QK_ROPE, MEM_W, N_BRANCH * D_MODEL)
    parts, off = [], 0
    for s in sizes:
        parts.append(z[..., off:off + s])
        off += s
    return parts


def pool_branch(u, left, pos, w_pool, pool_scale):
    B, L, _ = u.shape
    xp = jnp.concatenate([left.astype(jnp.float32), u.astype(jnp.float32)], axis=1)
    cs = jnp.concatenate([jnp.zeros((B, 1, POOL_W), jnp.float32), jnp.cumsum(xp, axis=1)], axis=1)
    top = cs[:, POOL_STATE + 1:]
    means = []
    for g, win in enumerate(POOL_WINDOWS):
        sl = slice(g * POOL_GW, (g + 1) * POOL_GW)
        lo = POOL_STATE + 1 - win
        wsum = top[..., sl] - cs[:, lo:lo + L, sl]
        cnt = jnp.minimum(pos + 1, win).astype(jnp.float32)[None, :, None]
        means.append(wsum / cnt)
    d = jnp.concatenate(means, axis=-1) - u.astype(jnp.float32)
    d = d.astype(u.dtype).reshape(B, L, N_POOL_GROUPS, POOL_GW)
    y = jnp.einsum('blgc,gco->blgo', d, w_pool).reshape(B, L, D_MODEL)
    new_state = xp[:, -POOL_STATE:].astype(u.dtype)
    return y * pool_scale, new_state


def mla_queries(q_lat, pos, w):
    B, L, _ = q_lat.shape
    q = (rmsnorm(q_lat, w['g_q_lat']) @ w['w_qb']).reshape(B, L, MLA_HEADS, QK_HEAD)
    q = rmsnorm(q, w['g_q_head'])
    return jnp.concatenate([q[..., :QK_NOPE], rope(q[..., QK_NOPE:], pos)], axis=-1)


def mla_keys_values(c_kv, k_pe, pos, w):
    B, L, _ = c_kv.shape
    k_nope = (c_kv @ w['w_kb']).reshape(B, L, MLA_HEADS, QK_NOPE)
    kpe = jnp.broadcast_to(k_pe[:, :, None, :], (B, L, MLA_HEADS, QK_ROPE))
    k = rmsnorm(jnp.concatenate([k_nope, kpe], axis=-1), w['g_k_head'])
    k = jnp.concatenate([k[..., :QK_NOPE], rope(k[..., QK_NOPE:], pos)], axis=-1)
    v = (c_kv @ w['w_vb']).reshape(B, L, MLA_HEADS, V_HEAD)
    return k, v


def memory_kv(mem, w):
    B = mem.shape[0]
    m = rmsnorm(mem, w['g_mem']) @ w['w_mem_kv']
    k = rmsnorm(m[..., :MEM_W].reshape(B, N_MEM, MEM_HEADS, MEM_HEAD_DIM), w['g_mem_k'])
    v = m[..., MEM_W:].reshape(B, N_MEM, MEM_HEADS, MEM_HEAD_DIM)
    return k, v


def trunk_layer(x, pos, pool_left, mla_past, mem_k, mem_v, w):
    B, L, _ = x.shape
    h = rmsnorm(x, w['g_mix'])
    u, q_lat, kv_lat, k_pe, mq, gate_logits = split_in(h @ w['w_in'])
    y_pool, new_pool = pool_branch(u, pool_left, pos, w['w_pool'], w['pool_scale'])
    c_kv = rmsnorm(kv_lat, w['g_kv_lat'])
    q = mla_queries(q_lat, pos, w)
    if mla_past is None:
        k, v = mla_keys_values(c_kv, k_pe, pos, w)
        nb = L // Q_BLOCK
        qb = q.reshape(B, nb, Q_BLOCK, MLA_HEADS, QK_HEAD).transpose(1, 0, 2, 3, 4)
        pb = pos.reshape(nb, Q_BLOCK)
        o = lax.map(lambda a: chunk_attend(a[0], a[1], k, v, pos), (qb, pb))
        o = o.transpose(1, 0, 2, 3, 4).reshape(B, L, MLA_HEADS, V_HEAD)
    else:
        past_lat, past_kpe = mla_past
        n_past = past_lat.shape[1]
        all_lat = jnp.concatenate([past_lat, c_kv], axis=1)
        all_kpe = jnp.concatenate([past_kpe, k_pe], axis=1)
        k_pos = jnp.arange(n_past + L, dtype=jnp.int32)
        k, v = mla_keys_values(all_lat, all_kpe, k_pos, w)
        o = chunk_attend(q, pos, k, v, k_pos)
    y_mla = o.reshape(B, L, MLA_HEADS * V_HEAD) @ w['w_mla_o']
    qm = rmsnorm(mq.reshape(B, L, MEM_HEADS, MEM_HEAD_DIM), w['g_mem_q'])
    y_mem = attend(qm, mem_k, mem_v, MEM_SCALE).reshape(B, L, MEM_W) @ w['w_mem_o']
    gates = jax.nn.sigmoid((gate_logits + w['b_gate']).astype(jnp.float32)).astype(x.dtype)
    gates = gates.reshape(B, L, N_BRANCH, D_MODEL)
    merged = gates[:, :, 0] * y_pool + gates[:, :, 1] * y_mla + gates[:, :, 2] * y_mem
    x = x + merged @ w['w_out']
    f = rmsnorm(x, w['g_ff']) @ w['w_up']
    x = x + jnp.square(jax.nn.relu(f)) @ w['w_down']
    return x, c_kv, k_pe, new_pool


def setup_inputs(seed: int = 0) -> dict:
    key = jax.random.key(seed)
    ks = iter(jax.random.split(key, 40))
    f32 = jnp.float32

    def nrm(shape, scale=1.0):
        return jax.random.normal(next(ks), shape, f32) * scale

    def gain(n):
        return 1.0 + nrm((n,), 0.02)

    return {
        'x_prompt': nrm((BATCH, SEQ, D_MODEL)),
        'mem_prompt': nrm((BATCH, N_MEM, D_MODEL)),
        'x_sample': nrm((DEC_BATCH, DEC_SEQ, D_MODEL)),
        'cache_mla_latent': nrm((DEC_BATCH, PAST_LEN, KV_LORA)),
        'cache_mla_kpe': nrm((DEC_BATCH, PAST_LEN, QK_ROPE)),
        'state_pool': nrm((DEC_BATCH, POOL_STATE, POOL_W)),
        'cache_mem_k': nrm((DEC_BATCH, N_MEM, MEM_HEADS, MEM_HEAD_DIM)),
        'cache_mem_v': nrm((DEC_BATCH, N_MEM, MEM_HEADS, MEM_HEAD_DIM)),
        'g_mix': gain(D_MODEL),
        'w_in': nrm((D_MODEL, D_IN), D_MODEL ** -0.5),
        'b_gate': nrm((N_BRANCH * D_MODEL,), 0.01),
        'w_pool': nrm((N_POOL_GROUPS, POOL_GW, POOL_OUT_GW), POOL_GW ** -0.5),
        'pool_scale': gain(D_MODEL),
        'g_q_lat': gain(Q_LORA),
        'w_qb': nrm((Q_LORA, MLA_HEADS * QK_HEAD), Q_LORA ** -0.5),
        'g_q_head': gain(QK_HEAD),
        'g_kv_lat': gain(KV_LORA),
        'w_kb': nrm((KV_LORA, MLA_HEADS * QK_NOPE), KV_LORA ** -0.5),
        'w_vb': nrm((KV_LORA, MLA_HEADS * V_HEAD), KV_LORA ** -0.5),
        'g_k_head': gain(QK_HEAD),
        'w_mla_o': nrm((MLA_HEADS * V_HEAD, D_MODEL), (MLA_HEADS * V_HEAD) ** -0.5),
        'g_mem': gain(D_MODEL),
        'w_mem_kv': nrm((D_MODEL, 2 * MEM_W), D_MODEL ** -0.5),
        'g_mem_q': gain(MEM_HEAD_DIM),
        'g_mem_k': gain(MEM_HEAD_DIM),
        'w_mem_o': nrm((MEM_W, D_MODEL), MEM_W ** -0.5),
        'w_out': nrm((D_MODEL, D_MODEL), D_MODEL ** -0.5),
        'g_ff': gain(D_MODEL),
        'w_up': nrm((D_MODEL, D_FF), D_MODEL ** -0.5),
        'w_down': nrm((D_FF, D_MODEL), D_FF ** -0.5),
    }


def reference(x_prompt, mem_prompt, x_sample, cache_mla_latent, cache_mla_kpe, state_pool,
              cache_mem_k, cache_mem_v, g_mix, w_in, b_gate, w_pool, pool_scale, g_q_lat, w_qb,
              g_q_head, g_kv_lat, w_kb, w_vb, g_k_head, w_mla_o, g_mem, w_mem_kv, g_mem_q,
              g_mem_k, w_mem_o, w_out, g_ff, w_up, w_down):
    w = dict(g_mix=g_mix, w_in=w_in, b_gate=b_gate, w_pool=w_pool, pool_scale=pool_scale,
             g_q_lat=g_q_lat, w_qb=w_qb, g_q_head=g_q_head, g_kv_lat=g_kv_lat, w_kb=w_kb,
             w_vb=w_vb, g_k_head=g_k_head, w_mla_o=w_mla_o, g_mem=g_mem, w_mem_kv=w_mem_kv,
             g_mem_q=g_mem_q, g_mem_k=g_mem_k, w_mem_o=w_mem_o, w_out=w_out, g_ff=g_ff,
             w_up=w_up, w_down=w_down)
    L_p = x_prompt.shape[1]
    pos_p = jnp.arange(L_p, dtype=jnp.int32)
    mem_k_p, mem_v_p = memory_kv(mem_prompt, w)
    y_p = x_prompt
    pool_left_p = jnp.zeros((x_prompt.shape[0], POOL_STATE, POOL_W), x_prompt.dtype)
    for _ in range(DEPTH):
        y_p, lat_p, kpe_p, pool_p = trunk_layer(y_p, pos_p, pool_left_p, None, mem_k_p, mem_v_p, w)
    n_past = cache_mla_latent.shape[1]
    pos_s = n_past + jnp.arange(x_sample.shape[1], dtype=jnp.int32)
    y_s = x_sample
    for _ in range(DEPTH):
        y_s, lat_s, kpe_s, pool_s = trunk_layer(y_s, pos_s, state_pool,
                                                (cache_mla_latent, cache_mla_kpe),
                                                cache_mem_k, cache_mem_v, w)
    return (y_p, y_s, lat_p, kpe_p, pool_p, mem_k_p, mem_v_p, lat_s, kpe_s, pool_s)
```

```python
import os
import numpy as np
from contextlib import ExitStack
import concourse.bass as bass
import concourse.mybir as mybir
from concourse.bass_utils import run_bass_kernel_spmd

F32 = mybir.dt.float32
BF16 = mybir.dt.bfloat16
AF = mybir.ActivationFunctionType
ALU = mybir.AluOpType
AX = mybir.AxisListType

D = 2048
SEQ = 4096
NB = 32
NKB = 33
TOWN = 1088
EPS = 1e-6
D_IN = 9280
OFF_U, OFF_QL, OFF_KV, OFF_KPE, OFF_MQ, OFF_G = 0, 1024, 1536, 2048, 2112, 3136
EPOCH = 30000
ARENA_BYTES = 210944

BLOCKS = [(s * 128, 128) for s in range(8)] + [(1024, 64)]


class Buf:
    def __init__(self, t, name, psum=False, ap=None):
        self.t = t
        self.name = name
        self.psum = psum
        self.w = {}
        self.r = {}
        self.dsem = None
        self.dval = 0
        self._ap = ap

    def __getitem__(self, key):
        base = self._ap if self._ap is not None else self.t
        return base[key]


class Eng:
    def __init__(self, name, obj):
        self.name = name
        self.obj = obj
        self.sem = None
        self.cnt = 0
        self.waited = {}


class Kern:
    def __init__(self, nc, stack):
        self.nc = nc
        self.stack = stack
        self.e = {n: Eng(n, o) for n, o in (('pe', nc.tensor), ('act', nc.scalar), ('dve', nc.vector),
                                            ('pool', nc.gpsimd), ('sp', nc.sync))}
        self.nsem = 0
        for n in ('pe', 'act', 'dve', 'pool'):
            self.e[n].sem = self.new_sem(n)
        self.store_toks = []
        self.dma_bufs = []
        self.ninst = 0
        self.rr = 0
        self._rec = None

    def new_sem(self, name):
        self.nsem += 1
        return self.stack.enter_context(self.nc.semaphore(f"s{self.nsem}_{name}"))

    def _need(self, en, reads, writes):
        E = self.e[en]
        need = {}

        def add(d, own_ok):
            for s, v in d.items():
                if s is E.sem and en == 'pe':
                    continue
                if need.get(s, 0) < v:
                    need[s] = v

        for b in reads:
            add(b.w, True)
            if b.psum:
                add(b.r, False)
        for b in writes:
            add(b.w, False)
            add(b.r, False)
        return need

    def _do_waits(self, en, need):
        E = self.e[en]
        for s, v in need.items():
            if E.waited.get(s, 0) >= v:
                continue
            E.obj.wait_ge(s, v)
            E.waited[s] = v
            self.ninst += 1

    def record(self):
        self._rec = []

    def stop_record(self):
        r = self._rec
        self._rec = None
        return r

    def py(self, fn):
        if self._rec is not None:
            self._rec.append(('py', (fn,), {}))
        else:
            fn()

    def replay(self, lists, width=2, stagger=0.5):
        active = []
        nxt = 0
        while nxt < len(lists) or active:
            if nxt < len(lists) and len(active) < width and (
                    not active or active[-1][1] >= stagger * len(active[-1][0])):
                active.append([lists[nxt], 0])
                nxt += 1
            for a in list(active):
                lst, pos = a
                if pos < len(lst):
                    kind, args, kw = lst[pos]
                    if kind == 'py':
                        args[0]()
                    else:
                        (self.op if kind == 'op' else self.dma)(*args, **kw)
                    a[1] += 1
                if a[1] >= len(lst):
                    active.remove(a)

    def op(self, en, fn, reads=(), writes=(), signal=True):
        if self._rec is not None:
            self._rec.append(('op', (en, fn, list(reads), list(writes), signal), {}))
            return None
        E = self.e[en]
        self._do_waits(en, self._need(en, reads, writes))
        ins = fn(E.obj)
        self.ninst += 1
        if E.cnt >= EPOCH - 200:
            E.sem = self.new_sem(en)
            E.cnt = 0
        if signal:
            E.cnt += 1
            ins.then_inc(E.sem, 1)
            tok = (E.sem, E.cnt)
        else:
            tok = (E.sem, E.cnt + 1)
        for b in reads:
            if b.r.get(tok[0], 0) < tok[1]:
                b.r[tok[0]] = tok[1]
        for b in writes:
            b.w = {tok[0]: tok[1]}
            b.r = {}
        return ins

    def dma(self, out_ap, in_ap, reads=(), writes=(), q='sp', final=False, **kw):
        if self._rec is not None:
            kw2 = dict(kw)
            kw2.update(reads=list(reads), writes=list(writes), q=q, final=final)
            self._rec.append(('dma', (out_ap, in_ap), kw2))
            return None
        E = self.e[q]
        self._do_waits(q, self._need(q, reads, writes))
        owner = writes[0] if writes else reads[0]
        if owner.dsem is None or owner.dval >= EPOCH:
            owner.dsem = self.new_sem('d')
            owner.dval = 0
            self.dma_bufs.append(owner)
        owner.dval += 16
        E.obj.dma_start(out=out_ap, in_=in_ap, **kw).then_inc(owner.dsem, 16)
        self.ninst += 1
        tok = (owner.dsem, owner.dval)
        for b in reads:
            if b.r.get(tok[0], 0) < tok[1]:
                b.r[tok[0]] = tok[1]
        for b in writes:
            b.w = {tok[0]: tok[1]}
            b.r = {}
        if final:
            self.store_toks.append(tok)
        return tok

    def barrier(self):
        names = ('pe', 'act', 'dve', 'pool')
        for a in names + ('sp',):
            need = {}
            for b in names:
                if a == b:
                    continue
                B = self.e[b]
                if B.cnt > 0:
                    need[B.sem] = B.cnt
            for b in self.dma_bufs:
                if b.dval > 0 and need.get(b.dsem, 0) < b.dval:
                    need[b.dsem] = b.dval
            self._do_waits(a, need)

    def finish(self):
        need = {}
        for s, v in self.store_toks:
            if need.get(s, 0) < v:
                need[s] = v
        for b in ('pe', 'act', 'dve', 'pool'):
            B = self.e[b]
            if B.cnt > 0:
                need[B.sem] = B.cnt
        self._do_waits('sp', need)

    def ev(self):
        self.rr ^= 1
        return 'act' if self.rr else 'dve'


class Arena:
    def __init__(self, K, nbytes):
        self.K = K
        self.t = K.stack.enter_context(K.nc.sbuf_tensor("arena", [128, nbytes // 4], F32))
        self.off = 0
        self.limit = nbytes
        self.n = 0

    def at(self, off):
        self.off = off

    def alloc(self, name, shape, dt=F32):
        nfree = int(np.prod(shape[1:]))
        esz = 2 if dt == BF16 else 4
        nb = (nfree * esz + 31) // 32 * 32
        o = self.off
        assert o % 4 == 0 and o + nb <= self.limit, (name, o, nb, self.limit)
        self.off = o + nb
        ap = self.t[0:shape[0], o // 4:(o + nb) // 4]
        if dt != F32:
            ap = ap.bitcast(dt)
        ap = ap[:, 0:nfree]
        if len(shape) == 3:
            ap = ap.rearrange("p (a b) -> p a b", a=shape[1])
        elif len(shape) == 4:
            ap = ap.rearrange("p (a b c) -> p a b c", a=shape[1], b=shape[2])
        self.n += 1
        return Buf(self.t, f"{name}_{self.n}", ap=ap)


def build_program(stage=99):
    nc = bass.Bass("TRN2", target_bir_lowering=False)

    def din(name, shape):
        return nc.dram_tensor(name, list(shape), F32, kind="ExternalInput").ap()

    def dout(name, shape):
        return nc.dram_tensor(name, list(shape), F32, kind="ExternalOutput").ap()

    xseq = din("xseq", [SEQ, D])
    xown = din("xown", [TOWN, D])
    xprev = din("xprev", [128, D])
    latc = din("latc", [2, SEQ, 512])
    kpec = din("kpec", [2, SEQ, 64])
    spool = din("spool", [32, 1024])
    memp = din("memp", [256, D])
    cmk = din("cmk", [2, 256, 1024])
    cmv = din("cmv", [2, 256, 1024])
    w_in = din("w_in", [D, D_IN])
    w_pool = din("w_pool", [4, 256, 512])
    w_qb = din("w_qb", [512, 3072])
    w_qbrot = din("w_qbrot", [512, 1024])
    w_kb = din("w_kb", [512, 2048])
    w_vb = din("w_vb", [512, 2048])
    w_mla_o = din("w_mla_o", [2048, 2048])
    w_mem_kv = din("w_mem_kv", [2048, 2048])
    w_mem_o = din("w_mem_o", [1024, 2048])
    w_out = din("w_out", [2048, 2048])
    w_up = din("w_up", [2048, 8192])
    w_down = din("w_down", [8192, 2048])
    g_mix = din("g_mix", [D])
    b_gate = din("b_gate", [6144])
    pool_scale = din("pool_scale", [D])
    g_q_lat = din("g_q_lat", [512])
    g_q_head = din("g_q_head", [192])
    g_kv_lat = din("g_kv_lat", [512])
    g_k_head = din("g_k_head", [192])
    g_mem = din("g_mem", [D])
    g_mem_q = din("g_mem_q", [256])
    g_mem_k = din("g_mem_k", [256])
    g_ff = din("g_ff", [D])
    ident = din("ident", [128, 128])
    ktab_c = din("ktab_c", [128, NKB, 64])
    ktab_s = din("ktab_s", [128, NKB, 64])
    qtab_c = din("qtab_c", [64, TOWN])
    qtab_s = din("qtab_s", [64, TOWN])
    mcur = din("mcur", [128, 8, 128])
    mprev = din("mprev", [128, 32, 128])
    mcurS = din("mcurS", [64, 4, 64])
    mprevS = din("mprevS", [32, 4, 64])
    kmask = din("kmask", [16, SEQ])
    qmask = din("qmask", [16, 1024])

    y_own = dout("y_own", [TOWN, D])
    lat_own = dout("lat_own", [TOWN, 512])
    kpe_own = dout("kpe_own", [TOWN, 64])
    poolp_out = dout("poolp_out", [15, 1024])
    pools_out = dout("pools_out", [2, 15, 1024])
    memk_out = dout("memk_out", [256, 1024])
    memv_out = dout("memv_out", [256, 1024])

    with ExitStack() as st:
        K = Kern(nc, st)
        A = Arena(K, ARENA_BYTES)
        pst = st.enter_context(nc.psum_tensor("pst", [128, 8, 512], F32))
        PB = [Buf(pst, f"bank{i}", psum=True, ap=pst[:, i, :]) for i in range(8)]

        def bfv(bank, n=128):
            return PB[bank][0:n, :].bitcast(BF16)

        A.at(0)
        ident_b = A.alloc("ident", [128, 128], BF16)
        ones_b = A.alloc("ones", [128, 128], BF16)
        ones_f = A.alloc("onesf", [128, 128])
        eps6 = A.alloc("eps6", [128, 1])
        eps192 = A.alloc("eps192", [128, 1])
        gqk = A.alloc("gqk", [128, 1])
        gq_n = A.alloc("gq_n", [128, 1])
        gk_n = A.alloc("gk_n", [128, 1])
        gq_r = A.alloc("gq_r", [64, 1])
        gq_rp = A.alloc("gq_rp", [64, 1])
        gbc = A.alloc("gbc", [128, D])
        cnew = A.alloc("cnew", [64, 512], BF16)
        kpnew = A.alloc("kpnew", [64, 64])
        sspe_new = A.alloc("sspe_new", [64, 1])
        sq_last = A.alloc("sq_last", [128, 32], BF16)
        assert A.off <= 14336
        A.at(14336)
        memkT_p = A.alloc("memkT_p", [128, 8, 256], BF16)
        memv_p = A.alloc("memv_p", [128, 2, 1024], BF16)
        assert A.off == 22528
        OT_OFF = 22528
        R1 = 57344
        R2 = 92160
        R3 = 142304

        K.dma(ident_b[:], ident, writes=[ident_b], q='pool')
        K.dma(gbc[:], g_mix.partition_broadcast(128), writes=[gbc])
        K.dma(gq_n[:], g_q_head[0:128].rearrange("(p o) -> p o", o=1), writes=[gq_n])
        K.dma(gk_n[:], g_k_head[0:128].rearrange("(p o) -> p o", o=1), writes=[gk_n])
        K.dma(gq_r[:], g_q_head[128:192].rearrange("(p o) -> p o", o=1), writes=[gq_r])
        K.dma(gq_rp[0:32, :], g_q_head[160:192].rearrange("(p o) -> p o", o=1), writes=[gq_rp])
        K.dma(gq_rp[32:64, :], g_q_head[128:160].rearrange("(p o) -> p o", o=1), writes=[gq_rp])
        K.op('dve', lambda e: e.memset(ones_f[:], 1.0), writes=[ones_f])
        K.op('dve', lambda e: e.tensor_copy(out=ones_b[:], in_=ones_f[:]), reads=[ones_f], writes=[ones_b])
        K.op('dve', lambda e: e.memset(eps6[:], EPS), writes=[eps6])
        K.op('dve', lambda e: e.memset(eps192[:], EPS * 192.0), writes=[eps192])
        K.op('dve', lambda e: e.tensor_tensor(out=gqk[:], in0=gq_n[:], in1=gk_n[:], op=ALU.mult),
             reads=[gq_n, gk_n], writes=[gqk])

        def rms_stats(src_ap, n, width, junk_ap, ss, rstd, reads, eps_b=eps6, extra_w=()):
            K.op('act', lambda e: e.activation(out=junk_ap, in_=src_ap, func=AF.Square, accum_out=ss[0:n, :]),
                 reads=reads, writes=[ss] + list(extra_w))
            K.op('act', lambda e: e.activation(out=rstd[0:n, :], in_=ss[0:n, :], func=AF.Sqrt, bias=eps_b[0:n, :],
                                               scale=1.0 / width), reads=[ss, eps_b], writes=[rstd])
            K.op('dve', lambda e: e.reciprocal(out=rstd[0:n, :], in_=rstd[0:n, :]), reads=[rstd], writes=[rstd])

        def transposes(src, src_bufs, n, nch, dst_fn, dst_bufs, banks, cw=128, pb=0, dst_grp=None):
            k = 0
            bi = 0
            while k < nch:
                g = min(4, nch - k)
                bk = banks[bi % len(banks)]
                bi += 1
                pv = bfv(bk, cw)
                for j in range(g):
                    K.op('pe', lambda e, kk=k + j, j=j, pv=pv: e.transpose(
                        out=pv[:, j * 128:j * 128 + n], in_=src[0:n, kk * cw:(kk + 1) * cw], identity=ident_b[pb:pb + n, pb:pb + n]),
                        reads=list(src_bufs) + [ident_b], writes=[PB[bk]], signal=(j == g - 1))
                if dst_grp is not None:
                    K.op(K.ev(), (lambda k0, g, pv: (lambda e: (
                        e.activation(out=dst_grp(k0, g), in_=pv[:, 0:g * 128].rearrange("p (a b) -> p a b", a=g)[:, :, 0:n], func=AF.Copy)
                        if e is nc.scalar else
                        e.tensor_copy(out=dst_grp(k0, g), in_=pv[:, 0:g * 128].rearrange("p (a b) -> p a b", a=g)[:, :, 0:n]))))(k, g, pv),
                        reads=[PB[bk]], writes=dst_bufs)
                else:
                    for j in range(g):
                        K.op(K.ev(), (lambda kk, j, pv: (lambda e: (
                            e.activation(out=dst_fn(kk), in_=pv[:, j * 128:j * 128 + n], func=AF.Copy)
                            if e is nc.scalar else e.tensor_copy(out=dst_fn(kk), in_=pv[:, j * 128:j * 128 + n]))))(k + j, j, pv),
                            reads=[PB[bk]], writes=dst_bufs)
                k += g

        def norm_block(x_dram_rows, n, xb, h, hT, ss, rstd, banks, col0=0):
            K.dma(xb[0:n, :], x_dram_rows, writes=[xb])
            rms_stats(xb[0:n, :], n, D, h[0:n, :], ss, rstd, reads=[xb], extra_w=[h])
            K.op('dve', lambda e: e.scalar_tensor_tensor(out=h[0:n, :], in0=xb[0:n, :], scalar=rstd[0:n, :], in1=gbc[0:n, :],
                                                         op0=ALU.mult, op1=ALU.mult), reads=[xb, rstd, gbc], writes=[h])
            transposes(h, [h], n, 16, lambda k: hT[:, k, col0:col0 + n], [hT], banks,
                       dst_grp=lambda k0, g: hT[:, k0:k0 + g, col0:col0 + n])

        def mm_tm(out_ap, bank, lhs_fn, rhs_fn, nk, reads):
            for k in range(nk):
                K.op('pe', lambda e, k=k: e.matmul(out_ap, lhsT=lhs_fn(k), rhs=rhs_fn(k), start=(k == 0), stop=(k == nk - 1)),
                     reads=reads, writes=[PB[bank]], signal=(k == nk - 1))

        A.at(OT_OFF)
        gmem_bc = A.alloc("gmem_bc", [128, D])
        gmk_bc = A.alloc("gmk_bc", [128, 256])
        xb0 = A.alloc("xb0", [128, D])
        h0 = A.alloc("h0", [128, D], BF16)
        memhT = A.alloc("memhT", [128, 16, 256], BF16)
        wt = [A.alloc(f"wmkv{i}", [128, 16, 512], BF16) for i in range(2)]
        mkf = A.alloc("mkf", [128, 2, 1024])
        mvf = A.alloc("mvf", [128, 2, 1024])
        mkb = A.alloc("mkb", [128, 2, 1024], BF16)
        sq4 = A.alloc("sq4", [128, 1024])
        ss4 = A.alloc("ss4", [128, 4])
        r4 = A.alloc("r4", [128, 4])
        ssm = A.alloc("ssm", [128, 1])
        rsm = A.alloc("rsm", [128, 1])
        K.dma(gmem_bc[:], g_mem.partition_broadcast(128), writes=[gmem_bc])
        K.dma(gmk_bc[:], g_mem_k.partition_broadcast(128), writes=[gmk_bc])
        for mb in range(2):
            K.dma(xb0[:], memp[mb * 128:(mb + 1) * 128, :], writes=[xb0])
            rms_stats(xb0[:], 128, D, h0[:], ssm, rsm, reads=[xb0], extra_w=[h0])
            K.op('dve', lambda e: e.scalar_tensor_tensor(out=h0[:], in0=xb0[:], scalar=rsm[:], in1=gmem_bc[:],
                                                         op0=ALU.mult, op1=ALU.mult), reads=[xb0, rsm, gmem_bc], writes=[h0])
            transposes(h0, [h0], 128, 16, lambda k, mb=mb: memhT[:, k, mb * 128:(mb + 1) * 128], [memhT], [0, 1])
        for ct in range(4):
            wtile = wt[ct % 2]
            K.dma(wtile[:], w_mem_kv[:, ct * 512:(ct + 1) * 512].rearrange("(k p) n -> p k n", p=128), writes=[wtile], q='pool')
            for mb in range(2):
                bank = 2 + (ct * 2 + mb) % 4
                mm_tm(PB[bank][:, :], bank, lambda k, mb=mb: memhT[:, k, mb * 128:(mb + 1) * 128],
                      lambda k, wtile=wtile: wtile[:, k, :], 16, [memhT, wtile])
                dstf = mkf if ct < 2 else mvf
                cc = (ct % 2) * 512
                K.op(K.ev(), (lambda bank, dstf, mb, cc: (lambda e: (
                    e.activation(out=dstf[:, mb, cc:cc + 512], in_=PB[bank][:, :], func=AF.Copy) if e is nc.scalar
                    else e.tensor_copy(out=dstf[:, mb, cc:cc + 512], in_=PB[bank][:, :]))))(bank, dstf, mb, cc),
                    reads=[PB[bank]], writes=[dstf])
        for mb in range(2):
            K.op('dve', lambda e, mb=mb: e.tensor_tensor(out=sq4[:], in0=mkf[:, mb, :], in1=mkf[:, mb, :], op=ALU.mult),
                 reads=[mkf], writes=[sq4])
            K.op('dve', lambda e: e.tensor_reduce(out=ss4[:], in_=sq4[:].rearrange("p (h d) -> p h d", h=4), axis=AX.X, op=ALU.add),
                 reads=[sq4], writes=[ss4])
            K.op('act', lambda e: e.activation(out=r4[:], in_=ss4[:], func=AF.Sqrt, bias=eps6[:], scale=1.0 / 256),
                 reads=[ss4, eps6], writes=[r4])
            K.op('dve', lambda e: e.reciprocal(out=r4[:], in_=r4[:]), reads=[r4], writes=[r4])
            for hh in range(4):
                K.op('dve', lambda e, mb=mb, hh=hh: e.scalar_tensor_tensor(
                    out=mkf[:, mb, hh * 256:(hh + 1) * 256], in0=mkf[:, mb, hh * 256:(hh + 1) * 256], scalar=r4[:, hh:hh + 1],
                    in1=gmk_bc[:], op0=ALU.mult, op1=ALU.mult), reads=[mkf, r4, gmk_bc], writes=[mkf])
            K.op('act', lambda e, mb=mb: e.activation(out=mkb[:, mb, :], in_=mkf[:, mb, :], func=AF.Copy), reads=[mkf], writes=[mkb])
            K.op('dve', lambda e, mb=mb: e.tensor_copy(out=memv_p[:, mb, :], in_=mvf[:, mb, :]), reads=[mvf], writes=[memv_p])
            K.dma(memk_out[mb * 128:(mb + 1) * 128, :], mkf[:, mb, :], reads=[mkf], final=True)
            K.dma(memv_out[mb * 128:(mb + 1) * 128, :], mvf[:, mb, :], reads=[mvf], final=True)

        def build_memkT(src_b, dst):
            for mb in range(2):
                transposes(src_b[:, mb, :], [src_b], 128, 8,
                           lambda c, mb=mb: dst[:, c, mb * 128:(mb + 1) * 128], [dst], [0, 1])

        build_memkT(mkb, memkT_p)
        K.barrier()
        if stage <= 0:
            K.finish()
            return nc

        A.at(OT_OFF)
        oT_all = A.alloc("oT_all", [128, 16, TOWN], BF16)
        assert A.off == R1
        Wa = A.alloc("Wa", [128, 16, 1088], BF16)
        assert A.off == R2
        ckvT = A.alloc("ckvT", [128, 4, 4128], BF16)
        kropeT = A.alloc("kropeT", [128, 4128], BF16)
        sspe = A.alloc("sspe", [128, NKB])
        qlatT = A.alloc("qlatT", [128, 4, TOWN], BF16)
        assert A.off <= R3, A.off
        A.at(R3)
        ktc = A.alloc("ktc", [128, NKB, 64])
        kts = A.alloc("kts", [128, NKB, 64])
        gkr_bc = A.alloc("gkr_bc", [128, 64])
        CH = [dict(), dict(), dict()]
        for c in range(3):
            CH[c]['kpg'] = A.alloc("kpg", [128, 64])
            CH[c]['kt1'] = A.alloc("kt1", [128, 64])
            CH[c]['kt2'] = A.alloc("kt2", [128, 64])
            CH[c]['krb'] = A.alloc("krb", [128, 64], BF16)
            CH[c]['bT'], CH[c]['bL'], CH[c]['bS'], CH[c]['bC'] = [(0, 1, 2, 2), (3, 4, 5, 5), (6, 7, 6, 6)][c]
        PBOFF = A.off
        for c in range(3):
            if c == 2:
                sv_off = A.off
                A.at(OT_OFF)
            CH[c]['xb'] = A.alloc("xb", [128, D])
            CH[c]['hb'] = A.alloc("hb", [128, D], BF16)
            CH[c]['hT'] = A.alloc("hT", [128, 16, 128], BF16)
            CH[c]['cf'] = A.alloc("cf", [128, 512])
            CH[c]['cb'] = A.alloc("cb", [128, 512], BF16)
            CH[c]['qlb'] = A.alloc("qlb", [128, 512], BF16)
            CH[c]['kpf'] = A.alloc("kpf", [128, 64])
            CH[c]['ss'] = [A.alloc("st_ss", [128, 1]) for i in range(3)]
            CH[c]['r'] = [A.alloc("st_r", [128, 1]) for i in range(3)]
            if c == 2:
                assert A.off <= R1
                A.at(sv_off)
        gkv_bc = A.alloc("gkv_bc", [128, 512])
        gql_bc = A.alloc("gql_bc", [128, 512])
        hscr_t = nc.dram_tensor("hT_scr", [128, 16, TOWN + 128], BF16, kind="Internal").ap()
        hscr = Buf(None, "hscr")
        K.op('dve', lambda e: e.memset(sspe[:], 1.0), writes=[sspe])
        K.op('dve', lambda e: e.memset(kropeT[64:128, :], 0.0), writes=[kropeT])
        K.dma(kropeT[64:80, 0:SEQ], kmask, writes=[kropeT], q='pool')
        K.dma(Wa[:, :, 0:512], w_in[:, OFF_QL:OFF_QL + 512].rearrange("(k p) n -> p k n", p=128), writes=[Wa], q='pool')
        K.dma(Wa[:, :, 512:1088], w_in[:, OFF_KV:OFF_KV + 576].rearrange("(k p) n -> p k n", p=128), writes=[Wa], q='pool')
        K.dma(ktc[:], ktab_c, writes=[ktc])
        K.dma(kts[:], ktab_s, writes=[kts])
        K.dma(gkv_bc[:], g_kv_lat.partition_broadcast(128), writes=[gkv_bc])
        K.dma(gql_bc[:], g_q_lat.partition_broadcast(128), writes=[gql_bc])
        K.dma(gkr_bc[:], g_k_head[128:192].partition_broadcast(128), writes=[gkr_bc])

        def key_side(ch, c_src_ap, c_reads, kpe_src_ap, kpe_reads, n, blk, col0, sspe_dst_ap, sspe_dst_buf, pb=0):
            P = slice(pb, pb + n)
            kpg, kt1, kt2, krb, bS = ch['kpg'], ch['kt1'], ch['kt2'], ch['krb'], ch['bS']
            transposes(c_src_ap, c_reads, n, 4, lambda k: ckvT[:, k, col0:col0 + n], [ckvT], [ch['bC']], pb=pb,
                       dst_grp=lambda k0, g: ckvT[:, k0:k0 + g, col0:col0 + n])
            K.op('act', lambda e: e.activation(out=kt1[P, :], in_=kpe_src_ap, func=AF.Square, accum_out=sspe_dst_ap),
                 reads=kpe_reads, writes=[kt1, sspe_dst_buf])
            K.op('dve', lambda e: e.tensor_tensor(out=kpg[P, :], in0=kpe_src_ap, in1=gkr_bc[P, :], op=ALU.mult),
                 reads=kpe_reads + [gkr_bc], writes=[kpg])
            K.op('dve', lambda e: e.tensor_tensor(out=kt1[P, :], in0=kpg[P, :], in1=ktc[P, blk, :], op=ALU.mult),
                 reads=[kpg, ktc], writes=[kt1])
            K.op('dve', lambda e: e.tensor_tensor(out=kt2[P, 0:32], in0=kpg[P, 32:64], in1=kts[P, blk, 0:32], op=ALU.mult),
                 reads=[kpg, kts], writes=[kt2])
            K.op('dve', lambda e: e.tensor_tensor(out=kt2[P, 32:64], in0=kpg[P, 0:32], in1=kts[P, blk, 32:64], op=ALU.mult),
                 reads=[kpg, kts], writes=[kt2])
            K.op('dve', lambda e: e.tensor_tensor(out=krb[P, :], in0=kt1[P, :], in1=kt2[P, :], op=ALU.add),
                 reads=[kt1, kt2], writes=[krb])
            K.op('pe', lambda e: e.transpose(out=bfv(bS, 64)[:, 512:512 + n], in_=krb[P, :], identity=ident_b[P, P]),
                 reads=[krb, ident_b], writes=[PB[bS]])
            K.op('act', lambda e: e.activation(out=kropeT[0:64, col0:col0 + n], in_=bfv(bS, 64)[:, 512:512 + n], func=AF.Copy),
                 reads=[PB[bS]], writes=[kropeT])

        def latents(ch, n):
            hT, bL, bS, cf, cb, kpf = ch['hT'], ch['bL'], ch['bS'], ch['cf'], ch['cb'], ch['kpf']
            mm_tm(PB[bL][0:n, :], bL, lambda k: hT[:, k, 0:n], lambda k: Wa[:, k, 512:1024], 16, [hT, Wa])
            mm_tm(PB[bS][0:n, 0:64], bS, lambda k: hT[:, k, 0:n], lambda k: Wa[:, k, 1024:1088], 16, [hT, Wa])
            rms_stats(PB[bL][0:n, :], n, 512, cb[0:n, :], ch['ss'][1], ch['r'][1], reads=[PB[bL]], extra_w=[cb])
            K.op('dve', lambda e: e.scalar_tensor_tensor(out=cf[0:n, :], in0=PB[bL][0:n, :], scalar=ch['r'][1][0:n, :], in1=gkv_bc[0:n, :],
                                                         op0=ALU.mult, op1=ALU.mult), reads=[PB[bL], ch['r'][1], gkv_bc], writes=[cf])
            K.op('act', lambda e: e.activation(out=cb[0:n, :], in_=cf[0:n, :], func=AF.Copy), reads=[cf], writes=[cb])
            K.op('dve', lambda e: e.tensor_copy(out=kpf[0:n, :], in_=PB[bS][0:n, 0:64]), reads=[PB[bS]], writes=[kpf])

        blks = [(xseq[blk * 128:(blk + 1) * 128, :], 128) for blk in range(NB)]
        blks += [(xown[r0:r0 + n, :], n) for (r0, n) in BLOCKS]
        blks += [(xprev[:, :], 128)]
        NBLK = len(blks)
        normed = set()

        def load_x(i):
            rows, n = blks[i]
            xb = CH[i % 3]['xb']
            K.dma(xb[0:n, :], rows, writes=[xb])

        def norm_a(i):
            rows, n = blks[i]
            ch = CH[i % 3]
            xb, h = ch['xb'], ch['hb']
            rms_stats(xb[0:n, :], n, D, h[0:n, :], ch['ss'][0], ch['r'][0], reads=[xb], extra_w=[h])
            K.op('dve', lambda e: e.scalar_tensor_tensor(out=h[0:n, :], in0=xb[0:n, :], scalar=ch['r'][0][0:n, :], in1=gbc[0:n, :],
                                                         op0=ALU.mult, op1=ALU.mult), reads=[xb, ch['r'][0], gbc], writes=[h])
            K.py(lambda: normed.add(i))
            if i + 3 < NBLK:
                load_x(i + 3)

        def head(i):
            rows, n = blks[i]
            ch = CH[i % 3]
            hT = ch['hT']

            def chk():
                assert i in normed, i
            K.py(chk)
            transposes(ch['hb'], [ch['hb']], n, 16, lambda k: hT[:, k, 0:n], [hT], [ch['bT']],
                       dst_grp=lambda k0, g: hT[:, k0:k0 + g, 0:n])
            if i + 3 < NBLK:
                norm_a(i + 3)

        for i in range(3):
            load_x(i)
        for i in range(3):
            norm_a(i)
        recs = []
        for blk in range(NB):
            ch = CH[len(recs) % 3]
            K.record()
            head(len(recs))
            latents(ch, 128)
            key_side(ch, ch['cb'][0:128, :], [ch['cb']], ch['kpf'][0:128, :], [ch['kpf']], 128, blk, blk * 128, sspe[:, blk:blk + 1], sspe)
            recs.append(K.stop_record())

        for bi, (r0, n) in enumerate(BLOCKS):
            ch = CH[len(recs) % 3]
            hT, cf, cb, kpf, qlb, bT, bC = ch['hT'], ch['cf'], ch['cb'], ch['kpf'], ch['qlb'], ch['bT'], ch['bC']
            K.record()
            head(len(recs))
            K.dma(hscr_t[:, :, r0:r0 + n], hT[:, :, 0:n], reads=[hT], writes=[hscr])
            latents(ch, n)
            mm_tm(PB[bT][0:n, :], bT, lambda k, hT=hT, n=n: hT[:, k, 0:n], lambda k: Wa[:, k, 0:512], 16, [hT, Wa])
            K.dma(lat_own[r0:r0 + n, :], cf[0:n, :], reads=[cf], final=True)
            K.dma(kpe_own[r0:r0 + n, :], kpf[0:n, :], reads=[kpf], final=True)
            if bi == 8:
                K.op('dve', lambda e, cb=cb: e.tensor_copy(out=cnew[:, :], in_=cb[0:64, :]), reads=[cb], writes=[cnew])
                K.op('dve', lambda e, kpf=kpf: e.tensor_copy(out=kpnew[:, :], in_=kpf[0:64, :]), reads=[kpf], writes=[kpnew])
            rms_stats(PB[bT][0:n, :], n, 512, qlb[0:n, :], ch['ss'][2], ch['r'][2], reads=[PB[bT]], extra_w=[qlb])
            K.op('dve', lambda e, ch=ch, qlb=qlb, bT=bT, n=n: e.scalar_tensor_tensor(
                out=qlb[0:n, :], in0=PB[bT][0:n, :], scalar=ch['r'][2][0:n, :], in1=gql_bc[0:n, :],
                op0=ALU.mult, op1=ALU.mult), reads=[PB[bT], ch['r'][2], gql_bc], writes=[qlb])
            transposes(qlb, [qlb], n, 4, lambda k, r0=r0, n=n: qlatT[:, k, r0:r0 + n], [qlatT], [bC],
                       dst_grp=lambda k0, g, r0=r0, n=n: qlatT[:, k0:k0 + g, r0:r0 + n])
            recs.append(K.stop_record())
        ch = CH[len(recs) % 3]
        K.record()
        head(len(recs))
        K.dma(hscr_t[:, :, TOWN:TOWN + 128], ch['hT'][:, :, :], reads=[ch['hT']], writes=[hscr])
        recs.append(K.stop_record())
        assert len(recs) == NBLK
        K.replay(recs, width=3, stagger=0.33)
        K.barrier()
        if stage <= 1:
            K.finish()
            return nc

        A.at(R1)
        V4 = A.alloc("V4", [128, NKB, 512], BF16)
        A.at(PBOFF)
        knT = A.alloc("knT", [128, 4128], BF16)
        qnT = A.alloc("qnT", [128, TOWN], BF16)
        qrT = A.alloc("qrT", [128, TOWN], BF16)
        Tc = A.alloc("Tc", [64, TOWN])
        Ts = A.alloc("Ts", [64, TOWN])
        pTs = [A.alloc(f"pT{i}", [128, 512], BF16) for i in range(3)]
        sqall = Buf(A.t, "sqall", ap=gbc._ap.bitcast(BF16))
        sqn = A.alloc("sqn", [128, 512], BF16)
        sqr = A.alloc("sqr", [64, 512], BF16)
        off_scr = A.off
        rrep = A.alloc("rrep", [128, 512])
        rrec = A.alloc("rrec", [128, 512])
        t1 = A.alloc("t1", [64, 512])
        t2 = A.alloc("t2", [64, 512])
        assert A.off == off_scr + 8192
        lat4 = [Buf(A.t, f"lat4_{i}", ap=A.t[0:128, (off_scr + i * 4096) // 4:(off_scr + (i + 1) * 4096) // 4].bitcast(BF16)
                    .rearrange("p (a b) -> p a b", a=4)) for i in range(2)]
        kpg4, kt1_4, kt2_4 = [Buf(A.t, f"ks4_{i}", ap=pTs[i]._ap.bitcast(F32).rearrange("p (a b) -> p a b", a=4)) for i in range(3)]
        rks = A.alloc("rks", [128, NKB])
        rkt = A.alloc("rkt", [128, NKB])
        KSPL = 24
        rksP = [Buf(A.t, f"rks{i}", ap=rks._ap) for i in range(2)]
        rktP = [Buf(A.t, f"rkt{i}", ap=rkt._ap) for i in range(2)]
        sqP = [Buf(A.t, f"sqP{i}", ap=sqall._ap) for i in range(2)]
        recip = A.alloc("recip", [128, 512])
        recips = [recip, rrep]
        wq_h = A.alloc("wq_h", [128, 4, 192], BF16)
        wrot_h = A.alloc("wrot_h", [128, 4, 64], BF16)
        wk_h = A.alloc("wk_h", [128, 4, 128], BF16)
        wv_g = A.alloc("wv_g", [128, 4, 512], BF16)
        kp4 = [A.alloc(f"kp4_{i}", [128, 4, 64]) for i in range(2)]
        krb4 = [A.alloc(f"krb4_{i}", [128, 4, 64], BF16) for i in range(2)]
        K.op('dve', lambda e: e.memset(qrT[64:128, :], 0.0), writes=[qrT])
        K.dma(qrT[64:80, 0:1024], qmask, writes=[qrT], q='pool')
        K.dma(Tc[:], qtab_c, writes=[Tc])
        K.dma(Ts[:], qtab_s, writes=[Ts])
        K.op('dve', lambda e: e.tensor_scalar_mul(out=Tc[:], in0=Tc[:], scalar1=gq_r[:, 0:1]), reads=[Tc, gq_r], writes=[Tc])
        K.op('dve', lambda e: e.tensor_scalar_mul(out=Ts[:], in0=Ts[:], scalar1=gq_rp[:, 0:1]), reads=[Ts, gq_rp], writes=[Ts])

        pT_i = [0]
        sc_i = [0]

        def attention_head(h, prob):
            hh = h % 4
            if prob == 0:
                nkb, nkeys = NB, SEQ
                qchunks = [(0, 512), (512, 512)]
            else:
                nkb, nkeys = NKB, 4128
                qchunks = [(1024 + 32 * (prob - 1), 32)]
            def v_build():
                for blk in range(nkb):
                    n = min(128, nkeys - blk * 128)
                    bank = 5 + blk % 2
                    mm_tm(PB[bank][0:n, :], bank, lambda k, blk=blk, n=n: ckvT[:, k, blk * 128:blk * 128 + n],
                          lambda k: wv_g[:, k, :], 4, [ckvT, wv_g])
                    K.op(K.ev(), (lambda bank, blk, n: (lambda e: (
                        e.activation(out=V4[0:n, blk, :], in_=PB[bank][0:n, :], func=AF.Copy) if e is nc.scalar
                        else e.tensor_copy(out=V4[0:n, blk, :], in_=PB[bank][0:n, :]))))(bank, blk, n), reads=[PB[bank]], writes=[V4])
                g2 = (h // 4 + 1) % 4
                K.dma(wv_g[:], w_vb[:, g2 * 512:(g2 + 1) * 512].rearrange("(k p) n -> p k n", p=128), writes=[wv_g], q='pool')

            def q_mm(c0, w):
                mm_tm(PB[0][:, 0:w], 0, lambda k: wq_h[:, k, 0:128], lambda k: qlatT[:, k, c0:c0 + w], 4, [wq_h, qlatT])
                mm_tm(PB[1][0:64, 0:w], 1, lambda k: wq_h[:, k, 128:192], lambda k: qlatT[:, k, c0:c0 + w], 4, [wq_h, qlatT])
                mm_tm(PB[2][0:64, 0:w], 2, lambda k: wrot_h[:, k, :], lambda k: qlatT[:, k, c0:c0 + w], 4, [wrot_h, qlatT])
                K.op('act', lambda e: e.activation(out=sqn[:, 0:w], in_=PB[0][:, 0:w], func=AF.Square), reads=[PB[0]], writes=[sqn])
                K.op('act', lambda e: e.activation(out=sqr[:, 0:w], in_=PB[1][0:64, 0:w], func=AF.Square), reads=[PB[1]], writes=[sqr])

            def q_fin(c0, w):
                K.op('pe', lambda e: e.matmul(PB[3][:, 0:w], lhsT=ones_b[:, :], rhs=sqn[:, 0:w], start=True, stop=False),
                     reads=[ones_b, sqn], writes=[PB[3]], signal=False)
                K.op('pe', lambda e: e.matmul(PB[3][:, 0:w], lhsT=ones_b[0:64, :], rhs=sqr[:, 0:w], start=False, stop=True),
                     reads=[ones_b, sqr], writes=[PB[3]])
                K.op('act', lambda e: e.activation(out=rrep[:, 0:w], in_=PB[3][:, 0:w], func=AF.Ln, bias=eps6[:], scale=1.0 / 192),
                     reads=[PB[3], eps6], writes=[rrep])
                K.op('act', lambda e: e.activation(out=rrec[:, 0:w], in_=rrep[:, 0:w], func=AF.Exp, scale=-0.5), reads=[rrep], writes=[rrec])
                K.op('dve', lambda e: e.scalar_tensor_tensor(out=qnT[:, c0:c0 + w], in0=PB[0][:, 0:w], scalar=gqk[:, 0:1],
                                                             in1=rrec[:, 0:w], op0=ALU.mult, op1=ALU.mult),
                     reads=[PB[0], gqk, rrec], writes=[qnT])
                K.op('dve', lambda e: e.tensor_tensor(out=t1[:, 0:w], in0=PB[1][0:64, 0:w], in1=Tc[:, c0:c0 + w], op=ALU.mult),
                     reads=[PB[1], Tc], writes=[t1])
                K.op('dve', lambda e: e.tensor_tensor(out=t2[:, 0:w], in0=PB[2][0:64, 0:w], in1=Ts[:, c0:c0 + w], op=ALU.mult),
                     reads=[PB[2], Ts], writes=[t2])
                K.op('dve', lambda e: e.tensor_tensor(out=t1[:, 0:w], in0=t1[:, 0:w], in1=t2[:, 0:w], op=ALU.add),
                     reads=[t1, t2], writes=[t1])
                K.op('dve', lambda e: e.tensor_tensor(out=qrT[0:64, c0:c0 + w], in0=t1[:, 0:w], in1=rrec[0:64, 0:w], op=ALU.mult),
                     reads=[t1, rrec], writes=[qrT])

            nkc = (nkeys + 511) // 512

            KBANKS = [4, 5, 6]

            def k_mm(kc):
                c0 = kc * 512
                w = min(512, nkeys - c0)
                bank = KBANKS[kc % 3]
                mm_tm(PB[bank][:, 0:w], bank, lambda k: wk_h[:, k, :], lambda k: ckvT[:, k, c0:c0 + w], 4, [wk_h, ckvT])

            def k_fin(kc):
                c0 = kc * 512
                w = min(512, nkeys - c0)
                bank = KBANKS[kc % 3]
                sqb, sqd = (sqP[0 if kc < KSPL // 4 else 1], sqall[:, c0:c0 + w]) if kc < 8 else (sq_last, sq_last[:, 0:w])
                K.op('act', lambda e: e.activation(out=sqd, in_=PB[bank][:, 0:w], func=AF.Square), reads=[PB[bank]], writes=[sqb])
                K.op('dve', lambda e: e.tensor_copy(out=knT[:, c0:c0 + w], in_=PB[bank][:, 0:w]), reads=[PB[bank]], writes=[knT])

            def k_ss(b_lo, b_hi):
                for blk in range(b_lo, b_hi):
                    nb_ = min(128, nkeys - blk * 128)
                    sqb, src = (sqP[0 if blk < KSPL else 1], sqall[:, blk * 128:blk * 128 + nb_]) if blk < 32 else (sq_last, sq_last[:, 0:nb_])
                    K.op('pe', lambda e, nb_=nb_, blk=blk, src=src: e.matmul(
                        PB[7][0:nb_, blk:blk + 1], lhsT=src, rhs=ones_b[:, 0:1], start=True, stop=True),
                        reads=[sqb, ones_b], writes=[PB[7]], signal=(blk == b_hi - 1))

            def rk_chain(pr, c_lo, c_hi, part):
                rkt_, rks_ = rktP[part], rksP[part]
                K.op('dve', lambda e: e.tensor_tensor(
                    out=rkt[0:pr, c_lo:c_hi], in0=PB[7][0:pr, c_lo:c_hi], in1=sspe[0:pr, c_lo:c_hi], op=ALU.add),
                    reads=[PB[7], sspe], writes=[rkt_])
                K.op('act', lambda e: e.activation(
                    out=rkt[0:pr, c_lo:c_hi], in_=rkt[0:pr, c_lo:c_hi], func=AF.Ln, bias=eps192[0:pr, :], scale=1.0),
                    reads=[rkt_, eps192], writes=[rkt_])
                K.op('act', lambda e: e.activation(
                    out=rks[0:pr, c_lo:c_hi], in_=rkt[0:pr, c_lo:c_hi], func=AF.Exp, scale=-0.5), reads=[rkt_], writes=[rks_])

            def k_seq(a, b):
                if a >= b:
                    return
                k_mm(a)
                for kc in range(a, b):
                    if kc + 1 < b:
                        k_mm(kc + 1)
                    k_fin(kc)

            k_seq(0, 3)
            q_mm(*qchunks[0])
            k_seq(3, 6)
            q_fin(*qchunks[0])
            k_seq(6, nkc)
            k_ss(0, KSPL)
            k_ss(KSPL, nkb)
            rk_chain(128, 0, KSPL, 0)
            rk_chain(128, KSPL, NB, 1)
            if nkb == NKB:
                rk_chain(32, NB, NKB, 1)
            if hh == 0:
                v_build()
            for qc in qchunks[1:]:
                q_mm(*qc)
                q_fin(*qc)
            h2 = (h + 1) % 16
            K.dma(wk_h[:], w_kb[:, h2 * 128:(h2 + 1) * 128].rearrange("(k p) n -> p k n", p=128), writes=[wk_h], q='pool')
            K.dma(wq_h[:], w_qb[:, h2 * 192:(h2 + 1) * 192].rearrange("(k p) n -> p k n", p=128), writes=[wq_h], q='pool')
            K.dma(wrot_h[:], w_qbrot[:, h2 * 64:(h2 + 1) * 64].rearrange("(k p) n -> p k n", p=128), writes=[wrot_h], q='pool')
            DEPTH = 2
            scbanks = [4, 7, 6]
            if prob == 0:
                last_kb = [15, 31]
                pairs = []
                for kb in range(NB):
                    s0 = kb // 4
                    for ci in range(2):
                        lo = max(s0 * 128, ci * 512)
                        hi = (ci + 1) * 512
                        if lo >= hi:
                            continue
                        pairs.append((kb, ci, lo, hi, lo - ci * 512, (ci * 512 <= s0 * 128 < hi), s0))

                def a_scores(i):
                    kb, ci, lo, hi, lr, diag, s0 = pairs[i]
                    scb = scbanks[i % 3]
                    K.op('pe', lambda e: e.matmul(PB[scb][:, lr:512], lhsT=knT[:, kb * 128:(kb + 1) * 128], rhs=qnT[:, lo:hi],
                                                  start=True, stop=False), reads=[knT, qnT], writes=[PB[scb]], signal=False)
                    K.op('pe', lambda e: e.matmul(PB[scb][:, lr:512], lhsT=kropeT[:, kb * 128:(kb + 1) * 128], rhs=qrT[:, lo:hi],
                                                  start=False, stop=True), reads=[kropeT, qrT], writes=[PB[scb]])

                def a_rest(i):
                    kb, ci, lo, hi, lr, diag, s0 = pairs[i]
                    scb = scbanks[i % 3]
                    pT = pTs[i % 3]
                    K.op('act', lambda e: e.activation(out=pT[:, lr:512], in_=PB[scb][:, lr:512], func=AF.Exp, scale=rks[:, kb:kb + 1]),
                         reads=[PB[scb], rksP[0 if kb < KSPL else 1]], writes=[pT])
                    K.op('pe', lambda e: e.matmul(PB[ci][:, lr:512], lhsT=V4[:, kb, hh * 128:(hh + 1) * 128], rhs=pT[:, lr:512],
                                                  start=(kb == 0), stop=(kb == last_kb[ci])), reads=[V4, pT], writes=[PB[ci]], signal=False)
                    K.op('pe', lambda e: e.matmul(PB[2 + ci][:, lr:512], lhsT=ones_b[:, :], rhs=pT[:, lr:512],
                                                  start=(kb == 0), stop=(kb == last_kb[ci])), reads=[ones_b, pT], writes=[PB[2 + ci]])

                npairs = len(pairs)
                for i in range(min(DEPTH, npairs)):
                    a_scores(i)
                for i in range(npairs):
                    if i + DEPTH < npairs:
                        a_scores(i + DEPTH)
                    a_rest(i)
                for ci in range(2):
                    rb_ = recips[ci]
                    K.op('act', lambda e, ci=ci, rb_=rb_: e.activation(out=rb_[:, :], in_=PB[2 + ci][:, :], func=AF.Ln), reads=[PB[2 + ci]], writes=[rb_])
                    K.op('act', lambda e, rb_=rb_: e.activation(out=rb_[:, :], in_=rb_[:, :], func=AF.Exp, scale=-1.0), reads=[rb_], writes=[rb_])
                    K.op('dve', lambda e, ci=ci, rb_=rb_: e.tensor_tensor(out=oT_all[:, h, ci * 512:(ci + 1) * 512], in0=PB[ci][:, :], in1=rb_[:, :],
                                                                 op=ALU.mult), reads=[PB[ci], rb_], writes=[oT_all])
            else:
                c0 = 1024 + 32 * (prob - 1)

                def s_scores(kb):
                    nk = min(128, 4128 - kb * 128)
                    scb = scbanks[kb % 3]
                    K.op('pe', lambda e: e.matmul(PB[scb][0:nk, 0:32], lhsT=knT[:, kb * 128:kb * 128 + nk], rhs=qnT[:, c0:c0 + 32],
                                                  start=True, stop=False), reads=[knT, qnT], writes=[PB[scb]], signal=False)
                    K.op('pe', lambda e: e.matmul(PB[scb][0:nk, 0:32], lhsT=kropeT[:, kb * 128:kb * 128 + nk], rhs=qrT[:, c0:c0 + 32],
                                                  start=False, stop=True), reads=[kropeT, qrT], writes=[PB[scb]])

                def s_rest(kb):
                    nk = min(128, 4128 - kb * 128)
                    scb = scbanks[kb % 3]
                    pT = pTs[kb % 3]
                    K.op('act', lambda e: e.activation(out=pT[0:nk, 0:32], in_=PB[scb][0:nk, 0:32], func=AF.Exp, scale=rks[0:nk, kb:kb + 1]),
                         reads=[PB[scb], rksP[0 if kb < KSPL else 1]], writes=[pT])
                    K.op('pe', lambda e: e.matmul(PB[0][:, 0:32], lhsT=V4[0:nk, kb, hh * 128:(hh + 1) * 128], rhs=pT[0:nk, 0:32],
                                                  start=(kb == 0), stop=(kb == NKB - 1)), reads=[V4, pT], writes=[PB[0]], signal=False)
                    K.op('pe', lambda e: e.matmul(PB[2][:, 0:32], lhsT=ones_b[0:nk, :], rhs=pT[0:nk, 0:32],
                                                  start=(kb == 0), stop=(kb == NKB - 1)), reads=[ones_b, pT], writes=[PB[2]])

                for kb in range(DEPTH):
                    s_scores(kb)
                for kb in range(NKB):
                    if kb + DEPTH < NKB:
                        s_scores(kb + DEPTH)
                    s_rest(kb)
                K.op('act', lambda e: e.activation(out=recip[:, 0:32], in_=PB[2][:, 0:32], func=AF.Ln), reads=[PB[2]], writes=[recip])
                K.op('act', lambda e: e.activation(out=recip[:, 0:32], in_=recip[:, 0:32], func=AF.Exp, scale=-1.0), reads=[recip], writes=[recip])
                K.op('dve', lambda e: e.tensor_tensor(out=oT_all[:, h, c0:c0 + 32], in0=PB[0][:, 0:32], in1=recip[:, 0:32], op=ALU.mult),
                     reads=[PB[0], recip], writes=[oT_all])

        def key_side4a(bi, s4):
            c0 = s4 * 512
            b0 = s4 * 4
            lt, kp, kr = lat4[s4 % 2], kp4[s4 % 2], krb4[s4 % 2]
            K.dma(lt[:], latc[bi, c0:c0 + 512, :].rearrange("(a p) n -> p a n", p=128), writes=[lt], q='pool')
            K.dma(kp[:], kpec[bi, c0:c0 + 512, :].rearrange("(a p) n -> p a n", p=128), writes=[kp])
            for k in range(4):
                pv = bfv(k)
                for a in range(4):
                    K.op('pe', lambda e, k=k, a=a, pv=pv: e.transpose(out=pv[:, a * 128:(a + 1) * 128], in_=lt[:, a, k * 128:(k + 1) * 128],
                                                                     identity=ident_b[:, :]),
                         reads=[lt, ident_b], writes=[PB[k]], signal=(a == 3))
                K.op(K.ev(), (lambda k, pv: (lambda e: (
                    e.activation(out=ckvT[:, k, c0:c0 + 512], in_=pv[:, 0:512], func=AF.Copy) if e is nc.scalar
                    else e.tensor_copy(out=ckvT[:, k, c0:c0 + 512], in_=pv[:, 0:512]))))(k, pv), reads=[PB[k]], writes=[ckvT])
            for a in range(4):
                K.op('act', lambda e, a=a: e.activation(out=kt2_4[:, a, :], in_=kp[:, a, :], func=AF.Square, accum_out=sspe[:, b0 + a:b0 + a + 1]),
                     reads=[kp], writes=[kt2_4, sspe])
            for a in range(4):
                K.op('dve', lambda e, a=a: e.tensor_tensor(out=kpg4[:, a, :], in0=kp[:, a, :], in1=gkr_bc[:, :], op=ALU.mult),
                     reads=[kp, gkr_bc], writes=[kpg4])
            K.op('dve', lambda e: e.tensor_tensor(out=kt1_4[:, :, :], in0=kpg4[:, :, :], in1=ktc[:, b0:b0 + 4, :], op=ALU.mult),
                 reads=[kpg4, ktc], writes=[kt1_4])
            K.op('dve', lambda e: e.tensor_tensor(out=kt2_4[:, :, 0:32], in0=kpg4[:, :, 32:64], in1=kts[:, b0:b0 + 4, 0:32], op=ALU.mult),
                 reads=[kpg4, kts], writes=[kt2_4])
            K.op('dve', lambda e: e.tensor_tensor(out=kt2_4[:, :, 32:64], in0=kpg4[:, :, 0:32], in1=kts[:, b0:b0 + 4, 32:64], op=ALU.mult),
                 reads=[kpg4, kts], writes=[kt2_4])
            K.op('dve', lambda e: e.tensor_tensor(out=kr[:, :, :], in0=kt1_4[:, :, :], in1=kt2_4[:, :, :], op=ALU.add),
                 reads=[kt1_4, kt2_4], writes=[kr])

        def key_side4b(s4):
            c0 = s4 * 512
            kr = krb4[s4 % 2]
            pr = bfv(4, 64)
            for a in range(4):
                K.op('pe', lambda e, a=a: e.transpose(out=pr[:, a * 128:(a + 1) * 128], in_=kr[:, a, :], identity=ident_b[:, :]),
                     reads=[kr, ident_b], writes=[PB[4]], signal=(a == 3))
            K.op('act', lambda e: e.activation(out=kropeT[0:64, c0:c0 + 512], in_=pr[:, 0:512], func=AF.Copy), reads=[PB[4]], writes=[kropeT])

        K.dma(wv_g[:], w_vb[:, 0:512].rearrange("(k p) n -> p k n", p=128), writes=[wv_g], q='pool')
        K.dma(wk_h[:], w_kb[:, 0:128].rearrange("(k p) n -> p k n", p=128), writes=[wk_h], q='pool')
        K.dma(wq_h[:], w_qb[:, 0:192].rearrange("(k p) n -> p k n", p=128), writes=[wq_h], q='pool')
        K.dma(wrot_h[:], w_qbrot[:, 0:64].rearrange("(k p) n -> p k n", p=128), writes=[wrot_h], q='pool')
        for prob in range(3):
            if prob > 0:
                bi = prob - 1
                K.barrier()
                for s4 in range(8):
                    key_side4a(bi, s4)
                    if s4 >= 1:
                        key_side4b(s4 - 1)
                key_side4b(7)
                pb = 32 * bi
                key_side(CH[0], cnew[pb:pb + 32, :], [cnew], kpnew[pb:pb + 32, :], [kpnew], 32, 32, 4096,
                         sspe_new[pb:pb + 32, 0:1], sspe_new, pb=pb)
                K.dma(sspe[0:32, 32:33], sspe_new[pb:pb + 32, 0:1], reads=[sspe_new], writes=[sspe])
                K.barrier()
            for h in range(16):
                attention_head(h, prob)
        K.barrier()
        if stage <= 2:
            K.finish()
            return nc
        A.at(R1)
        hT_own = A.alloc("hT_own", [128, 16, TOWN], BF16)
        assert A.off == R2
        dT = A.alloc("dT", [128, 8, TOWN], BF16)
        a_memT = A.alloc("a_memT", [128, 8, TOWN], BF16)
        C_T = A.off
        assert C_T == 126976
        A.at(C_T)
        u_tm = A.alloc("u_tm", [128, 10, 1024], BF16)
        hT_prev = A.alloc("hT_prev", [128, 16, 128], BF16)
        spool_b = A.alloc("spool_b", [32, 1024], BF16)
        mcur_b = A.alloc("mcur_b", [128, 8, 128], BF16)
        mprev_b = A.alloc("mprev_b", [128, 32, 128], BF16)
        mcurS_b = A.alloc("mcurS_b", [64, 4, 64], BF16)
        mprevS_b = A.alloc("mprevS_b", [32, 4, 64], BF16)
        wpool_b = A.alloc("wpool_b", [128, 8, 512], BF16)
        uf = A.alloc("uf", [128, 2, 1024])
        c_ss = A.alloc("c_ss", [128, 1])
        c_r = A.alloc("c_r", [128, 1])
        C1_T = A.off
        xbc = [A.alloc(f"xbc{i}", [128, D]) for i in range(2)]
        hbc = A.alloc("hbc", [128, D], BF16)
        K.dma(hT_own[:, :, :], hscr_t[:, :, 0:TOWN], reads=[hscr], writes=[hT_own])
        K.dma(hT_prev[:, :, :], hscr_t[:, :, TOWN:TOWN + 128], reads=[hscr], writes=[hT_prev])
        A.at(C1_T)
        wu = [A.alloc(f"wu{i}", [128, 16, 256], BF16) for i in range(2)]
        tblocks = [(s, r0, n) for s, (r0, n) in enumerate(BLOCKS)] + [(9, None, 128)]

        def c1_consts():
            K.dma(spool_b[:], spool, writes=[spool_b], q='pool')
            K.dma(mcur_b[:], mcur, writes=[mcur_b], q='pool')
            K.dma(mprev_b[:], mprev, writes=[mprev_b], q='pool')
            K.dma(mcurS_b[:], mcurS, writes=[mcurS_b], q='pool')
            K.dma(mprevS_b[:], mprevS, writes=[mprevS_b], q='pool')
        for ct in range(4):
            wt_ = wu[ct % 2]
            K.dma(wt_[:], w_in[:, OFF_U + ct * 256:OFF_U + (ct + 1) * 256].rearrange("(k p) n -> p k n", p=128), writes=[wt_], q='pool')
            if ct == 1:
                c1_consts()
            for ti, (slot, r0, n) in enumerate(tblocks):
                bank = 2 + ti % 4
                if slot == 9:
                    lf = lambda k: hT_prev[:, k, :]
                    rd = [hT_prev, wt_]
                else:
                    lf = lambda k, r0=r0, n=n: hT_own[:, k, r0:r0 + n]
                    rd = [hT_own, wt_]
                mm_tm(PB[bank][0:n, 0:256], bank, lf, lambda k, wt_=wt_: wt_[:, k, :], 16, rd)
                K.op('act', lambda e, bank=bank, n=n, slot=slot, ct=ct: e.activation(
                    out=u_tm[0:n, slot, ct * 256:(ct + 1) * 256], in_=PB[bank][0:n, 0:256], func=AF.Copy), reads=[PB[bank]], writes=[u_tm])
                if slot in (7, 8):
                    K.op('dve', lambda e, bank=bank, n=n, slot=slot, ct=ct: e.tensor_copy(
                        out=uf[0:n, slot - 7, ct * 256:(ct + 1) * 256], in_=PB[bank][0:n, 0:256]), reads=[PB[bank]], writes=[uf])
        K.dma(poolp_out[:, :], uf[113:128, 0, :], reads=[uf], final=True)
        K.dma(pools_out[0], uf[17:32, 1, :], reads=[uf], final=True)
        K.dma(pools_out[1], uf[49:64, 1, :], reads=[uf], final=True)
        for s, (r0, n) in enumerate(BLOCKS):
            for half in range(2):
                bank = 6 + (s * 2 + half) % 2
                for q4 in range(4):
                    c8 = half * 4 + q4
                    g = c8 // 2
                    o = PB[bank][:, q4 * 128:q4 * 128 + n]
                    if s < 8:
                        idx = g if s == 0 else 4 + g
                        K.op('pe', lambda e, o=o, s=s, c8=c8, idx=idx: e.matmul(
                            o, lhsT=u_tm[:, s, c8 * 128:(c8 + 1) * 128], rhs=mcur_b[:, idx, :], start=True, stop=False),
                            reads=[u_tm, mcur_b], writes=[PB[bank]], signal=False)
                        K.op('pe', lambda e, o=o, s=s, c8=c8, g=g: e.matmul(
                            o, lhsT=u_tm[:, 9, c8 * 128:(c8 + 1) * 128], rhs=mprev_b[:, s * 4 + g, :], start=False, stop=True),
                            reads=[u_tm, mprev_b], writes=[PB[bank]], signal=(q4 == 3))
                    else:
                        K.op('pe', lambda e, o=o, c8=c8, g=g: e.matmul(
                            o, lhsT=u_tm[0:64, 8, c8 * 128:(c8 + 1) * 128], rhs=mcurS_b[:, g, :], start=True, stop=False),
                            reads=[u_tm, mcurS_b], writes=[PB[bank]], signal=False)
                        K.op('pe', lambda e, o=o, c8=c8, g=g: e.matmul(
                            o, lhsT=spool_b[:, c8 * 128:(c8 + 1) * 128], rhs=mprevS_b[:, g, :], start=False, stop=True),
                            reads=[spool_b, mprevS_b], writes=[PB[bank]], signal=(q4 == 3))
                K.op(K.ev(), (lambda bank, half, r0, n: (lambda e: (
                    e.activation(out=dT[:, half * 4:half * 4 + 4, r0:r0 + n],
                                 in_=PB[bank][:, :].rearrange("p (a b) -> p a b", a=4)[:, :, 0:n], func=AF.Copy) if e is nc.scalar
                    else e.tensor_copy(out=dT[:, half * 4:half * 4 + 4, r0:r0 + n],
                                       in_=PB[bank][:, :].rearrange("p (a b) -> p a b", a=4)[:, :, 0:n]))))(bank, half, r0, n),
                    reads=[PB[bank]], writes=[dT])
        K.barrier()
        A.at(C_T)
        wmqs = [A.alloc(f"wmq{i}", [128, 16, 512], BF16) for i in range(2)]
        qmT_all = A.alloc("qmT_all", [128, 8, TOWN], BF16)
        sqh = [[A.alloc(f"sqh{i}{j}", [128, 512], BF16) for j in range(2)] for i in range(2)]
        rrc = [A.alloc(f"rrc{i}", [128, 512]) for i in range(2)]
        pTm = A.alloc("pTm", [128, 2, 512], BF16)
        recm = A.alloc("recm", [128, 512])
        gmqT = A.alloc("gmqT", [128, 2])
        cmk_b = A.alloc("cmk_b", [128, 2, 1024], BF16)
        memkT_s = [A.alloc(f"memkT_s{i}", [128, 8, 256], BF16) for i in range(2)]
        memv_s = [A.alloc(f"memv_s{i}", [128, 2, 1024], BF16) for i in range(2)]
        for i in range(2):
            K.dma(wmqs[i][:], w_in[:, OFF_MQ + i * 512:OFF_MQ + (i + 1) * 512].rearrange("(k p) n -> p k n", p=128),
                  writes=[wmqs[i]], q='pool')
        with nc.allow_non_contiguous_dma(reason="tiny per-partition gain vector"):
            K.dma(gmqT[:], g_mem_q.rearrange("(c p) -> p c", p=128), writes=[gmqT])
        for bi in range(2):
            K.dma(cmk_b[:], cmk[bi].rearrange("(a p) n -> p a n", p=128), writes=[cmk_b], q='pool')
            K.dma(memv_s[bi][:], cmv[bi].rearrange("(a p) n -> p a n", p=128), writes=[memv_s[bi]], q='pool')
            build_memkT(cmk_b, memkT_s[bi])
        MEM_SCALE = 1.0 / 16.0
        c2chunks = [(0, 512), (512, 512), (1024, 64)]
        ui = 0
        for hh in range(4):
            wm = wmqs[hh // 2]
            for (c0, w) in c2chunks:
                st_ = ui % 2
                ui += 1
                bA = (0, 1) if st_ == 0 else (3, 4)
                bS = 2 if st_ == 0 else 5
                for dc in range(2):
                    col0 = (hh % 2) * 256 + dc * 128
                    mm_tm(PB[bA[dc]][:, 0:w], bA[dc], lambda k, wm=wm, col0=col0: wm[:, k, col0:col0 + 128],
                          lambda k, c0=c0, w=w: hT_own[:, k, c0:c0 + w], 16, [wm, hT_own])
                    K.op('act', lambda e, dc=dc, st_=st_, w=w, bA=bA: e.activation(out=sqh[st_][dc][:, 0:w], in_=PB[bA[dc]][:, 0:w], func=AF.Square),
                         reads=[PB[bA[dc]]], writes=[sqh[st_][dc]])
                for dc in range(2):
                    K.op('pe', lambda e, dc=dc, st_=st_, w=w, bS=bS: e.matmul(PB[bS][:, 0:w], lhsT=ones_b[:, :], rhs=sqh[st_][dc][:, 0:w],
                                                                             start=(dc == 0), stop=(dc == 1)),
                         reads=[ones_b, sqh[st_][dc]], writes=[PB[bS]], signal=(dc == 1))
                K.op('act', lambda e, st_=st_, w=w, bS=bS: e.activation(out=rrc[st_][:, 0:w], in_=PB[bS][:, 0:w], func=AF.Ln, bias=eps6[:], scale=1.0 / 256),
                     reads=[PB[bS], eps6], writes=[rrc[st_]])
                K.op('act', lambda e, st_=st_, w=w: e.activation(out=rrc[st_][:, 0:w], in_=rrc[st_][:, 0:w], func=AF.Exp, scale=-0.5),
                     reads=[rrc[st_]], writes=[rrc[st_]])
                for dc in range(2):
                    K.op('dve', lambda e, dc=dc, st_=st_, w=w, c0=c0, hh=hh, bA=bA: e.scalar_tensor_tensor(
                        out=qmT_all[:, hh * 2 + dc, c0:c0 + w], in0=PB[bA[dc]][:, 0:w], scalar=gmqT[:, dc:dc + 1], in1=rrc[st_][:, 0:w],
                        op0=ALU.mult, op1=ALU.mult), reads=[PB[bA[dc]], gmqT, rrc[st_]], writes=[qmT_all])
        units = [(memkT_p, memv_p, hh, c0, 512) for hh in range(4) for c0 in (0, 512)]
        units += [(memkT_s[bi], memv_s[bi], hh, 1024 + 32 * bi, 32) for bi in range(2) for hh in range(4)]

        def m_scores(u):
            mkT, mv, hh, c0, w = units[u]
            sc = (0, 1) if u % 2 == 0 else (2, 3)
            for mb in range(2):
                for dc in range(2):
                    K.op('pe', lambda e, mb=mb, dc=dc: e.matmul(
                        PB[sc[mb]][:, 0:w], lhsT=mkT[:, hh * 2 + dc, mb * 128:(mb + 1) * 128], rhs=qmT_all[:, hh * 2 + dc, c0:c0 + w],
                        start=(dc == 0), stop=(dc == 1)), reads=[mkT, qmT_all], writes=[PB[sc[mb]]], signal=(dc == 1))

        def m_rest(u):
            mkT, mv, hh, c0, w = units[u]
            sc = (0, 1) if u % 2 == 0 else (2, 3)
            for mb in range(2):
                K.op('act', lambda e, mb=mb: e.activation(out=pTm[:, mb, 0:w], in_=PB[sc[mb]][:, 0:w], func=AF.Exp, scale=MEM_SCALE),
                     reads=[PB[sc[mb]]], writes=[pTm])
            for dvc in range(2):
                for mb in range(2):
                    K.op('pe', lambda e, dvc=dvc, mb=mb: e.matmul(
                        PB[4 + dvc][:, 0:w], lhsT=mv[:, mb, hh * 256 + dvc * 128:hh * 256 + (dvc + 1) * 128], rhs=pTm[:, mb, 0:w],
                        start=(mb == 0), stop=(mb == 1)), reads=[mv, pTm], writes=[PB[4 + dvc]], signal=(mb == 1))
            for mb in range(2):
                K.op('pe', lambda e, mb=mb: e.matmul(PB[6][:, 0:w], lhsT=ones_b[:, :], rhs=pTm[:, mb, 0:w], start=(mb == 0), stop=(mb == 1)),
                     reads=[ones_b, pTm], writes=[PB[6]], signal=(mb == 1))
            K.op('act', lambda e: e.activation(out=recm[:, 0:w], in_=PB[6][:, 0:w], func=AF.Ln), reads=[PB[6]], writes=[recm])
            K.op('act', lambda e: e.activation(out=recm[:, 0:w], in_=recm[:, 0:w], func=AF.Exp, scale=-1.0), reads=[recm], writes=[recm])
            for dvc in range(2):
                K.op('dve', lambda e, dvc=dvc: e.tensor_tensor(out=a_memT[:, hh * 2 + dvc, c0:c0 + w], in0=PB[4 + dvc][:, 0:w], in1=recm[:, 0:w],
                                                               op=ALU.mult), reads=[PB[4 + dvc], recm], writes=[a_memT])

        m_scores(0)
        for u in range(len(units)):
            if u + 1 < len(units):
                m_scores(u + 1)
            m_rest(u)
        K.barrier()
        A.at(C_T)
        mergedT = A.alloc("mergedT", [128, 16, TOWN], BF16)
        C3_T = A.off
        wg = [A.alloc(f"wg{i}", [128, 3, 16, 128], BF16) for i in range(2)]
        wmo = [A.alloc(f"wmo{i}", [128, 16, 128], BF16) for i in range(2)]
        wme = [A.alloc(f"wme{i}", [128, 8, 128], BF16) for i in range(2)]
        wpo = [A.alloc(f"wpo{i}", [128, 2, 128], BF16) for i in range(2)]
        gs = A.alloc("gs", [128, 512])
        acc = A.alloc("acc", [128, 512])
        tmpc = A.alloc("tmpc", [128, 512])
        bgT = A.alloc("bgT", [128, 48])
        psT = A.alloc("psT", [128, 16])
        with nc.allow_non_contiguous_dma(reason="tiny per-partition bias/scale vectors"):
            K.dma(bgT[:], b_gate.rearrange("(c p) -> p c", p=128), writes=[bgT])
            K.dma(psT[:], pool_scale.rearrange("(c p) -> p c", p=128), writes=[psT])
        tchunks = [(0, 512), (512, 512), (1024, 64)]
        for cg in range(16):
            x_ = cg % 2
            for br in range(3):
                K.dma(wg[x_][:, br, :, :], w_in[:, OFF_G + br * 2048 + cg * 128:OFF_G + br * 2048 + (cg + 1) * 128].rearrange(
                    "(k p) n -> p k n", p=128), writes=[wg[x_]], q='pool')
            K.dma(wmo[x_][:], w_mla_o[:, cg * 128:(cg + 1) * 128].rearrange("(k p) n -> p k n", p=128), writes=[wmo[x_]], q='pool')
            K.dma(wme[x_][:], w_mem_o[:, cg * 128:(cg + 1) * 128].rearrange("(k p) n -> p k n", p=128), writes=[wme[x_]], q='pool')
            gi = cg // 4
            K.dma(wpo[x_][:], w_pool[gi, :, (cg % 4) * 128:(cg % 4 + 1) * 128].rearrange("(k p) n -> p k n", p=128), writes=[wpo[x_]], q='pool')
            for (c0, w) in tchunks:
                for br in range(3):
                    mm_tm(PB[br][:, 0:w], br, lambda k, br=br: wg[x_][:, br, k, :], lambda k, c0=c0, w=w: hT_own[:, k, c0:c0 + w], 16,
                          [wg[x_], hT_own])
                mm_tm(PB[3][:, 0:w], 3, lambda k: wpo[x_][:, k, :], lambda k, c0=c0, w=w: dT[:, gi * 2 + k, c0:c0 + w], 2, [wpo[x_], dT])
                mm_tm(PB[4][:, 0:w], 4, lambda k: wmo[x_][:, k, :], lambda k, c0=c0, w=w: oT_all[:, k, c0:c0 + w], 16, [wmo[x_], oT_all])
                mm_tm(PB[5][:, 0:w], 5, lambda k: wme[x_][:, k, :], lambda k, c0=c0, w=w: a_memT[:, k, c0:c0 + w], 8, [wme[x_], a_memT])
                K.op('act', lambda e, w=w: e.activation(out=gs[:, 0:w], in_=PB[0][:, 0:w], func=AF.Sigmoid, bias=bgT[:, cg:cg + 1], scale=1.0),
                     reads=[PB[0], bgT], writes=[gs])
                K.op('dve', lambda e, w=w: e.scalar_tensor_tensor(out=acc[:, 0:w], in0=PB[3][:, 0:w], scalar=psT[:, cg:cg + 1], in1=gs[:, 0:w],
                                                                  op0=ALU.mult, op1=ALU.mult), reads=[PB[3], psT, gs], writes=[acc])
                K.op('act', lambda e, w=w: e.activation(out=gs[:, 0:w], in_=PB[1][:, 0:w], func=AF.Sigmoid, bias=bgT[:, 16 + cg:17 + cg], scale=1.0),
                     reads=[PB[1], bgT], writes=[gs])
                K.op('dve', lambda e, w=w: e.tensor_tensor(out=tmpc[:, 0:w], in0=PB[4][:, 0:w], in1=gs[:, 0:w], op=ALU.mult),
                     reads=[PB[4], gs], writes=[tmpc])
                K.op('dve', lambda e, w=w: e.tensor_tensor(out=acc[:, 0:w], in0=acc[:, 0:w], in1=tmpc[:, 0:w], op=ALU.add),
                     reads=[acc, tmpc], writes=[acc])
                K.op('act', lambda e, w=w: e.activation(out=gs[:, 0:w], in_=PB[2][:, 0:w], func=AF.Sigmoid, bias=bgT[:, 32 + cg:33 + cg], scale=1.0),
                     reads=[PB[2], bgT], writes=[gs])
                K.op('dve', lambda e, w=w: e.tensor_tensor(out=tmpc[:, 0:w], in0=PB[5][:, 0:w], in1=gs[:, 0:w], op=ALU.mult),
                     reads=[PB[5], gs], writes=[tmpc])
                K.op('dve', lambda e, c0=c0, w=w: e.tensor_tensor(out=mergedT[:, cg, c0:c0 + w], in0=acc[:, 0:w], in1=tmpc[:, 0:w], op=ALU.add),
                     reads=[acc, tmpc], writes=[mergedT])
        K.barrier()
        A.at(OT_OFF)
        y_acc = A.alloc("y_acc", [128, 9, D])
        assert A.off == 96256
        yb = [Buf(A.t, f"yacc{s}", ap=y_acc._ap[:, s, :]) for s in range(9)]
        wo = [A.alloc(f"wo{i}", [128, 16, 256], BF16) for i in range(2)]
        hb2s = [A.alloc(f"hb2_{i}", [128, D], BF16) for i in range(3)]
        d_sss = [A.alloc(f"d_ss{i}", [128, 1]) for i in range(3)]
        d_rs = [A.alloc(f"d_r{i}", [128, 1]) for i in range(3)]
        assert A.off <= C_T
        A.at(C3_T)
        h2T = A.alloc("h2T", [128, 16, TOWN], BF16)
        K.dma(gbc[:], g_ff.partition_broadcast(128), writes=[gbc])

        def load_wo(ct):
            K.dma(wo[ct % 2][:], w_out[:, ct * 256:(ct + 1) * 256].rearrange("(k p) n -> p k n", p=128), writes=[wo[ct % 2]], q='pool')

        def load_xres(s):
            r0, n = BLOCKS[s]
            K.dma(yb[s][0:n, :], xown[r0:r0 + n, :], writes=[yb[s]], q='pool')

        load_wo(0)
        for s in range(4):
            load_xres(s)
        load_wo(1)
        for s in range(4, 9):
            load_xres(s)

        def c4_norm_a(s):
            r0, n = BLOCKS[s]
            hb2, d_ss, d_r = hb2s[s % 3], d_sss[s % 3], d_rs[s % 3]
            rms_stats(yb[s][0:n, :], n, D, hb2[0:n, :], d_ss, d_r, reads=[yb[s]], extra_w=[hb2])
            K.op('dve', lambda e: e.scalar_tensor_tensor(out=hb2[0:n, :], in0=yb[s][0:n, :], scalar=d_r[0:n, :], in1=gbc[0:n, :],
                                                         op0=ALU.mult, op1=ALU.mult), reads=[yb[s], d_r, gbc], writes=[hb2])

        def c4_norm_t(s):
            r0, n = BLOCKS[s]
            hb2 = hb2s[s % 3]
            transposes(hb2, [hb2], n, 16, lambda k: h2T[:, k, r0:r0 + n], [h2T], [4, 5] if s % 2 == 0 else [6, 7],
                       dst_grp=lambda k0, g: h2T[:, k0:k0 + g, r0:r0 + n])

        for ct in range(8):
            wt_ = wo[ct % 2]
            if ct >= 2:
                load_wo(ct)
            for s, (r0, n) in enumerate(BLOCKS):
                bank = s % 4
                mm_tm(PB[bank][0:n, 0:256], bank, lambda k, r0=r0, n=n: mergedT[:, k, r0:r0 + n], lambda k, wt_=wt_: wt_[:, k, :], 16,
                      [mergedT, wt_])
                K.op('dve', lambda e, bank=bank, n=n, s=s, ct=ct: e.tensor_tensor(
                    out=yb[s][0:n, ct * 256:(ct + 1) * 256], in0=PB[bank][0:n, 0:256], in1=yb[s][0:n, ct * 256:(ct + 1) * 256], op=ALU.add),
                    reads=[PB[bank], yb[s]], writes=[yb[s]])
                if ct == 7:
                    if s >= 4:
                        c4_norm_t(s - 4)
                    if s >= 1:
                        c4_norm_a(s - 1)
        c4_norm_t(5)
        c4_norm_a(8)
        for s in range(6, 9):
            c4_norm_t(s)
        K.barrier()
        A.at(96256)
        fT = A.alloc("fT", [128, 8, TOWN], BF16)
        wdn = [A.alloc(f"wdn{i}", [128, D], BF16) for i in range(8)]
        wup0 = A.alloc("wup0", [128, 16, 256], BF16)
        rl = [A.alloc(f"rl{i}", [128, 512]) for i in range(2)]
        assert A.off <= C3_T, A.off
        A.at(C3_T + 16 * TOWN * 2)
        wup1 = A.alloc("wup1", [128, 16, 256], BF16)
        wup = [wup0, wup1]
        NT = 32

        def load_wup(T):
            K.dma(wup[T % 2][:], w_up[:, T * 256:(T + 1) * 256].rearrange("(k p) n -> p k n", p=128), writes=[wup[T % 2]], q='pool')

        def load_wdn(G):
            for hc in range(8):
                f_ = G * 8 + hc
                K.dma(wdn[hc][:], w_down[f_ * 128:(f_ + 1) * 128, :], writes=[wdn[hc]], q='pool')

        load_wup(0)
        load_wup(1)
        load_wdn(0)
        ui = 0
        for G in range(8):
            for q4 in range(4):
                T = G * 4 + q4
                wt_ = wup[T % 2]
                for sub in range(2):
                    hc = 2 * q4 + sub
                    for (c0, w) in tchunks:
                        bank = ui % 4
                        r_ = rl[ui % 2]
                        ui += 1
                        mm_tm(PB[bank][:, 0:w], bank, lambda k, wt_=wt_, sub=sub: wt_[:, k, sub * 128:(sub + 1) * 128],
                              lambda k, c0=c0, w=w: h2T[:, k, c0:c0 + w], 16, [wt_, h2T])
                        K.op('act', lambda e, bank=bank, w=w, r_=r_: e.activation(out=r_[:, 0:w], in_=PB[bank][:, 0:w], func=AF.Relu),
                             reads=[PB[bank]], writes=[r_])
                        K.op('dve', lambda e, w=w, r_=r_, hc=hc, c0=c0: e.tensor_tensor(out=fT[:, hc, c0:c0 + w], in0=r_[:, 0:w], in1=r_[:, 0:w],
                                                                                       op=ALU.mult), reads=[r_], writes=[fT])
                if T + 2 < NT:
                    load_wup(T + 2)
            for s, (r0, n) in enumerate(BLOCKS):
                for c4 in range(4):
                    bank = 4 + c4
                    mm_tm(PB[bank][0:n, :], bank, lambda k, r0=r0, n=n: fT[:, k, r0:r0 + n], lambda k, c4=c4: wdn[k][:, c4 * 512:(c4 + 1) * 512], 8,
                          [fT] + wdn)
                    K.op('dve', lambda e, bank=bank, n=n, s=s, c4=c4: e.tensor_tensor(
                        out=yb[s][0:n, c4 * 512:(c4 + 1) * 512], in0=PB[bank][0:n, :], in1=yb[s][0:n, c4 * 512:(c4 + 1) * 512], op=ALU.add),
                        reads=[PB[bank], yb[s]], writes=[yb[s]])
                if G == 7:
                    K.dma(y_own[r0:r0 + n, :], yb[s][0:n, :], reads=[yb[s]], final=True)
            if G + 1 < 8:
                load_wdn(G + 1)
        K.finish()
        print("program built: ninst", K.ninst, "nsem", K.nsem, flush=True)
        return nc


_PROG = {}
POOL_WINDOWS = (2, 4, 8, 16)


def _rope_tables():
    half = 32
    inv = (1.0 / (np.float32(10000.0) ** (np.arange(half, dtype=np.float32) * np.float32(2.0 / 64)))).astype(np.float32)
    return inv


def _consts(j):
    inv = _rope_tables()
    c = {}
    c["ident"] = np.eye(128, dtype=np.float32)
    pos = (np.arange(NKB)[None, :] * 128 + np.arange(128)[:, None]).astype(np.float32)
    pos[:, 32] = 4096 + (np.arange(128) % 32)
    ang = pos[:, :, None] * inv[None, None, :]
    cs, sn = np.cos(ang).astype(np.float32), np.sin(ang).astype(np.float32)
    c["ktab_c"] = np.concatenate([cs, cs], axis=-1)
    c["ktab_s"] = np.concatenate([-sn, sn], axis=-1)
    qpos = np.zeros(TOWN, np.float32)
    for s in range(8):
        qpos[s * 128:(s + 1) * 128] = (4 * s + j) * 128 + np.arange(128)
    qpos[1024:1056] = 4096 + np.arange(32)
    qpos[1056:1088] = 4096 + np.arange(32)
    qa = qpos[None, :] * inv[:, None]
    qc, qs = np.cos(qa).astype(np.float32), np.sin(qa).astype(np.float32)
    c["qtab_c"] = np.concatenate([qc, qc], axis=0)
    c["qtab_s"] = np.concatenate([-qs, qs], axis=0)
    mcur = np.zeros((128, 8, 128), np.float32)
    for g, win in enumerate(POOL_WINDOWS):
        for t in range(128):
            for first in (True, False):
                cnt = min(t + 1, win) if first else win
                idx = g if first else 4 + g
                lo = max(0, t - win + 1)
                mcur[lo:t + 1, idx, t] += 1.0 / cnt
                mcur[t, idx, t] -= 1.0
    if j != 0:
        mcur[:, 0:4, :] = mcur[:, 4:8, :]
    c["mcur"] = mcur
    mprev = np.zeros((128, 32, 128), np.float32)
    for s in range(8):
        for g, win in enumerate(POOL_WINDOWS):
            for t in range(min(win - 1, 128)):
                for ps in range(t - win + 1, 0):
                    mprev[16 * s + 16 + ps, s * 4 + g, t] += 1.0 / win
    if j == 0:
        mprev[:, 0:4, :] = 0.0
    c["mprev"] = mprev
    mcs = np.zeros((64, 4, 64), np.float32)
    mps = np.zeros((32, 4, 64), np.float32)
    for bi in range(2):
        for g, win in enumerate(POOL_WINDOWS):
            for t in range(32):
                lo = max(0, t - win + 1)
                mcs[bi * 32 + lo:bi * 32 + t + 1, g, bi * 32 + t] += 1.0 / win
                mcs[bi * 32 + t, g, bi * 32 + t] -= 1.0
                for ps in range(t - win + 1, 0):
                    mps[bi * 16 + 16 + ps, g, bi * 32 + t] += 1.0 / win
    c["mcurS"] = mcs
    c["mprevS"] = mps
    km = np.zeros((16, SEQ), np.float32)
    qm = np.zeros((16, 1024), np.float32)
    for s in range(8):
        qm[2 * s, s * 128:s * 128 + 64] = -30000.0
        qm[2 * s + 1, s * 128 + 64:(s + 1) * 128] = -30000.0
        for d in range(4):
            kb = 4 * s + d
            if d > j:
                km[2 * s, kb * 128:(kb + 1) * 128] = 1.0
                km[2 * s + 1, kb * 128:(kb + 1) * 128] = 1.0
            elif d == j:
                km[2 * s, kb * 128 + 64:(kb + 1) * 128] = 1.0
    c["kmask"] = km
    c["qmask"] = qm
    return c


def kernel(x_prompt, mem_prompt, x_sample, cache_mla_latent, cache_mla_kpe, state_pool,
           cache_mem_k, cache_mem_v, g_mix, w_in, b_gate, w_pool, pool_scale, g_q_lat, w_qb,
           g_q_head, g_kv_lat, w_kb, w_vb, g_k_head, w_mla_o, g_mem, w_mem_kv, g_mem_q,
           g_mem_k, w_mem_o, w_out, g_ff, w_up, w_down):
    stage = int(os.environ.get("MK_STAGE", "99"))
    f = lambda a: np.ascontiguousarray(np.asarray(a, dtype=np.float32))
    x_prompt, mem_prompt, x_sample = f(x_prompt), f(mem_prompt), f(x_sample)
    cache_mla_latent, cache_mla_kpe, state_pool = f(cache_mla_latent), f(cache_mla_kpe), f(state_pool)
    cache_mem_k, cache_mem_v = f(cache_mem_k), f(cache_mem_v)
    if stage not in _PROG:
        _PROG[stage] = build_program(stage)
    nc = _PROG[stage]
    w_qb = f(w_qb)
    wq3 = w_qb.reshape(512, 16, 192)
    w_qbrot = np.ascontiguousarray(np.concatenate([wq3[:, :, 160:192], wq3[:, :, 128:160]], axis=-1).reshape(512, 1024))
    shared = dict(w_in=f(w_in), w_pool=f(w_pool), w_qb=w_qb, w_qbrot=w_qbrot, w_kb=f(w_kb), w_vb=f(w_vb),
                  w_mla_o=f(w_mla_o), w_mem_kv=f(w_mem_kv), w_mem_o=f(w_mem_o), w_out=f(w_out), w_up=f(w_up),
                  w_down=f(w_down), g_mix=f(g_mix), b_gate=f(b_gate), pool_scale=f(pool_scale), g_q_lat=f(g_q_lat),
                  g_q_head=f(g_q_head), g_kv_lat=f(g_kv_lat), g_k_head=f(g_k_head), g_mem=f(g_mem),
                  g_mem_q=f(g_mem_q), g_mem_k=f(g_mem_k), g_ff=f(g_ff))
    in_maps = []
    for c in range(8):
        b, j = c // 4, c % 4
        xs = x_prompt[b]
        xown = np.empty((TOWN, D), np.float32)
        xprev = np.zeros((128, D), np.float32)
        for s in range(8):
            i = 4 * s + j
            xown[s * 128:(s + 1) * 128] = xs[i * 128:(i + 1) * 128]
            if i > 0:
                xprev[16 * s:16 * s + 16] = xs[i * 128 - 16:i * 128]
        xown[1024:1056] = x_sample[2 * c]
        xown[1056:1088] = x_sample[2 * c + 1]
        sp = np.zeros((32, 1024), np.float32)
        sp[1:16] = state_pool[2 * c]
        sp[17:32] = state_pool[2 * c + 1]
        m = dict(xseq=xs, xown=xown, xprev=xprev,
                 latc=cache_mla_latent[2 * c:2 * c + 2], kpec=cache_mla_kpe[2 * c:2 * c + 2], spool=sp,
                 memp=mem_prompt[b], cmk=cache_mem_k[2 * c:2 * c + 2].reshape(2, 256, 1024),
                 cmv=cache_mem_v[2 * c:2 * c + 2].reshape(2, 256, 1024))
        m.update(shared)
        m.update(_consts(j))
        in_maps.append({k: np.ascontiguousarray(v) for k, v in m.items()})
    res = run_bass_kernel_spmd(nc, in_maps, core_ids=list(range(8)))
    R = res.results
    y_p = np.zeros((2, SEQ, D), np.float32)
    lat_p = np.zeros((2, SEQ, 512), np.float32)
    kpe_p = np.zeros((2, SEQ, 64), np.float32)
    y_s = np.zeros((16, 32, D), np.float32)
    lat_s = np.zeros((16, 32, 512), np.float32)
    kpe_s = np.zeros((16, 32, 64), np.float32)
    pool_p = np.zeros((2, 15, 1024), np.float32)
    pool_s = np.zeros((16, 15, 1024), np.float32)
    mem_k_p = np.zeros((2, 256, 4, 256), np.float32)
    mem_v_p = np.zeros((2, 256, 4, 256), np.float32)
    for c in range(8):
        b, j = c // 4, c % 4
        r = R[c]
        for s in range(8):
            i = 4 * s + j
            y_p[b, i * 128:(i + 1) * 128] = r["y_own"][s * 128:(s + 1) * 128]
            lat_p[b, i * 128:(i + 1) * 128] = r["lat_own"][s * 128:(s + 1) * 128]
            kpe_p[b, i * 128:(i + 1) * 128] = r["kpe_own"][s * 128:(s + 1) * 128]
        for bi in range(2):
            y_s[2 * c + bi] = r["y_own"][1024 + 32 * bi:1056 + 32 * bi]
            lat_s[2 * c + bi] = r["lat_own"][1024 + 32 * bi:1056 + 32 * bi]
            kpe_s[2 * c + bi] = r["kpe_own"][1024 + 32 * bi:1056 + 32 * bi]
            pool_s[2 * c + bi] = r["pools_out"][bi]
        if j == 3:
            pool_p[b] = r["poolp_out"]
        if j == 0:
            mem_k_p[b] = r["memk_out"].reshape(256, 4, 256)
            mem_v_p[b] = r["memv_out"].reshape(256, 4, 256)
    return (y_p, y_s, lat_p, kpe_p, pool_p, mem_k_p, mem_v_p, lat_s, kpe_s, pool_s)
```

```python
import os
import numpy as np
from contextlib import ExitStack
import concourse.bass as bass
import concourse.mybir as mybir
from concourse.bass_utils import run_bass_kernel_spmd

F32 = mybir.dt.float32
BF16 = mybir.dt.bfloat16
AF = mybir.ActivationFunctionType
ALU = mybir.AluOpType
AX = mybir.AxisListType

D = 2048
SEQ = 4096
NB = 32
NKB = 33
TOWN = 1088
EPS = 1e-6
D_IN = 9280
OFF_U, OFF_QL, OFF_KV, OFF_KPE, OFF_MQ, OFF_G = 0, 1024, 1536, 2048, 2112, 3136
EPOCH = 30000
ARENA_BYTES = 210944

BLOCKS = [(s * 128, 128) for s in range(8)] + [(1024, 64)]


class Buf:
    def __init__(self, t, name, psum=False, ap=None):
        self.t = t
        self.name = name
        self.psum = psum
        self.w = {}
        self.r = {}
        self.dsem = None
        self.dval = 0
        self._ap = ap

    def __getitem__(self, key):
        base = self._ap if self._ap is not None else self.t
        return base[key]


class Eng:
    def __init__(self, name, obj):
        self.name = name
        self.obj = obj
        self.sem = None
        self.cnt = 0
        self.waited = {}


class Kern:
    def __init__(self, nc, stack):
        self.nc = nc
        self.stack = stack
        self.e = {n: Eng(n, o) for n, o in (('pe', nc.tensor), ('act', nc.scalar), ('dve', nc.vector),
                                            ('pool', nc.gpsimd), ('sp', nc.sync))}
        self.nsem = 0
        for n in ('pe', 'act', 'dve', 'pool'):
            self.e[n].sem = self.new_sem(n)
        self.store_toks = []
        self.dma_bufs = []
        self.ninst = 0
        self.rr = 0
        self._rec = None

    def new_sem(self, name):
        self.nsem += 1
        return self.stack.enter_context(self.nc.semaphore(f"s{self.nsem}_{name}"))

    def _need(self, en, reads, writes):
        E = self.e[en]
        need = {}

        def add(d, own_ok):
            for s, v in d.items():
                if s is E.sem and en == 'pe':
                    continue
                if need.get(s, 0) < v:
                    need[s] = v

        for b in reads:
            add(b.w, True)
            if b.psum:
                add(b.r, False)
        for b in writes:
            add(b.w, False)
            add(b.r, False)
        return need

    def _do_waits(self, en, need):
        E = self.e[en]
        for s, v in need.items():
            if E.waited.get(s, 0) >= v:
                continue
            E.obj.wait_ge(s, v)
            E.waited[s] = v
            self.ninst += 1

    def record(self):
        self._rec = []

    def stop_record(self):
        r = self._rec
        self._rec = None
        return r

    def py(self, fn):
        if self._rec is not None:
            self._rec.append(('py', (fn,), {}))
        else:
            fn()

    def replay(self, lists, width=2, stagger=0.5):
        active = []
        nxt = 0
        while nxt < len(lists) or active:
            if nxt < len(lists) and len(active) < width and (
                    not active or active[-1][1] >= stagger * len(active[-1][0])):
                active.append([lists[nxt], 0])
                nxt += 1
            for a in list(active):
                lst, pos = a
                if pos < len(lst):
                    kind, args, kw = lst[pos]
                    if kind == 'py':
                        args[0]()
                    else:
                        (self.op if kind == 'op' else self.dma)(*args, **kw)
                    a[1] += 1
                if a[1] >= len(lst):
                    active.remove(a)

    def op(self, en, fn, reads=(), writes=(), signal=True):
        if self._rec is not None:
            self._rec.append(('op', (en, fn, list(reads), list(writes), signal), {}))
            return None
        E = self.e[en]
        self._do_waits(en, self._need(en, reads, writes))
        ins = fn(E.obj)
        self.ninst += 1
        if E.cnt >= EPOCH - 200:
            E.sem = self.new_sem(en)
            E.cnt = 0
        if signal:
            E.cnt += 1
            ins.then_inc(E.sem, 1)
            tok = (E.sem, E.cnt)
        else:
            tok = (E.sem, E.cnt + 1)
        for b in reads:
            if b.r.get(tok[0], 0) < tok[1]:
                b.r[tok[0]] = tok[1]
        for b in writes:
            b.w = {tok[0]: tok[1]}
            b.r = {}
        return ins

    def dma(self, out_ap, in_ap, reads=(), writes=(), q='sp', final=False, skip_wait=False, **kw):
        if self._rec is not None:
            kw2 = dict(kw)
            kw2.update(reads=list(reads), writes=list(writes), q=q, final=final, skip_wait=skip_wait)
            self._rec.append(('dma', (out_ap, in_ap), kw2))
            return None
        E = self.e[q]
        if not skip_wait:
            self._do_waits(q, self._need(q, reads, writes))
        owner = writes[0] if writes else reads[0]
        if owner.dsem is None or owner.dval >= EPOCH:
            owner.dsem = self.new_sem('d')
            owner.dval = 0
            self.dma_bufs.append(owner)
        owner.dval += 16
        E.obj.dma_start(out=out_ap, in_=in_ap, **kw).then_inc(owner.dsem, 16)
        self.ninst += 1
        tok = (owner.dsem, owner.dval)
        for b in reads:
            if b.r.get(tok[0], 0) < tok[1]:
                b.r[tok[0]] = tok[1]
        for b in writes:
            b.w = {tok[0]: tok[1]}
            b.r = {}
        if final:
            self.store_toks.append(tok)
        return tok

    def dma_k(self, buf, dst_fn, src2d, nk, nsplit, q='pool', reads=()):
        g = nk // nsplit
        assert g * nsplit == nk
        for i in range(nsplit):
            self.dma(dst_fn(i * g, (i + 1) * g), src2d[i * g * 128:(i + 1) * g * 128, :].rearrange("(k p) n -> p k n", p=128),
                     reads=list(reads), writes=[buf], q=q, skip_wait=(i > 0))

    def barrier(self):
        names = ('pe', 'act', 'dve', 'pool')
        for a in names + ('sp',):
            need = {}
            for b in names:
                if a == b:
                    continue
                B = self.e[b]
                if B.cnt > 0:
                    need[B.sem] = B.cnt
            for b in self.dma_bufs:
                if b.dval > 0 and need.get(b.dsem, 0) < b.dval:
                    need[b.dsem] = b.dval
            self._do_waits(a, need)

    def finish(self):
        need = {}
        for s, v in self.store_toks:
            if need.get(s, 0) < v:
                need[s] = v
        for b in ('pe', 'act', 'dve', 'pool'):
            B = self.e[b]
            if B.cnt > 0:
                need[B.sem] = B.cnt
        self._do_waits('sp', need)

    def ev(self):
        self.rr ^= 1
        return 'act' if self.rr else 'dve'


class Arena:
    def __init__(self, K, nbytes):
        self.K = K
        self.t = K.stack.enter_context(K.nc.sbuf_tensor("arena", [128, nbytes // 4], F32))
        self.off = 0
        self.limit = nbytes
        self.n = 0

    def at(self, off):
        self.off = off

    def alloc(self, name, shape, dt=F32):
        nfree = int(np.prod(shape[1:]))
        esz = 2 if dt == BF16 else 4
        nb = (nfree * esz + 31) // 32 * 32
        o = self.off
        assert o % 4 == 0 and o + nb <= self.limit, (name, o, nb, self.limit)
        self.off = o + nb
        ap = self.t[0:shape[0], o // 4:(o + nb) // 4]
        if dt != F32:
            ap = ap.bitcast(dt)
        ap = ap[:, 0:nfree]
        if len(shape) == 3:
            ap = ap.rearrange("p (a b) -> p a b", a=shape[1])
        elif len(shape) == 4:
            ap = ap.rearrange("p (a b c) -> p a b c", a=shape[1], b=shape[2])
        self.n += 1
        return Buf(self.t, f"{name}_{self.n}", ap=ap)


def build_program(stage=99):
    nc = bass.Bass("TRN2", target_bir_lowering=False)

    def din(name, shape):
        return nc.dram_tensor(name, list(shape), F32, kind="ExternalInput").ap()

    def dout(name, shape):
        return nc.dram_tensor(name, list(shape), F32, kind="ExternalOutput").ap()

    xseq = din("xseq", [SEQ, D])
    xown = din("xown", [TOWN, D])
    xprev = din("xprev", [128, D])
    latc = din("latc", [2, SEQ, 512])
    kpec = din("kpec", [2, SEQ, 64])
    spool = din("spool", [32, 1024])
    memp = din("memp", [256, D])
    cmk = din("cmk", [2, 256, 1024])
    cmv = din("cmv", [2, 256, 1024])
    w_in = din("w_in", [D, D_IN])
    w_pool = din("w_pool", [4, 256, 512])
    w_qb = din("w_qb", [512, 3072])
    w_qbrot = din("w_qbrot", [512, 1024])
    w_kb = din("w_kb", [512, 2048])
    w_vb = din("w_vb", [512, 2048])
    w_mla_o = din("w_mla_o", [2048, 2048])
    w_mem_kv = din("w_mem_kv", [2048, 2048])
    w_mem_o = din("w_mem_o", [1024, 2048])
    w_out = din("w_out", [2048, 2048])
    w_up = din("w_up", [2048, 8192])
    w_down = din("w_down", [8192, 2048])
    g_mix = din("g_mix", [D])
    b_gate = din("b_gate", [6144])
    pool_scale = din("pool_scale", [D])
    g_q_lat = din("g_q_lat", [512])
    g_q_head = din("g_q_head", [192])
    g_kv_lat = din("g_kv_lat", [512])
    g_k_head = din("g_k_head", [192])
    g_mem = din("g_mem", [D])
    g_mem_q = din("g_mem_q", [256])
    g_mem_k = din("g_mem_k", [256])
    g_ff = din("g_ff", [D])
    ident = din("ident", [128, 128])
    ktab_c = din("ktab_c", [128, NKB, 64])
    ktab_s = din("ktab_s", [128, NKB, 64])
    qtab_c = din("qtab_c", [64, TOWN])
    qtab_s = din("qtab_s", [64, TOWN])
    mcur = din("mcur", [128, 8, 128])
    mprev = din("mprev", [128, 32, 128])
    mcurS = din("mcurS", [64, 4, 64])
    mprevS = din("mprevS", [32, 4, 64])
    kmask = din("kmask", [16, SEQ])
    qmask = din("qmask", [16, 1024])

    y_own = dout("y_own", [TOWN, D])
    lat_own = dout("lat_own", [TOWN, 512])
    kpe_own = dout("kpe_own", [TOWN, 64])
    poolp_out = dout("poolp_out", [15, 1024])
    pools_out = dout("pools_out", [2, 15, 1024])
    memk_out = dout("memk_out", [256, 1024])
    memv_out = dout("memv_out", [256, 1024])

    with ExitStack() as st:
        K = Kern(nc, st)
        A = Arena(K, ARENA_BYTES)
        pst = st.enter_context(nc.psum_tensor("pst", [128, 8, 512], F32))
        PB = [Buf(pst, f"bank{i}", psum=True, ap=pst[:, i, :]) for i in range(8)]

        def bfv(bank, n=128):
            return PB[bank][0:n, :].bitcast(BF16)

        A.at(0)
        ident_b = A.alloc("ident", [128, 128], BF16)
        ones_b = A.alloc("ones", [128, 128], BF16)
        ones_f = A.alloc("onesf", [128, 128])
        eps6 = A.alloc("eps6", [128, 1])
        eps192 = A.alloc("eps192", [128, 1])
        gqk = A.alloc("gqk", [128, 1])
        gq_n = A.alloc("gq_n", [128, 1])
        gk_n = A.alloc("gk_n", [128, 1])
        gq_r = A.alloc("gq_r", [64, 1])
        gq_rp = A.alloc("gq_rp", [64, 1])
        gbc = A.alloc("gbc", [128, D])
        cnew = A.alloc("cnew", [64, 512], BF16)
        kpnew = A.alloc("kpnew", [64, 64])
        sspe_new = A.alloc("sspe_new", [64, 1])
        sq_last = A.alloc("sq_last", [128, 32], BF16)
        assert A.off <= 14336
        A.at(14336)
        memkT_p = A.alloc("memkT_p", [128, 8, 256], BF16)
        memv_p = A.alloc("memv_p", [128, 2, 1024], BF16)
        assert A.off == 22528
        OT_OFF = 22528
        R1 = 57344
        R2 = 92160
        R3 = 142304

        K.dma(ident_b[:], ident, writes=[ident_b], q='pool')
        K.dma(gbc[:], g_mix.partition_broadcast(128), writes=[gbc])
        K.dma(gq_n[:], g_q_head[0:128].rearrange("(p o) -> p o", o=1), writes=[gq_n])
        K.dma(gk_n[:], g_k_head[0:128].rearrange("(p o) -> p o", o=1), writes=[gk_n])
        K.dma(gq_r[:], g_q_head[128:192].rearrange("(p o) -> p o", o=1), writes=[gq_r])
        K.dma(gq_rp[0:32, :], g_q_head[160:192].rearrange("(p o) -> p o", o=1), writes=[gq_rp])
        K.dma(gq_rp[32:64, :], g_q_head[128:160].rearrange("(p o) -> p o", o=1), writes=[gq_rp])
        K.op('dve', lambda e: e.memset(ones_f[:], 1.0), writes=[ones_f])
        K.op('dve', lambda e: e.tensor_copy(out=ones_b[:], in_=ones_f[:]), reads=[ones_f], writes=[ones_b])
        K.op('dve', lambda e: e.memset(eps6[:], EPS), writes=[eps6])
        K.op('dve', lambda e: e.memset(eps192[:], EPS * 192.0), writes=[eps192])
        K.op('dve', lambda e: e.tensor_tensor(out=gqk[:], in0=gq_n[:], in1=gk_n[:], op=ALU.mult),
             reads=[gq_n, gk_n], writes=[gqk])

        def rms_stats(src_ap, n, width, junk_ap, ss, rstd, reads, eps_b=eps6, extra_w=()):
            K.op('act', lambda e: e.activation(out=junk_ap, in_=src_ap, func=AF.Square, accum_out=ss[0:n, :]),
                 reads=reads, writes=[ss] + list(extra_w))
            K.op('act', lambda e: e.activation(out=rstd[0:n, :], in_=ss[0:n, :], func=AF.Sqrt, bias=eps_b[0:n, :],
                                               scale=1.0 / width), reads=[ss, eps_b], writes=[rstd])
            K.op('dve', lambda e: e.reciprocal(out=rstd[0:n, :], in_=rstd[0:n, :]), reads=[rstd], writes=[rstd])

        def transposes(src, src_bufs, n, nch, dst_fn, dst_bufs, banks, cw=128, pb=0, dst_grp=None):
            k = 0
            bi = 0
            while k < nch:
                g = min(4, nch - k)
                bk = banks[bi % len(banks)]
                bi += 1
                pv = bfv(bk, cw)
                for j in range(g):
                    K.op('pe', lambda e, kk=k + j, j=j, pv=pv: e.transpose(
                        out=pv[:, j * 128:j * 128 + n], in_=src[0:n, kk * cw:(kk + 1) * cw], identity=ident_b[pb:pb + n, pb:pb + n]),
                        reads=list(src_bufs) + [ident_b], writes=[PB[bk]], signal=(j == g - 1))
                if dst_grp is not None:
                    K.op(K.ev(), (lambda k0, g, pv: (lambda e: (
                        e.activation(out=dst_grp(k0, g), in_=pv[:, 0:g * 128].rearrange("p (a b) -> p a b", a=g)[:, :, 0:n], func=AF.Copy)
                        if e is nc.scalar else
                        e.tensor_copy(out=dst_grp(k0, g), in_=pv[:, 0:g * 128].rearrange("p (a b) -> p a b", a=g)[:, :, 0:n]))))(k, g, pv),
                        reads=[PB[bk]], writes=dst_bufs)
                else:
                    for j in range(g):
                        K.op(K.ev(), (lambda kk, j, pv: (lambda e: (
                            e.activation(out=dst_fn(kk), in_=pv[:, j * 128:j * 128 + n], func=AF.Copy)
                            if e is nc.scalar else e.tensor_copy(out=dst_fn(kk), in_=pv[:, j * 128:j * 128 + n]))))(k + j, j, pv),
                            reads=[PB[bk]], writes=dst_bufs)
                k += g

        def norm_block(x_dram_rows, n, xb, h, hT, ss, rstd, banks, col0=0):
            K.dma(xb[0:n, :], x_dram_rows, writes=[xb])
            rms_stats(xb[0:n, :], n, D, h[0:n, :], ss, rstd, reads=[xb], extra_w=[h])
            K.op('dve', lambda e: e.scalar_tensor_tensor(out=h[0:n, :], in0=xb[0:n, :], scalar=rstd[0:n, :], in1=gbc[0:n, :],
                                                         op0=ALU.mult, op1=ALU.mult), reads=[xb, rstd, gbc], writes=[h])
            transposes(h, [h], n, 16, lambda k: hT[:, k, col0:col0 + n], [hT], banks,
                       dst_grp=lambda k0, g: hT[:, k0:k0 + g, col0:col0 + n])

        def mm_tm(out_ap, bank, lhs_fn, rhs_fn, nk, reads):
            for k in range(nk):
                K.op('pe', lambda e, k=k: e.matmul(out_ap, lhsT=lhs_fn(k), rhs=rhs_fn(k), start=(k == 0), stop=(k == nk - 1)),
                     reads=reads, writes=[PB[bank]], signal=(k == nk - 1))

        A.at(OT_OFF)
        gmem_bc = A.alloc("gmem_bc", [128, D])
        gmk_bc = A.alloc("gmk_bc", [128, 256])
        xb0 = A.alloc("xb0", [128, D])
        h0 = A.alloc("h0", [128, D], BF16)
        memhT = A.alloc("memhT", [128, 16, 256], BF16)
        wt = [A.alloc(f"wmkv{i}", [128, 16, 512], BF16) for i in range(2)]
        mkf = A.alloc("mkf", [128, 2, 1024])
        mvf = A.alloc("mvf", [128, 2, 1024])
        mkb = A.alloc("mkb", [128, 2, 1024], BF16)
        sq4 = A.alloc("sq4", [128, 1024])
        ss4 = A.alloc("ss4", [128, 4])
        r4 = A.alloc("r4", [128, 4])
        ssm = A.alloc("ssm", [128, 1])
        rsm = A.alloc("rsm", [128, 1])
        K.dma(gmem_bc[:], g_mem.partition_broadcast(128), writes=[gmem_bc])
        K.dma(gmk_bc[:], g_mem_k.partition_broadcast(128), writes=[gmk_bc])
        for mb in range(2):
            K.dma(xb0[:], memp[mb * 128:(mb + 1) * 128, :], writes=[xb0])
            rms_stats(xb0[:], 128, D, h0[:], ssm, rsm, reads=[xb0], extra_w=[h0])
            K.op('dve', lambda e: e.scalar_tensor_tensor(out=h0[:], in0=xb0[:], scalar=rsm[:], in1=gmem_bc[:],
                                                         op0=ALU.mult, op1=ALU.mult), reads=[xb0, rsm, gmem_bc], writes=[h0])
            transposes(h0, [h0], 128, 16, lambda k, mb=mb: memhT[:, k, mb * 128:(mb + 1) * 128], [memhT], [0, 1])
        for ct in range(4):
            wtile = wt[ct % 2]
            K.dma_k(wtile, lambda a, b, wtile=wtile: wtile[:, a:b, :], w_mem_kv[:, ct * 512:(ct + 1) * 512], 16, 4)
            for mb in range(2):
                bank = 2 + (ct * 2 + mb) % 4
                mm_tm(PB[bank][:, :], bank, lambda k, mb=mb: memhT[:, k, mb * 128:(mb + 1) * 128],
                      lambda k, wtile=wtile: wtile[:, k, :], 16, [memhT, wtile])
                dstf = mkf if ct < 2 else mvf
                cc = (ct % 2) * 512
                K.op(K.ev(), (lambda bank, dstf, mb, cc: (lambda e: (
                    e.activation(out=dstf[:, mb, cc:cc + 512], in_=PB[bank][:, :], func=AF.Copy) if e is nc.scalar
                    else e.tensor_copy(out=dstf[:, mb, cc:cc + 512], in_=PB[bank][:, :]))))(bank, dstf, mb, cc),
                    reads=[PB[bank]], writes=[dstf])
        for mb in range(2):
            K.op('dve', lambda e, mb=mb: e.tensor_tensor(out=sq4[:], in0=mkf[:, mb, :], in1=mkf[:, mb, :], op=ALU.mult),
                 reads=[mkf], writes=[sq4])
            K.op('dve', lambda e: e.tensor_reduce(out=ss4[:], in_=sq4[:].rearrange("p (h d) -> p h d", h=4), axis=AX.X, op=ALU.add),
                 reads=[sq4], writes=[ss4])
            K.op('act', lambda e: e.activation(out=r4[:], in_=ss4[:], func=AF.Sqrt, bias=eps6[:], scale=1.0 / 256),
                 reads=[ss4, eps6], writes=[r4])
            K.op('dve', lambda e: e.reciprocal(out=r4[:], in_=r4[:]), reads=[r4], writes=[r4])
            for hh in range(4):
                K.op('dve', lambda e, mb=mb, hh=hh: e.scalar_tensor_tensor(
                    out=mkf[:, mb, hh * 256:(hh + 1) * 256], in0=mkf[:, mb, hh * 256:(hh + 1) * 256], scalar=r4[:, hh:hh + 1],
                    in1=gmk_bc[:], op0=ALU.mult, op1=ALU.mult), reads=[mkf, r4, gmk_bc], writes=[mkf])
            K.op('act', lambda e, mb=mb: e.activation(out=mkb[:, mb, :], in_=mkf[:, mb, :], func=AF.Copy), reads=[mkf], writes=[mkb])
            K.op('dve', lambda e, mb=mb: e.tensor_copy(out=memv_p[:, mb, :], in_=mvf[:, mb, :]), reads=[mvf], writes=[memv_p])
            K.dma(memk_out[mb * 128:(mb + 1) * 128, :], mkf[:, mb, :], reads=[mkf], final=True)
            K.dma(memv_out[mb * 128:(mb + 1) * 128, :], mvf[:, mb, :], reads=[mvf], final=True)

        def build_memkT(src_b, dst):
            for mb in range(2):
                transposes(src_b[:, mb, :], [src_b], 128, 8,
                           lambda c, mb=mb: dst[:, c, mb * 128:(mb + 1) * 128], [dst], [0, 1])

        build_memkT(mkb, memkT_p)
        K.barrier()
        if stage <= 0:
            K.finish()
            return nc

        A.at(OT_OFF)
        oT_all = A.alloc("oT_all", [128, 16, TOWN], BF16)
        assert A.off == R1
        Wa = A.alloc("Wa", [128, 16, 1088], BF16)
        assert A.off == R2
        ckvT = A.alloc("ckvT", [128, 4, 4128], BF16)
        kropeT = A.alloc("kropeT", [128, 4128], BF16)
        sspe = A.alloc("sspe", [128, NKB])
        qlatT = A.alloc("qlatT", [128, 4, TOWN], BF16)
        assert A.off <= R3, A.off
        A.at(R3)
        ktc = A.alloc("ktc", [128, NKB, 64])
        kts = A.alloc("kts", [128, NKB, 64])
        gkr_bc = A.alloc("gkr_bc", [128, 64])
        CH = [dict(), dict(), dict()]
        for c in range(3):
            CH[c]['kpg'] = A.alloc("kpg", [128, 64])
            CH[c]['kt1'] = A.alloc("kt1", [128, 64])
            CH[c]['kt2'] = A.alloc("kt2", [128, 64])
            CH[c]['krb'] = A.alloc("krb", [128, 64], BF16)
            CH[c]['bT'], CH[c]['bL'], CH[c]['bS'], CH[c]['bC'] = [(0, 1, 2, 2), (3, 4, 5, 5), (6, 7, 6, 6)][c]
        PBOFF = A.off
        for c in range(3):
            if c == 2:
                sv_off = A.off
                A.at(OT_OFF)
            CH[c]['xb'] = A.alloc("xb", [128, D])
            CH[c]['hb'] = A.alloc("hb", [128, D], BF16)
            CH[c]['hT'] = A.alloc("hT", [128, 16, 128], BF16)
            CH[c]['cf'] = A.alloc("cf", [128, 512])
            CH[c]['cb'] = A.alloc("cb", [128, 512], BF16)
            CH[c]['qlb'] = A.alloc("qlb", [128, 512], BF16)
            CH[c]['kpf'] = A.alloc("kpf", [128, 64])
            CH[c]['ss'] = [A.alloc("st_ss", [128, 1]) for i in range(3)]
            CH[c]['r'] = [A.alloc("st_r", [128, 1]) for i in range(3)]
            if c == 2:
                assert A.off <= R1
                A.at(sv_off)
        gkv_bc = A.alloc("gkv_bc", [128, 512])
        gql_bc = A.alloc("gql_bc", [128, 512])
        hscr_t = nc.dram_tensor("hT_scr", [128, 16, TOWN + 128], BF16, kind="Internal").ap()
        hscr = Buf(None, "hscr")
        K.op('dve', lambda e: e.memset(sspe[:], 1.0), writes=[sspe])
        K.op('dve', lambda e: e.memset(kropeT[64:128, :], 0.0), writes=[kropeT])
        K.dma(kropeT[64:80, 0:SEQ], kmask, writes=[kropeT], q='pool')
        K.dma_k(Wa, lambda a, b: Wa[:, a:b, 512:1088], w_in[:, OFF_KV:OFF_KV + 576], 16, 4)
        K.dma_k(Wa, lambda a, b: Wa[:, a:b, 0:512], w_in[:, OFF_QL:OFF_QL + 512], 16, 4)
        K.dma(ktc[:], ktab_c, writes=[ktc])
        K.dma(kts[:], ktab_s, writes=[kts])
        K.dma(gkv_bc[:], g_kv_lat.partition_broadcast(128), writes=[gkv_bc])
        K.dma(gql_bc[:], g_q_lat.partition_broadcast(128), writes=[gql_bc])
        K.dma(gkr_bc[:], g_k_head[128:192].partition_broadcast(128), writes=[gkr_bc])

        def key_side(ch, c_src_ap, c_reads, kpe_src_ap, kpe_reads, n, blk, col0, sspe_dst_ap, sspe_dst_buf, pb=0):
            P = slice(pb, pb + n)
            kpg, kt1, kt2, krb, bS = ch['kpg'], ch['kt1'], ch['kt2'], ch['krb'], ch['bS']
            transposes(c_src_ap, c_reads, n, 4, lambda k: ckvT[:, k, col0:col0 + n], [ckvT], [ch['bC']], pb=pb,
                       dst_grp=lambda k0, g: ckvT[:, k0:k0 + g, col0:col0 + n])
            K.op('act', lambda e: e.activation(out=kt1[P, :], in_=kpe_src_ap, func=AF.Square, accum_out=sspe_dst_ap),
                 reads=kpe_reads, writes=[kt1, sspe_dst_buf])
            K.op('dve', lambda e: e.tensor_tensor(out=kpg[P, :], in0=kpe_src_ap, in1=gkr_bc[P, :], op=ALU.mult),
                 reads=kpe_reads + [gkr_bc], writes=[kpg])
            K.op('dve', lambda e: e.tensor_tensor(out=kt1[P, :], in0=kpg[P, :], in1=ktc[P, blk, :], op=ALU.mult),
                 reads=[kpg, ktc], writes=[kt1])
            K.op('dve', lambda e: e.tensor_tensor(out=kt2[P, 0:32], in0=kpg[P, 32:64], in1=kts[P, blk, 0:32], op=ALU.mult),
                 reads=[kpg, kts], writes=[kt2])
            K.op('dve', lambda e: e.tensor_tensor(out=kt2[P, 32:64], in0=kpg[P, 0:32], in1=kts[P, blk, 32:64], op=ALU.mult),
                 reads=[kpg, kts], writes=[kt2])
            K.op('dve', lambda e: e.tensor_tensor(out=krb[P, :], in0=kt1[P, :], in1=kt2[P, :], op=ALU.add),
                 reads=[kt1, kt2], writes=[krb])
            K.op('pe', lambda e: e.transpose(out=bfv(bS, 64)[:, 512:512 + n], in_=krb[P, :], identity=ident_b[P, P]),
                 reads=[krb, ident_b], writes=[PB[bS]])
            K.op('act', lambda e: e.activation(out=kropeT[0:64, col0:col0 + n], in_=bfv(bS, 64)[:, 512:512 + n], func=AF.Copy),
                 reads=[PB[bS]], writes=[kropeT])

        def latents(ch, n):
            hT, bL, bS, cf, cb, kpf = ch['hT'], ch['bL'], ch['bS'], ch['cf'], ch['cb'], ch['kpf']
            mm_tm(PB[bL][0:n, :], bL, lambda k: hT[:, k, 0:n], lambda k: Wa[:, k, 512:1024], 16, [hT, Wa])
            mm_tm(PB[bS][0:n, 0:64], bS, lambda k: hT[:, k, 0:n], lambda k: Wa[:, k, 1024:1088], 16, [hT, Wa])
            rms_stats(PB[bL][0:n, :], n, 512, cb[0:n, :], ch['ss'][1], ch['r'][1], reads=[PB[bL]], extra_w=[cb])
            K.op('dve', lambda e: e.scalar_tensor_tensor(out=cf[0:n, :], in0=PB[bL][0:n, :], scalar=ch['r'][1][0:n, :], in1=gkv_bc[0:n, :],
                                                         op0=ALU.mult, op1=ALU.mult), reads=[PB[bL], ch['r'][1], gkv_bc], writes=[cf])
            K.op('act', lambda e: e.activation(out=cb[0:n, :], in_=cf[0:n, :], func=AF.Copy), reads=[cf], writes=[cb])
            K.op('dve', lambda e: e.tensor_copy(out=kpf[0:n, :], in_=PB[bS][0:n, 0:64]), reads=[PB[bS]], writes=[kpf])

        blks = [(xseq[blk * 128:(blk + 1) * 128, :], 128) for blk in range(NB)]
        blks += [(xown[r0:r0 + n, :], n) for (r0, n) in BLOCKS]
        blks += [(xprev[:, :], 128)]
        NBLK = len(blks)
        normed = set()

        def load_x(i):
            rows, n = blks[i]
            xb = CH[i % 3]['xb']
            K.dma(xb[0:n, :], rows, writes=[xb])

        def norm_a(i):
            rows, n = blks[i]
            ch = CH[i % 3]
            xb, h = ch['xb'], ch['hb']
            rms_stats(xb[0:n, :], n, D, h[0:n, :], ch['ss'][0], ch['r'][0], reads=[xb], extra_w=[h])
            K.op('dve', lambda e: e.scalar_tensor_tensor(out=h[0:n, :], in0=xb[0:n, :], scalar=ch['r'][0][0:n, :], in1=gbc[0:n, :],
                                                         op0=ALU.mult, op1=ALU.mult), reads=[xb, ch['r'][0], gbc], writes=[h])
            K.py(lambda: normed.add(i))
            if i + 3 < NBLK:
                load_x(i + 3)

        def head(i):
            rows, n = blks[i]
            ch = CH[i % 3]
            hT = ch['hT']

            def chk():
                assert i in normed, i
            K.py(chk)
            transposes(ch['hb'], [ch['hb']], n, 16, lambda k: hT[:, k, 0:n], [hT], [ch['bT']],
                       dst_grp=lambda k0, g: hT[:, k0:k0 + g, 0:n])
            if i + 3 < NBLK:
                norm_a(i + 3)

        for i in range(3):
            load_x(i)
        for i in range(3):
            norm_a(i)
        recs = []
        for blk in range(NB):
            ch = CH[len(recs) % 3]
            K.record()
            head(len(recs))
            latents(ch, 128)
            key_side(ch, ch['cb'][0:128, :], [ch['cb']], ch['kpf'][0:128, :], [ch['kpf']], 128, blk, blk * 128, sspe[:, blk:blk + 1], sspe)
            recs.append(K.stop_record())

        for bi, (r0, n) in enumerate(BLOCKS):
            ch = CH[len(recs) % 3]
            hT, cf, cb, kpf, qlb, bT, bC = ch['hT'], ch['cf'], ch['cb'], ch['kpf'], ch['qlb'], ch['bT'], ch['bC']
            K.record()
            head(len(recs))
            K.dma(hscr_t[:, :, r0:r0 + n], hT[:, :, 0:n], reads=[hT], writes=[hscr])
            latents(ch, n)
            mm_tm(PB[bT][0:n, :], bT, lambda k, hT=hT, n=n: hT[:, k, 0:n], lambda k: Wa[:, k, 0:512], 16, [hT, Wa])
            K.dma(lat_own[r0:r0 + n, :], cf[0:n, :], reads=[cf], final=True)
            K.dma(kpe_own[r0:r0 + n, :], kpf[0:n, :], reads=[kpf], final=True)
            if bi == 8:
                K.op('dve', lambda e, cb=cb: e.tensor_copy(out=cnew[:, :], in_=cb[0:64, :]), reads=[cb], writes=[cnew])
                K.op('dve', lambda e, kpf=kpf: e.tensor_copy(out=kpnew[:, :], in_=kpf[0:64, :]), reads=[kpf], writes=[kpnew])
            rms_stats(PB[bT][0:n, :], n, 512, qlb[0:n, :], ch['ss'][2], ch['r'][2], reads=[PB[bT]], extra_w=[qlb])
            K.op('dve', lambda e, ch=ch, qlb=qlb, bT=bT, n=n: e.scalar_tensor_tensor(
                out=qlb[0:n, :], in0=PB[bT][0:n, :], scalar=ch['r'][2][0:n, :], in1=gql_bc[0:n, :],
                op0=ALU.mult, op1=ALU.mult), reads=[PB[bT], ch['r'][2], gql_bc], writes=[qlb])
            transposes(qlb, [qlb], n, 4, lambda k, r0=r0, n=n: qlatT[:, k, r0:r0 + n], [qlatT], [bC],
                       dst_grp=lambda k0, g, r0=r0, n=n: qlatT[:, k0:k0 + g, r0:r0 + n])
            recs.append(K.stop_record())
        ch = CH[len(recs) % 3]
        K.record()
        head(len(recs))
        K.dma(hscr_t[:, :, TOWN:TOWN + 128], ch['hT'][:, :, :], reads=[ch['hT']], writes=[hscr])
        recs.append(K.stop_record())
        assert len(recs) == NBLK
        K.replay(recs, width=3, stagger=0.33)
        K.barrier()
        if stage <= 1:
            K.finish()
            return nc

        A.at(R1)
        V4 = A.alloc("V4", [128, NKB, 512], BF16)
        A.at(PBOFF)
        knT = A.alloc("knT", [128, 4128], BF16)
        qnT = A.alloc("qnT", [128, TOWN], BF16)
        qrT = A.alloc("qrT", [128, TOWN], BF16)
        Tc = A.alloc("Tc", [64, TOWN])
        Ts = A.alloc("Ts", [64, TOWN])
        pTs = [A.alloc(f"pT{i}", [128, 512], BF16) for i in range(3)]
        sqall = Buf(A.t, "sqall", ap=gbc._ap.bitcast(BF16))
        sqn = A.alloc("sqn", [128, 512], BF16)
        sqr = A.alloc("sqr", [64, 512], BF16)
        off_scr = A.off
        rrep = A.alloc("rrep", [128, 512])
        rrec = A.alloc("rrec", [128, 512])
        t1 = A.alloc("t1", [64, 512])
        t2 = A.alloc("t2", [64, 512])
        assert A.off == off_scr + 8192
        lat4 = [Buf(A.t, f"lat4_{i}", ap=A.t[0:128, (off_scr + i * 4096) // 4:(off_scr + (i + 1) * 4096) // 4].bitcast(BF16)
                    .rearrange("p (a b) -> p a b", a=4)) for i in range(2)]
        kpg4, kt1_4, kt2_4 = [Buf(A.t, f"ks4_{i}", ap=pTs[i]._ap.bitcast(F32).rearrange("p (a b) -> p a b", a=4)) for i in range(3)]
        rks = A.alloc("rks", [128, NKB])
        rkt = A.alloc("rkt", [128, NKB])
        KSPL = 24
        rksP = [Buf(A.t, f"rks{i}", ap=rks._ap) for i in range(2)]
        rktP = [Buf(A.t, f"rkt{i}", ap=rkt._ap) for i in range(2)]
        sqP = [Buf(A.t, f"sqP{i}", ap=sqall._ap) for i in range(2)]
        recip = A.alloc("recip", [128, 512])
        recips = [recip, rrep]
        wq_h = A.alloc("wq_h", [128, 4, 192], BF16)
        wrot_h = A.alloc("wrot_h", [128, 4, 64], BF16)
        wk_h = A.alloc("wk_h", [128, 4, 128], BF16)
        wv_g = A.alloc("wv_g", [128, 4, 512], BF16)
        kp4 = [A.alloc(f"kp4_{i}", [128, 4, 64]) for i in range(2)]
        krb4 = [A.alloc(f"krb4_{i}", [128, 4, 64], BF16) for i in range(2)]
        K.op('dve', lambda e: e.memset(qrT[64:128, :], 0.0), writes=[qrT])
        K.dma(qrT[64:80, 0:1024], qmask, writes=[qrT], q='pool')
        K.dma(Tc[:], qtab_c, writes=[Tc])
        K.dma(Ts[:], qtab_s, writes=[Ts])
        K.op('dve', lambda e: e.tensor_scalar_mul(out=Tc[:], in0=Tc[:], scalar1=gq_r[:, 0:1]), reads=[Tc, gq_r], writes=[Tc])
        K.op('dve', lambda e: e.tensor_scalar_mul(out=Ts[:], in0=Ts[:], scalar1=gq_rp[:, 0:1]), reads=[Ts, gq_rp], writes=[Ts])

        pT_i = [0]
        sc_i = [0]

        def attention_head(h, prob):
            hh = h % 4
            if prob == 0:
                nkb, nkeys = NB, SEQ
                qchunks = [(0, 512), (512, 512)]
            else:
                nkb, nkeys = NKB, 4128
                qchunks = [(1024 + 32 * (prob - 1), 32)]
            def v_build():
                for blk in range(nkb):
                    n = min(128, nkeys - blk * 128)
                    bank = 5 + blk % 2
                    mm_tm(PB[bank][0:n, :], bank, lambda k, blk=blk, n=n: ckvT[:, k, blk * 128:blk * 128 + n],
                          lambda k: wv_g[:, k, :], 4, [ckvT, wv_g])
                    K.op(K.ev(), (lambda bank, blk, n: (lambda e: (
                        e.activation(out=V4[0:n, blk, :], in_=PB[bank][0:n, :], func=AF.Copy) if e is nc.scalar
                        else e.tensor_copy(out=V4[0:n, blk, :], in_=PB[bank][0:n, :]))))(bank, blk, n), reads=[PB[bank]], writes=[V4])
                g2 = (h // 4 + 1) % 4
                K.dma(wv_g[:], w_vb[:, g2 * 512:(g2 + 1) * 512].rearrange("(k p) n -> p k n", p=128), writes=[wv_g], q='pool')

            def q_mm(c0, w):
                mm_tm(PB[0][:, 0:w], 0, lambda k: wq_h[:, k, 0:128], lambda k: qlatT[:, k, c0:c0 + w], 4, [wq_h, qlatT])
                mm_tm(PB[1][0:64, 0:w], 1, lambda k: wq_h[:, k, 128:192], lambda k: qlatT[:, k, c0:c0 + w], 4, [wq_h, qlatT])
                mm_tm(PB[2][0:64, 0:w], 2, lambda k: wrot_h[:, k, :], lambda k: qlatT[:, k, c0:c0 + w], 4, [wrot_h, qlatT])
                K.op('act', lambda e: e.activation(out=sqn[:, 0:w], in_=PB[0][:, 0:w], func=AF.Square), reads=[PB[0]], writes=[sqn])
                K.op('act', lambda e: e.activation(out=sqr[:, 0:w], in_=PB[1][0:64, 0:w], func=AF.Square), reads=[PB[1]], writes=[sqr])

            def q_fin(c0, w):
                K.op('pe', lambda e: e.matmul(PB[3][:, 0:w], lhsT=ones_b[:, :], rhs=sqn[:, 0:w], start=True, stop=False),
                     reads=[ones_b, sqn], writes=[PB[3]], signal=False)
                K.op('pe', lambda e: e.matmul(PB[3][:, 0:w], lhsT=ones_b[0:64, :], rhs=sqr[:, 0:w], start=False, stop=True),
                     reads=[ones_b, sqr], writes=[PB[3]])
                K.op('act', lambda e: e.activation(out=rrep[:, 0:w], in_=PB[3][:, 0:w], func=AF.Ln, bias=eps6[:], scale=1.0 / 192),
                     reads=[PB[3], eps6], writes=[rrep])
                K.op('act', lambda e: e.activation(out=rrec[:, 0:w], in_=rrep[:, 0:w], func=AF.Exp, scale=-0.5), reads=[rrep], writes=[rrec])
                K.op('dve', lambda e: e.scalar_tensor_tensor(out=qnT[:, c0:c0 + w], in0=PB[0][:, 0:w], scalar=gqk[:, 0:1],
                                                             in1=rrec[:, 0:w], op0=ALU.mult, op1=ALU.mult),
                     reads=[PB[0], gqk, rrec], writes=[qnT])
                K.op('dve', lambda e: e.tensor_tensor(out=t1[:, 0:w], in0=PB[1][0:64, 0:w], in1=Tc[:, c0:c0 + w], op=ALU.mult),
                     reads=[PB[1], Tc], writes=[t1])
                K.op('dve', lambda e: e.tensor_tensor(out=t2[:, 0:w], in0=PB[2][0:64, 0:w], in1=Ts[:, c0:c0 + w], op=ALU.mult),
                     reads=[PB[2], Ts], writes=[t2])
                K.op('dve', lambda e: e.tensor_tensor(out=t1[:, 0:w], in0=t1[:, 0:w], in1=t2[:, 0:w], op=ALU.add),
                     reads=[t1, t2], writes=[t1])
                K.op('dve', lambda e: e.tensor_tensor(out=qrT[0:64, c0:c0 + w], in0=t1[:, 0:w], in1=rrec[0:64, 0:w], op=ALU.mult),
                     reads=[t1, rrec], writes=[qrT])

            nkc = (nkeys + 511) // 512

            KBANKS = [4, 5, 6]

            def k_mm(kc):
                c0 = kc * 512
                w = min(512, nkeys - c0)
                bank = KBANKS[kc % 3]
                mm_tm(PB[bank][:, 0:w], bank, lambda k: wk_h[:, k, :], lambda k: ckvT[:, k, c0:c0 + w], 4, [wk_h, ckvT])

            def k_fin(kc):
                c0 = kc * 512
                w = min(512, nkeys - c0)
                bank = KBANKS[kc % 3]
                sqb, sqd = (sqP[0 if kc < KSPL // 4 else 1], sqall[:, c0:c0 + w]) if kc < 8 else (sq_last, sq_last[:, 0:w])
                K.op('act', lambda e: e.activation(out=sqd, in_=PB[bank][:, 0:w], func=AF.Square), reads=[PB[bank]], writes=[sqb])
                K.op('dve', lambda e: e.tensor_copy(out=knT[:, c0:c0 + w], in_=PB[bank][:, 0:w]), reads=[PB[bank]], writes=[knT])

            def k_ss(b_lo, b_hi):
                for blk in range(b_lo, b_hi):
                    nb_ = min(128, nkeys - blk * 128)
                    sqb, src = (sqP[0 if blk < KSPL else 1], sqall[:, blk * 128:blk * 128 + nb_]) if blk < 32 else (sq_last, sq_last[:, 0:nb_])
                    K.op('pe', lambda e, nb_=nb_, blk=blk, src=src: e.matmul(
                        PB[7][0:nb_, blk:blk + 1], lhsT=src, rhs=ones_b[:, 0:1], start=True, stop=True),
                        reads=[sqb, ones_b], writes=[PB[7]], signal=(blk == b_hi - 1))

            def rk_chain(pr, c_lo, c_hi, part):
                rkt_, rks_ = rktP[part], rksP[part]
                K.op('dve', lambda e: e.tensor_tensor(
                    out=rkt[0:pr, c_lo:c_hi], in0=PB[7][0:pr, c_lo:c_hi], in1=sspe[0:pr, c_lo:c_hi], op=ALU.add),
                    reads=[PB[7], sspe], writes=[rkt_])
                K.op('act', lambda e: e.activation(
                    out=rkt[0:pr, c_lo:c_hi], in_=rkt[0:pr, c_lo:c_hi], func=AF.Ln, bias=eps192[0:pr, :], scale=1.0),
                    reads=[rkt_, eps192], writes=[rkt_])
                K.op('act', lambda e: e.activation(
                    out=rks[0:pr, c_lo:c_hi], in_=rkt[0:pr, c_lo:c_hi], func=AF.Exp, scale=-0.5), reads=[rkt_], writes=[rks_])

            def k_seq(a, b):
                if a >= b:
                    return
                k_mm(a)
                for kc in range(a, b):
                    if kc + 1 < b:
                        k_mm(kc + 1)
                    k_fin(kc)

            k_seq(0, 3)
            q_mm(*qchunks[0])
            k_seq(3, 6)
            q_fin(*qchunks[0])
            k_seq(6, nkc)
            k_ss(0, KSPL)
            k_ss(KSPL, nkb)
            rk_chain(128, 0, KSPL, 0)
            rk_chain(128, KSPL, NB, 1)
            if nkb == NKB:
                rk_chain(32, NB, NKB, 1)
            if hh == 0:
                v_build()
            for qc in qchunks[1:]:
                q_mm(*qc)
                q_fin(*qc)
            h2 = (h + 1) % 16
            K.dma(wk_h[:], w_kb[:, h2 * 128:(h2 + 1) * 128].rearrange("(k p) n -> p k n", p=128), writes=[wk_h], q='pool')
            K.dma(wq_h[:], w_qb[:, h2 * 192:(h2 + 1) * 192].rearrange("(k p) n -> p k n", p=128), writes=[wq_h], q='pool')
            K.dma(wrot_h[:], w_qbrot[:, h2 * 64:(h2 + 1) * 64].rearrange("(k p) n -> p k n", p=128), writes=[wrot_h], q='pool')
            DEPTH = 2
            scbanks = [4, 7, 6]
            if prob == 0:
                last_kb = [15, 31]
                pairs = []
                for kb in range(NB):
                    s0 = kb // 4
                    for ci in range(2):
                        lo = max(s0 * 128, ci * 512)
                        hi = (ci + 1) * 512
                        if lo >= hi:
                            continue
                        pairs.append((kb, ci, lo, hi, lo - ci * 512, (ci * 512 <= s0 * 128 < hi), s0))

                def a_scores(i):
                    kb, ci, lo, hi, lr, diag, s0 = pairs[i]
                    scb = scbanks[i % 3]
                    K.op('pe', lambda e: e.matmul(PB[scb][:, lr:512], lhsT=knT[:, kb * 128:(kb + 1) * 128], rhs=qnT[:, lo:hi],
                                                  start=True, stop=False), reads=[knT, qnT], writes=[PB[scb]], signal=False)
                    K.op('pe', lambda e: e.matmul(PB[scb][:, lr:512], lhsT=kropeT[:, kb * 128:(kb + 1) * 128], rhs=qrT[:, lo:hi],
                                                  start=False, stop=True), reads=[kropeT, qrT], writes=[PB[scb]])

                def a_rest(i):
                    kb, ci, lo, hi, lr, diag, s0 = pairs[i]
                    scb = scbanks[i % 3]
                    pT = pTs[i % 3]
                    K.op('act', lambda e: e.activation(out=pT[:, lr:512], in_=PB[scb][:, lr:512], func=AF.Exp, scale=rks[:, kb:kb + 1]),
                         reads=[PB[scb], rksP[0 if kb < KSPL else 1]], writes=[pT])
                    K.op('pe', lambda e: e.matmul(PB[ci][:, lr:512], lhsT=V4[:, kb, hh * 128:(hh + 1) * 128], rhs=pT[:, lr:512],
                                                  start=(kb == 0), stop=(kb == last_kb[ci])), reads=[V4, pT], writes=[PB[ci]], signal=False)
                    K.op('pe', lambda e: e.matmul(PB[2 + ci][:, lr:512], lhsT=ones_b[:, :], rhs=pT[:, lr:512],
                                                  start=(kb == 0), stop=(kb == last_kb[ci])), reads=[ones_b, pT], writes=[PB[2 + ci]])

                npairs = len(pairs)
                for i in range(min(DEPTH, npairs)):
                    a_scores(i)
                for i in range(npairs):
                    if i + DEPTH < npairs:
                        a_scores(i + DEPTH)
                    a_rest(i)
                for ci in range(2):
                    rb_ = recips[ci]
                    K.op('act', lambda e, ci=ci, rb_=rb_: e.activation(out=rb_[:, :], in_=PB[2 + ci][:, :], func=AF.Ln), reads=[PB[2 + ci]], writes=[rb_])
                    K.op('act', lambda e, rb_=rb_: e.activation(out=rb_[:, :], in_=rb_[:, :], func=AF.Exp, scale=-1.0), reads=[rb_], writes=[rb_])
                    K.op('dve', lambda e, ci=ci, rb_=rb_: e.tensor_tensor(out=oT_all[:, h, ci * 512:(ci + 1) * 512], in0=PB[ci][:, :], in1=rb_[:, :],
                                                                 op=ALU.mult), reads=[PB[ci], rb_], writes=[oT_all])
            else:
                c0 = 1024 + 32 * (prob - 1)

                def s_scores(kb):
                    nk = min(128, 4128 - kb * 128)
                    scb = scbanks[kb % 3]
                    K.op('pe', lambda e: e.matmul(PB[scb][0:nk, 0:32], lhsT=knT[:, kb * 128:kb * 128 + nk], rhs=qnT[:, c0:c0 + 32],
                                                  start=True, stop=False), reads=[knT, qnT], writes=[PB[scb]], signal=False)
                    K.op('pe', lambda e: e.matmul(PB[scb][0:nk, 0:32], lhsT=kropeT[:, kb * 128:kb * 128 + nk], rhs=qrT[:, c0:c0 + 32],
                                                  start=False, stop=True), reads=[kropeT, qrT], writes=[PB[scb]])

                def s_rest(kb):
                    nk = min(128, 4128 - kb * 128)
                    scb = scbanks[kb % 3]
                    pT = pTs[kb % 3]
                    K.op('act', lambda e: e.activation(out=pT[0:nk, 0:32], in_=PB[scb][0:nk, 0:32], func=AF.Exp, scale=rks[0:nk, kb:kb + 1]),
                         reads=[PB[scb], rksP[0 if kb < KSPL else 1]], writes=[pT])
                    K.op('pe', lambda e: e.matmul(PB[0][:, 0:32], lhsT=V4[0:nk, kb, hh * 128:(hh + 1) * 128], rhs=pT[0:nk, 0:32],
                                                  start=(kb == 0), stop=(kb == NKB - 1)), reads=[V4, pT], writes=[PB[0]], signal=False)
                    K.op('pe', lambda e: e.matmul(PB[2][:, 0:32], lhsT=ones_b[0:nk, :], rhs=pT[0:nk, 0:32],
                                                  start=(kb == 0), stop=(kb == NKB - 1)), reads=[ones_b, pT], writes=[PB[2]])

                for kb in range(DEPTH):
                    s_scores(kb)
                for kb in range(NKB):
                    if kb + DEPTH < NKB:
                        s_scores(kb + DEPTH)
                    s_rest(kb)
                K.op('act', lambda e: e.activation(out=recip[:, 0:32], in_=PB[2][:, 0:32], func=AF.Ln), reads=[PB[2]], writes=[recip])
                K.op('act', lambda e: e.activation(out=recip[:, 0:32], in_=recip[:, 0:32], func=AF.Exp, scale=-1.0), reads=[recip], writes=[recip])
                K.op('dve', lambda e: e.tensor_tensor(out=oT_all[:, h, c0:c0 + 32], in0=PB[0][:, 0:32], in1=recip[:, 0:32], op=ALU.mult),
                     reads=[PB[0], recip], writes=[oT_all])

        def key_side4a(bi, s4):
            c0 = s4 * 512
            b0 = s4 * 4
            lt, kp, kr = lat4[s4 % 2], kp4[s4 % 2], krb4[s4 % 2]
            K.dma(lt[:], latc[bi, c0:c0 + 512, :].rearrange("(a p) n -> p a n", p=128), writes=[lt], q='pool')
            K.dma(kp[:], kpec[bi, c0:c0 + 512, :].rearrange("(a p) n -> p a n", p=128), writes=[kp])
            for k in range(4):
                pv = bfv(k)
                for a in range(4):
                    K.op('pe', lambda e, k=k, a=a, pv=pv: e.transpose(out=pv[:, a * 128:(a + 1) * 128], in_=lt[:, a, k * 128:(k + 1) * 128],
                                                                     identity=ident_b[:, :]),
                         reads=[lt, ident_b], writes=[PB[k]], signal=(a == 3))
                K.op(K.ev(), (lambda k, pv: (lambda e: (
                    e.activation(out=ckvT[:, k, c0:c0 + 512], in_=pv[:, 0:512], func=AF.Copy) if e is nc.scalar
                    else e.tensor_copy(out=ckvT[:, k, c0:c0 + 512], in_=pv[:, 0:512]))))(k, pv), reads=[PB[k]], writes=[ckvT])
            for a in range(4):
                K.op('act', lambda e, a=a: e.activation(out=kt2_4[:, a, :], in_=kp[:, a, :], func=AF.Square, accum_out=sspe[:, b0 + a:b0 + a + 1]),
                     reads=[kp], writes=[kt2_4, sspe])
            for a in range(4):
                K.op('dve', lambda e, a=a: e.tensor_tensor(out=kpg4[:, a, :], in0=kp[:, a, :], in1=gkr_bc[:, :], op=ALU.mult),
                     reads=[kp, gkr_bc], writes=[kpg4])
            K.op('dve', lambda e: e.tensor_tensor(out=kt1_4[:, :, :], in0=kpg4[:, :, :], in1=ktc[:, b0:b0 + 4, :], op=ALU.mult),
                 reads=[kpg4, ktc], writes=[kt1_4])
            K.op('dve', lambda e: e.tensor_tensor(out=kt2_4[:, :, 0:32], in0=kpg4[:, :, 32:64], in1=kts[:, b0:b0 + 4, 0:32], op=ALU.mult),
                 reads=[kpg4, kts], writes=[kt2_4])
            K.op('dve', lambda e: e.tensor_tensor(out=kt2_4[:, :, 32:64], in0=kpg4[:, :, 0:32], in1=kts[:, b0:b0 + 4, 32:64], op=ALU.mult),
                 reads=[kpg4, kts], writes=[kt2_4])
            K.op('dve', lambda e: e.tensor_tensor(out=kr[:, :, :], in0=kt1_4[:, :, :], in1=kt2_4[:, :, :], op=ALU.add),
                 reads=[kt1_4, kt2_4], writes=[kr])

        def key_side4b(s4):
            c0 = s4 * 512
            kr = krb4[s4 % 2]
            pr = bfv(4, 64)
            for a in range(4):
                K.op('pe', lambda e, a=a: e.transpose(out=pr[:, a * 128:(a + 1) * 128], in_=kr[:, a, :], identity=ident_b[:, :]),
                     reads=[kr, ident_b], writes=[PB[4]], signal=(a == 3))
            K.op('act', lambda e: e.activation(out=kropeT[0:64, c0:c0 + 512], in_=pr[:, 0:512], func=AF.Copy), reads=[PB[4]], writes=[kropeT])

        K.dma(wv_g[:], w_vb[:, 0:512].rearrange("(k p) n -> p k n", p=128), writes=[wv_g], q='pool')
        K.dma(wk_h[:], w_kb[:, 0:128].rearrange("(k p) n -> p k n", p=128), writes=[wk_h], q='pool')
        K.dma(wq_h[:], w_qb[:, 0:192].rearrange("(k p) n -> p k n", p=128), writes=[wq_h], q='pool')
        K.dma(wrot_h[:], w_qbrot[:, 0:64].rearrange("(k p) n -> p k n", p=128), writes=[wrot_h], q='pool')
        for prob in range(3):
            if prob > 0:
                bi = prob - 1
                K.barrier()
                for s4 in range(8):
                    key_side4a(bi, s4)
                    if s4 >= 1:
                        key_side4b(s4 - 1)
                key_side4b(7)
                pb = 32 * bi
                key_side(CH[0], cnew[pb:pb + 32, :], [cnew], kpnew[pb:pb + 32, :], [kpnew], 32, 32, 4096,
                         sspe_new[pb:pb + 32, 0:1], sspe_new, pb=pb)
                K.dma(sspe[0:32, 32:33], sspe_new[pb:pb + 32, 0:1], reads=[sspe_new], writes=[sspe])
                K.barrier()
            for h in range(16):
                attention_head(h, prob)
        K.barrier()
        if stage <= 2:
            K.finish()
            return nc
        A.at(R1)
        hT_own = A.alloc("hT_own", [128, 16, TOWN], BF16)
        assert A.off == R2
        dT = A.alloc("dT", [128, 8, TOWN], BF16)
        a_memT = A.alloc("a_memT", [128, 8, TOWN], BF16)
        C_T = A.off
        assert C_T == 126976
        A.at(C_T)
        u_tm = A.alloc("u_tm", [128, 10, 1024], BF16)
        hT_prev = A.alloc("hT_prev", [128, 16, 128], BF16)
        spool_b = A.alloc("spool_b", [32, 1024], BF16)
        mcur_b = A.alloc("mcur_b", [128, 8, 128], BF16)
        mprev_b = A.alloc("mprev_b", [128, 32, 128], BF16)
        mcurS_b = A.alloc("mcurS_b", [64, 4, 64], BF16)
        mprevS_b = A.alloc("mprevS_b", [32, 4, 64], BF16)
        wpool_b = A.alloc("wpool_b", [128, 8, 512], BF16)
        uf = A.alloc("uf", [128, 2, 1024])
        c_ss = A.alloc("c_ss", [128, 1])
        c_r = A.alloc("c_r", [128, 1])
        C1_T = A.off
        xbc = [A.alloc(f"xbc{i}", [128, D]) for i in range(2)]
        hbc = A.alloc("hbc", [128, D], BF16)
        for i in range(4):
            K.dma(hT_own[:, 4 * i:4 * i + 4, :], hscr_t[:, 4 * i:4 * i + 4, 0:TOWN], reads=[hscr], writes=[hT_own], skip_wait=(i > 0))
        K.dma(hT_prev[:, :, :], hscr_t[:, :, TOWN:TOWN + 128], reads=[hscr], writes=[hT_prev])
        A.at(C1_T)
        wu = [A.alloc(f"wu{i}", [128, 16, 256], BF16) for i in range(2)]
        tblocks = [(s, r0, n) for s, (r0, n) in enumerate(BLOCKS)] + [(9, None, 128)]

        def c1_consts():
            K.dma(spool_b[:], spool, writes=[spool_b], q='pool')
            K.dma(mcur_b[:], mcur, writes=[mcur_b], q='pool')
            K.dma(mprev_b[:], mprev, writes=[mprev_b], q='pool')
            K.dma(mcurS_b[:], mcurS, writes=[mcurS_b], q='pool')
            K.dma(mprevS_b[:], mprevS, writes=[mprevS_b], q='pool')
        for ct in range(4):
            wt_ = wu[ct % 2]
            K.dma_k(wt_, lambda a, b, wt_=wt_: wt_[:, a:b, :], w_in[:, OFF_U + ct * 256:OFF_U + (ct + 1) * 256], 16, 2)
            if ct == 1:
                c1_consts()
            for ti, (slot, r0, n) in enumerate(tblocks):
                bank = 2 + ti % 4
                if slot == 9:
                    lf = lambda k: hT_prev[:, k, :]
                    rd = [hT_prev, wt_]
                else:
                    lf = lambda k, r0=r0, n=n: hT_own[:, k, r0:r0 + n]
                    rd = [hT_own, wt_]
                mm_tm(PB[bank][0:n, 0:256], bank, lf, lambda k, wt_=wt_: wt_[:, k, :], 16, rd)
                K.op('act', lambda e, bank=bank, n=n, slot=slot, ct=ct: e.activation(
                    out=u_tm[0:n, slot, ct * 256:(ct + 1) * 256], in_=PB[bank][0:n, 0:256], func=AF.Copy), reads=[PB[bank]], writes=[u_tm])
                if slot in (7, 8):
                    K.op('dve', lambda e, bank=bank, n=n, slot=slot, ct=ct: e.tensor_copy(
                        out=uf[0:n, slot - 7, ct * 256:(ct + 1) * 256], in_=PB[bank][0:n, 0:256]), reads=[PB[bank]], writes=[uf])
        K.dma(poolp_out[:, :], uf[113:128, 0, :], reads=[uf], final=True)
        K.dma(pools_out[0], uf[17:32, 1, :], reads=[uf], final=True)
        K.dma(pools_out[1], uf[49:64, 1, :], reads=[uf], final=True)
        for s, (r0, n) in enumerate(BLOCKS):
            for half in range(2):
                bank = 6 + (s * 2 + half) % 2
                for q4 in range(4):
                    c8 = half * 4 + q4
                    g = c8 // 2
                    o = PB[bank][:, q4 * 128:q4 * 128 + n]
                    if s < 8:
                        idx = g if s == 0 else 4 + g
                        K.op('pe', lambda e, o=o, s=s, c8=c8, idx=idx: e.matmul(
                            o, lhsT=u_tm[:, s, c8 * 128:(c8 + 1) * 128], rhs=mcur_b[:, idx, :], start=True, stop=False),
                            reads=[u_tm, mcur_b], writes=[PB[bank]], signal=False)
                        K.op('pe', lambda e, o=o, s=s, c8=c8, g=g: e.matmul(
                            o, lhsT=u_tm[:, 9, c8 * 128:(c8 + 1) * 128], rhs=mprev_b[:, s * 4 + g, :], start=False, stop=True),
                            reads=[u_tm, mprev_b], writes=[PB[bank]], signal=(q4 == 3))
                    else:
                        K.op('pe', lambda e, o=o, c8=c8, g=g: e.matmul(
                            o, lhsT=u_tm[0:64, 8, c8 * 128:(c8 + 1) * 128], rhs=mcurS_b[:, g, :], start=True, stop=False),
                            reads=[u_tm, mcurS_b], writes=[PB[bank]], signal=False)
                        K.op('pe', lambda e, o=o, c8=c8, g=g: e.matmul(
                            o, lhsT=spool_b[:, c8 * 128:(c8 + 1) * 128], rhs=mprevS_b[:, g, :], start=False, stop=True),
                            reads=[spool_b, mprevS_b], writes=[PB[bank]], signal=(q4 == 3))
                K.op(K.ev(), (lambda bank, half, r0, n: (lambda e: (
                    e.activation(out=dT[:, half * 4:half * 4 + 4, r0:r0 + n],
                                 in_=PB[bank][:, :].rearrange("p (a b) -> p a b", a=4)[:, :, 0:n], func=AF.Copy) if e is nc.scalar
                    else e.tensor_copy(out=dT[:, half * 4:half * 4 + 4, r0:r0 + n],
                                       in_=PB[bank][:, :].rearrange("p (a b) -> p a b", a=4)[:, :, 0:n]))))(bank, half, r0, n),
                    reads=[PB[bank]], writes=[dT])
        K.barrier()
        A.at(C_T)
        wmqs = [A.alloc(f"wmq{i}", [128, 16, 512], BF16) for i in range(2)]
        qmT_all = A.alloc("qmT_all", [128, 8, TOWN], BF16)
        sqh = [[A.alloc(f"sqh{i}{j}", [128, 512], BF16) for j in range(2)] for i in range(2)]
        rrc = [A.alloc(f"rrc{i}", [128, 512]) for i in range(2)]
        pTm = A.alloc("pTm", [128, 2, 512], BF16)
        recm = A.alloc("recm", [128, 512])
        gmqT = A.alloc("gmqT", [128, 2])
        cmk_b = A.alloc("cmk_b", [128, 2, 1024], BF16)
        memkT_s = [A.alloc(f"memkT_s{i}", [128, 8, 256], BF16) for i in range(2)]
        memv_s = [A.alloc(f"memv_s{i}", [128, 2, 1024], BF16) for i in range(2)]
        for i in range(2):
            K.dma_k(wmqs[i], lambda a, b, i=i: wmqs[i][:, a:b, :], w_in[:, OFF_MQ + i * 512:OFF_MQ + (i + 1) * 512], 16, 4)
        with nc.allow_non_contiguous_dma(reason="tiny per-partition gain vector"):
            K.dma(gmqT[:], g_mem_q.rearrange("(c p) -> p c", p=128), writes=[gmqT])
        for bi in range(2):
            K.dma(cmk_b[:], cmk[bi].rearrange("(a p) n -> p a n", p=128), writes=[cmk_b], q='pool')
            K.dma(memv_s[bi][:], cmv[bi].rearrange("(a p) n -> p a n", p=128), writes=[memv_s[bi]], q='pool')
            build_memkT(cmk_b, memkT_s[bi])
        MEM_SCALE = 1.0 / 16.0
        c2chunks = [(0, 512), (512, 512), (1024, 64)]
        ui = 0
        for hh in range(4):
            wm = wmqs[hh // 2]
            for (c0, w) in c2chunks:
                st_ = ui % 2
                ui += 1
                bA = (0, 1) if st_ == 0 else (3, 4)
                bS = 2 if st_ == 0 else 5
                for dc in range(2):
                    col0 = (hh % 2) * 256 + dc * 128
                    mm_tm(PB[bA[dc]][:, 0:w], bA[dc], lambda k, wm=wm, col0=col0: wm[:, k, col0:col0 + 128],
                          lambda k, c0=c0, w=w: hT_own[:, k, c0:c0 + w], 16, [wm, hT_own])
                    K.op('act', lambda e, dc=dc, st_=st_, w=w, bA=bA: e.activation(out=sqh[st_][dc][:, 0:w], in_=PB[bA[dc]][:, 0:w], func=AF.Square),
                         reads=[PB[bA[dc]]], writes=[sqh[st_][dc]])
                for dc in range(2):
                    K.op('pe', lambda e, dc=dc, st_=st_, w=w, bS=bS: e.matmul(PB[bS][:, 0:w], lhsT=ones_b[:, :], rhs=sqh[st_][dc][:, 0:w],
                                                                             start=(dc == 0), stop=(dc == 1)),
                         reads=[ones_b, sqh[st_][dc]], writes=[PB[bS]], signal=(dc == 1))
                K.op('act', lambda e, st_=st_, w=w, bS=bS: e.activation(out=rrc[st_][:, 0:w], in_=PB[bS][:, 0:w], func=AF.Ln, bias=eps6[:], scale=1.0 / 256),
                     reads=[PB[bS], eps6], writes=[rrc[st_]])
                K.op('act', lambda e, st_=st_, w=w: e.activation(out=rrc[st_][:, 0:w], in_=rrc[st_][:, 0:w], func=AF.Exp, scale=-0.5),
                     reads=[rrc[st_]], writes=[rrc[st_]])
                for dc in range(2):
                    K.op('dve', lambda e, dc=dc, st_=st_, w=w, c0=c0, hh=hh, bA=bA: e.scalar_tensor_tensor(
                        out=qmT_all[:, hh * 2 + dc, c0:c0 + w], in0=PB[bA[dc]][:, 0:w], scalar=gmqT[:, dc:dc + 1], in1=rrc[st_][:, 0:w],
                        op0=ALU.mult, op1=ALU.mult), reads=[PB[bA[dc]], gmqT, rrc[st_]], writes=[qmT_all])
        units = [(memkT_p, memv_p, hh, c0, 512) for hh in range(4) for c0 in (0, 512)]
        units += [(memkT_s[bi], memv_s[bi], hh, 1024 + 32 * bi, 32) for bi in range(2) for hh in range(4)]

        def m_scores(u):
            mkT, mv, hh, c0, w = units[u]
            sc = (0, 1) if u % 2 == 0 else (2, 3)
            for mb in range(2):
                for dc in range(2):
                    K.op('pe', lambda e, mb=mb, dc=dc: e.matmul(
                        PB[sc[mb]][:, 0:w], lhsT=mkT[:, hh * 2 + dc, mb * 128:(mb + 1) * 128], rhs=qmT_all[:, hh * 2 + dc, c0:c0 + w],
                        start=(dc == 0), stop=(dc == 1)), reads=[mkT, qmT_all], writes=[PB[sc[mb]]], signal=(dc == 1))

        def m_rest(u):
            mkT, mv, hh, c0, w = units[u]
            sc = (0, 1) if u % 2 == 0 else (2, 3)
            for mb in range(2):
                K.op('act', lambda e, mb=mb: e.activation(out=pTm[:, mb, 0:w], in_=PB[sc[mb]][:, 0:w], func=AF.Exp, scale=MEM_SCALE),
                     reads=[PB[sc[mb]]], writes=[pTm])
            for dvc in range(2):
                for mb in range(2):
                    K.op('pe', lambda e, dvc=dvc, mb=mb: e.matmul(
                        PB[4 + dvc][:, 0:w], lhsT=mv[:, mb, hh * 256 + dvc * 128:hh * 256 + (dvc + 1) * 128], rhs=pTm[:, mb, 0:w],
                        start=(mb == 0), stop=(mb == 1)), reads=[mv, pTm], writes=[PB[4 + dvc]], signal=(mb == 1))
            for mb in range(2):
                K.op('pe', lambda e, mb=mb: e.matmul(PB[6][:, 0:w], lhsT=ones_b[:, :], rhs=pTm[:, mb, 0:w], start=(mb == 0), stop=(mb == 1)),
                     reads=[ones_b, pTm], writes=[PB[6]], signal=(mb == 1))
            K.op('act', lambda e: e.activation(out=recm[:, 0:w], in_=PB[6][:, 0:w], func=AF.Ln), reads=[PB[6]], writes=[recm])
            K.op('act', lambda e: e.activation(out=recm[:, 0:w], in_=recm[:, 0:w], func=AF.Exp, scale=-1.0), reads=[recm], writes=[recm])
            for dvc in range(2):
                K.op('dve', lambda e, dvc=dvc: e.tensor_tensor(out=a_memT[:, hh * 2 + dvc, c0:c0 + w], in0=PB[4 + dvc][:, 0:w], in1=recm[:, 0:w],
                                                               op=ALU.mult), reads=[PB[4 + dvc], recm], writes=[a_memT])

        m_scores(0)
        for u in range(len(units)):
            if u + 1 < len(units):
                m_scores(u + 1)
            m_rest(u)
        K.barrier()
        A.at(C_T)
        mergedT = A.alloc("mergedT", [128, 16, TOWN], BF16)
        C3_T = A.off
        wg = [A.alloc(f"wg{i}", [128, 3, 16, 128], BF16) for i in range(2)]
        wmo = [A.alloc(f"wmo{i}", [128, 16, 128], BF16) for i in range(2)]
        wme = [A.alloc(f"wme{i}", [128, 8, 128], BF16) for i in range(2)]
        wpo = [A.alloc(f"wpo{i}", [128, 2, 128], BF16) for i in range(2)]
        gs = A.alloc("gs", [128, 512])
        acc = A.alloc("acc", [128, 512])
        tmpc = A.alloc("tmpc", [128, 512])
        bgT = A.alloc("bgT", [128, 48])
        psT = A.alloc("psT", [128, 16])
        with nc.allow_non_contiguous_dma(reason="tiny per-partition bias/scale vectors"):
            K.dma(bgT[:], b_gate.rearrange("(c p) -> p c", p=128), writes=[bgT])
            K.dma(psT[:], pool_scale.rearrange("(c p) -> p c", p=128), writes=[psT])
        tchunks = [(0, 512), (512, 512), (1024, 64)]
        for cg in range(16):
            x_ = cg % 2
            for br in range(3):
                K.dma(wg[x_][:, br, :, :], w_in[:, OFF_G + br * 2048 + cg * 128:OFF_G + br * 2048 + (cg + 1) * 128].rearrange(
                    "(k p) n -> p k n", p=128), writes=[wg[x_]], q='pool')
            K.dma(wmo[x_][:], w_mla_o[:, cg * 128:(cg + 1) * 128].rearrange("(k p) n -> p k n", p=128), writes=[wmo[x_]], q='pool')
            K.dma(wme[x_][:], w_mem_o[:, cg * 128:(cg + 1) * 128].rearrange("(k p) n -> p k n", p=128), writes=[wme[x_]], q='pool')
            gi = cg // 4
            K.dma(wpo[x_][:], w_pool[gi, :, (cg % 4) * 128:(cg % 4 + 1) * 128].rearrange("(k p) n -> p k n", p=128), writes=[wpo[x_]], q='pool')
            for (c0, w) in tchunks:
                for br in range(3):
                    mm_tm(PB[br][:, 0:w], br, lambda k, br=br: wg[x_][:, br, k, :], lambda k, c0=c0, w=w: hT_own[:, k, c0:c0 + w], 16,
                          [wg[x_], hT_own])
                mm_tm(PB[3][:, 0:w], 3, lambda k: wpo[x_][:, k, :], lambda k, c0=c0, w=w: dT[:, gi * 2 + k, c0:c0 + w], 2, [wpo[x_], dT])
                mm_tm(PB[4][:, 0:w], 4, lambda k: wmo[x_][:, k, :], lambda k, c0=c0, w=w: oT_all[:, k, c0:c0 + w], 16, [wmo[x_], oT_all])
                mm_tm(PB[5][:, 0:w], 5, lambda k: wme[x_][:, k, :], lambda k, c0=c0, w=w: a_memT[:, k, c0:c0 + w], 8, [wme[x_], a_memT])
                K.op('act', lambda e, w=w: e.activation(out=gs[:, 0:w], in_=PB[0][:, 0:w], func=AF.Sigmoid, bias=bgT[:, cg:cg + 1], scale=1.0),
                     reads=[PB[0], bgT], writes=[gs])
                K.op('dve', lambda e, w=w: e.scalar_tensor_tensor(out=acc[:, 0:w], in0=PB[3][:, 0:w], scalar=psT[:, cg:cg + 1], in1=gs[:, 0:w],
                                                                  op0=ALU.mult, op1=ALU.mult), reads=[PB[3], psT, gs], writes=[acc])
                K.op('act', lambda e, w=w: e.activation(out=gs[:, 0:w], in_=PB[1][:, 0:w], func=AF.Sigmoid, bias=bgT[:, 16 + cg:17 + cg], scale=1.0),
                     reads=[PB[1], bgT], writes=[gs])
                K.op('dve', lambda e, w=w: e.tensor_tensor(out=tmpc[:, 0:w], in0=PB[4][:, 0:w], in1=gs[:, 0:w], op=ALU.mult),
                     reads=[PB[4], gs], writes=[tmpc])
                K.op('dve', lambda e, w=w: e.tensor_tensor(out=acc[:, 0:w], in0=acc[:, 0:w], in1=tmpc[:, 0:w], op=ALU.add),
                     reads=[acc, tmpc], writes=[acc])
                K.op('act', lambda e, w=w: e.activation(out=gs[:, 0:w], in_=PB[2][:, 0:w], func=AF.Sigmoid, bias=bgT[:, 32 + cg:33 + cg], scale=1.0),
                     reads=[PB[2], bgT], writes=[gs])
                K.op('dve', lambda e, w=w: e.tensor_tensor(out=tmpc[:, 0:w], in0=PB[5][:, 0:w], in1=gs[:, 0:w], op=ALU.mult),
                     reads=[PB[5], gs], writes=[tmpc])
                K.op('dve', lambda e, c0=c0, w=w: e.tensor_tensor(out=mergedT[:, cg, c0:c0 + w], in0=acc[:, 0:w], in1=tmpc[:, 0:w], op=ALU.add),
                     reads=[acc, tmpc], writes=[mergedT])
        K.barrier()
        A.at(OT_OFF)
        y_acc = A.alloc("y_acc", [128, 9, D])
        assert A.off == 96256
        yb = [Buf(A.t, f"yacc{s}", ap=y_acc._ap[:, s, :]) for s in range(9)]
        wo = [A.alloc(f"wo{i}", [128, 16, 256], BF16) for i in range(2)]
        hb2s = [A.alloc(f"hb2_{i}", [128, D], BF16) for i in range(3)]
        d_sss = [A.alloc(f"d_ss{i}", [128, 1]) for i in range(3)]
        d_rs = [A.alloc(f"d_r{i}", [128, 1]) for i in range(3)]
        assert A.off <= C_T
        A.at(C3_T)
        h2T = A.alloc("h2T", [128, 16, TOWN], BF16)
        K.dma(gbc[:], g_ff.partition_broadcast(128), writes=[gbc])

        def load_wo(ct):
            K.dma_k(wo[ct % 2], lambda a, b: wo[ct % 2][:, a:b, :], w_out[:, ct * 256:(ct + 1) * 256], 16, 2)

        def load_xres(s):
            r0, n = BLOCKS[s]
            K.dma(yb[s][0:n, :], xown[r0:r0 + n, :], writes=[yb[s]], q='pool')

        load_wo(0)
        for s in range(4):
            load_xres(s)
        load_wo(1)
        for s in range(4, 9):
            load_xres(s)

        def c4_norm_a(s):
            r0, n = BLOCKS[s]
            hb2, d_ss, d_r = hb2s[s % 3], d_sss[s % 3], d_rs[s % 3]
            rms_stats(yb[s][0:n, :], n, D, hb2[0:n, :], d_ss, d_r, reads=[yb[s]], extra_w=[hb2])
            K.op('dve', lambda e: e.scalar_tensor_tensor(out=hb2[0:n, :], in0=yb[s][0:n, :], scalar=d_r[0:n, :], in1=gbc[0:n, :],
                                                         op0=ALU.mult, op1=ALU.mult), reads=[yb[s], d_r, gbc], writes=[hb2])

        def c4_norm_t(s):
            r0, n = BLOCKS[s]
            hb2 = hb2s[s % 3]
            transposes(hb2, [hb2], n, 16, lambda k: h2T[:, k, r0:r0 + n], [h2T], [4, 5] if s % 2 == 0 else [6, 7],
                       dst_grp=lambda k0, g: h2T[:, k0:k0 + g, r0:r0 + n])

        for ct in range(8):
            wt_ = wo[ct % 2]
            if ct >= 2:
                load_wo(ct)
            for s, (r0, n) in enumerate(BLOCKS):
                bank = s % 4
                mm_tm(PB[bank][0:n, 0:256], bank, lambda k, r0=r0, n=n: mergedT[:, k, r0:r0 + n], lambda k, wt_=wt_: wt_[:, k, :], 16,
                      [mergedT, wt_])
                K.op('dve', lambda e, bank=bank, n=n, s=s, ct=ct: e.tensor_tensor(
                    out=yb[s][0:n, ct * 256:(ct + 1) * 256], in0=PB[bank][0:n, 0:256], in1=yb[s][0:n, ct * 256:(ct + 1) * 256], op=ALU.add),
                    reads=[PB[bank], yb[s]], writes=[yb[s]])
                if ct == 7:
                    if s >= 4:
                        c4_norm_t(s - 4)
                    if s >= 1:
                        c4_norm_a(s - 1)
        c4_norm_t(5)
        c4_norm_a(8)
        for s in range(6, 9):
            c4_norm_t(s)
        K.barrier()
        A.at(96256)
        fT = A.alloc("fT", [128, 8, TOWN], BF16)
        wdn = [A.alloc(f"wdn{i}", [128, D], BF16) for i in range(8)]
        wup0 = A.alloc("wup0", [128, 16, 256], BF16)
        rl = [A.alloc(f"rl{i}", [128, 512]) for i in range(2)]
        assert A.off <= C3_T, A.off
        A.at(C3_T + 16 * TOWN * 2)
        wup1 = A.alloc("wup1", [128, 16, 256], BF16)
        wup = [wup0, wup1]
        NT = 32

        def load_wup(T):
            K.dma(wup[T % 2][:], w_up[:, T * 256:(T + 1) * 256].rearrange("(k p) n -> p k n", p=128), writes=[wup[T % 2]], q='pool')

        def load_wdn(G):
            for hc in range(8):
                f_ = G * 8 + hc
                K.dma(wdn[hc][:], w_down[f_ * 128:(f_ + 1) * 128, :], writes=[wdn[hc]], q='pool')

        load_wup(0)
        load_wup(1)
        load_wdn(0)
        ui = 0
        for G in range(8):
            for q4 in range(4):
                T = G * 4 + q4
                wt_ = wup[T % 2]
                for sub in range(2):
                    hc = 2 * q4 + sub
                    for (c0, w) in tchunks:
                        bank = ui % 4
                        r_ = rl[ui % 2]
                        ui += 1
                        mm_tm(PB[bank][:, 0:w], bank, lambda k, wt_=wt_, sub=sub: wt_[:, k, sub * 128:(sub + 1) * 128],
                              lambda k, c0=c0, w=w: h2T[:, k, c0:c0 + w], 16, [wt_, h2T])
                        K.op('act', lambda e, bank=bank, w=w, r_=r_: e.activation(out=r_[:, 0:w], in_=PB[bank][:, 0:w], func=AF.Relu),
                             reads=[PB[bank]], writes=[r_])
                        K.op('dve', lambda e, w=w, r_=r_, hc=hc, c0=c0: e.tensor_tensor(out=fT[:, hc, c0:c0 + w], in0=r_[:, 0:w], in1=r_[:, 0:w],
                                                                                       op=ALU.mult), reads=[r_], writes=[fT])
                if T + 2 < NT:
                    load_wup(T + 2)
            for s, (r0, n) in enumerate(BLOCKS):
                for c4 in range(4):
                    bank = 4 + c4
                    mm_tm(PB[bank][0:n, :], bank, lambda k, r0=r0, n=n: fT[:, k, r0:r0 + n], lambda k, c4=c4: wdn[k][:, c4 * 512:(c4 + 1) * 512], 8,
                          [fT] + wdn)
                    K.op('dve', lambda e, bank=bank, n=n, s=s, c4=c4: e.tensor_tensor(
                        out=yb[s][0:n, c4 * 512:(c4 + 1) * 512], in0=PB[bank][0:n, :], in1=yb[s][0:n, c4 * 512:(c4 + 1) * 512], op=ALU.add),
                        reads=[PB[bank], yb[s]], writes=[yb[s]])
                if G == 7:
                    K.dma(y_own[r0:r0 + n, :], yb[s][0:n, :], reads=[yb[s]], final=True)
            if G + 1 < 8:
                load_wdn(G + 1)
        K.finish()
        print("program built: ninst", K.ninst, "nsem", K.nsem, flush=True)
        return nc


_PROG = {}
POOL_WINDOWS = (2, 4, 8, 16)


def _rope_tables():
    half = 32
    inv = (1.0 / (np.float32(10000.0) ** (np.arange(half, dtype=np.float32) * np.float32(2.0 / 64)))).astype(np.float32)
    return inv


def _consts(j):
    inv = _rope_tables()
    c = {}
    c["ident"] = np.eye(128, dtype=np.float32)
    pos = (np.arange(NKB)[None, :] * 128 + np.arange(128)[:, None]).astype(np.float32)
    pos[:, 32] = 4096 + (np.arange(128) % 32)
    ang = pos[:, :, None] * inv[None, None, :]
    cs, sn = np.cos(ang).astype(np.float32), np.sin(ang).astype(np.float32)
    c["ktab_c"] = np.concatenate([cs, cs], axis=-1)
    c["ktab_s"] = np.concatenate([-sn, sn], axis=-1)
    qpos = np.zeros(TOWN, np.float32)
    for s in range(8):
        qpos[s * 128:(s + 1) * 128] = (4 * s + j) * 128 + np.arange(128)
    qpos[1024:1056] = 4096 + np.arange(32)
    qpos[1056:1088] = 4096 + np.arange(32)
    qa = qpos[None, :] * inv[:, None]
    qc, qs = np.cos(qa).astype(np.float32), np.sin(qa).astype(np.float32)
    c["qtab_c"] = np.concatenate([qc, qc], axis=0)
    c["qtab_s"] = np.concatenate([-qs, qs], axis=0)
    mcur = np.zeros((128, 8, 128), np.float32)
    for g, win in enumerate(POOL_WINDOWS):
        for t in range(128):
            for first in (True, False):
                cnt = min(t + 1, win) if first else win
                idx = g if first else 4 + g
                lo = max(0, t - win + 1)
                mcur[lo:t + 1, idx, t] += 1.0 / cnt
                mcur[t, idx, t] -= 1.0
    if j != 0:
        mcur[:, 0:4, :] = mcur[:, 4:8, :]
    c["mcur"] = mcur
    mprev = np.zeros((128, 32, 128), np.float32)
    for s in range(8):
        for g, win in enumerate(POOL_WINDOWS):
            for t in range(min(win - 1, 128)):
                for ps in range(t - win + 1, 0):
                    mprev[16 * s + 16 + ps, s * 4 + g, t] += 1.0 / win
    if j == 0:
        mprev[:, 0:4, :] = 0.0
    c["mprev"] = mprev
    mcs = np.zeros((64, 4, 64), np.float32)
    mps = np.zeros((32, 4, 64), np.float32)
    for bi in range(2):
        for g, win in enumerate(POOL_WINDOWS):
            for t in range(32):
                lo = max(0, t - win + 1)
                mcs[bi * 32 + lo:bi * 32 + t + 1, g, bi * 32 + t] += 1.0 / win
                mcs[bi * 32 + t, g, bi * 32 + t] -= 1.0
                for ps in range(t - win + 1, 0):
                    mps[bi * 16 + 16 + ps, g, bi * 32 + t] += 1.0 / win
    c["mcurS"] = mcs
    c["mprevS"] = mps
    km = np.zeros((16, SEQ), np.float32)
    qm = np.zeros((16, 1024), np.float32)
    for s in range(8):
        qm[2 * s, s * 128:s * 128 + 64] = -30000.0
        qm[2 * s + 1, s * 128 + 64:(s + 1) * 128] = -30000.0
        for d in range(4):
            kb = 4 * s + d
            if d > j:
                km[2 * s, kb * 128:(kb + 1) * 128] = 1.0
                km[2 * s + 1, kb * 128:(kb + 1) * 128] = 1.0
            elif d == j:
                km[2 * s, kb * 128 + 64:(kb + 1) * 128] = 1.0
    c["kmask"] = km
    c["qmask"] = qm
    return c


def kernel(x_prompt, mem_prompt, x_sample, cache_mla_latent, cache_mla_kpe, state_pool,
           cache_mem_k, cache_mem_v, g_mix, w_in, b_gate, w_pool, pool_scale, g_q_lat, w_qb,
           g_q_head, g_kv_lat, w_kb, w_vb, g_k_head, w_mla_o, g_mem, w_mem_kv, g_mem_q,
           g_mem_k, w_mem_o, w_out, g_ff, w_up, w_down):
    stage = int(os.environ.get("MK_STAGE", "99"))
    f = lambda a: np.ascontiguousarray(np.asarray(a, dtype=np.float32))
    x_prompt, mem_prompt, x_sample = f(x_prompt), f(mem_prompt), f(x_sample)
    cache_mla_latent, cache_mla_kpe, state_pool = f(cache_mla_latent), f(cache_mla_kpe), f(state_pool)
    cache_mem_k, cache_mem_v = f(cache_mem_k), f(cache_mem_v)
    if stage not in _PROG:
        _PROG[stage] = build_program(stage)
    nc = _PROG[stage]
    w_qb = f(w_qb)
    wq3 = w_qb.reshape(512, 16, 192)
    w_qbrot = np.ascontiguousarray(np.concatenate([wq3[:, :, 160:192], wq3[:, :, 128:160]], axis=-1).reshape(512, 1024))
    shared = dict(w_in=f(w_in), w_pool=f(w_pool), w_qb=w_qb, w_qbrot=w_qbrot, w_kb=f(w_kb), w_vb=f(w_vb),
                  w_mla_o=f(w_mla_o), w_mem_kv=f(w_mem_kv), w_mem_o=f(w_mem_o), w_out=f(w_out), w_up=f(w_up),
                  w_down=f(w_down), g_mix=f(g_mix), b_gate=f(b_gate), pool_scale=f(pool_scale), g_q_lat=f(g_q_lat),
                  g_q_head=f(g_q_head), g_kv_lat=f(g_kv_lat), g_k_head=f(g_k_head), g_mem=f(g_mem),
                  g_mem_q=f(g_mem_q), g_mem_k=f(g_mem_k), g_ff=f(g_ff))
    in_maps = []
    for c in range(8):
        b, j = c // 4, c % 4
        xs = x_prompt[b]
        xown = np.empty((TOWN, D), np.float32)
        xprev = np.zeros((128, D), np.float32)
        for s in range(8):
            i = 4 * s + j
            xown[s * 128:(s + 1) * 128] = xs[i * 128:(i + 1) * 128]
            if i > 0:
                xprev[16 * s:16 * s + 16] = xs[i * 128 - 16:i * 128]
        xown[1024:1056] = x_sample[2 * c]
        xown[1056:1088] = x_sample[2 * c + 1]
        sp = np.zeros((32, 1024), np.float32)
        sp[1:16] = state_pool[2 * c]
        sp[17:32] = state_pool[2 * c + 1]
        m = dict(xseq=xs, xown=xown, xprev=xprev,
                 latc=cache_mla_latent[2 * c:2 * c + 2], kpec=cache_mla_kpe[2 * c:2 * c + 2], spool=sp,
                 memp=mem_prompt[b], cmk=cache_mem_k[2 * c:2 * c + 2].reshape(2, 256, 1024),
                 cmv=cache_mem_v[2 * c:2 * c + 2].reshape(2, 256, 1024))
        m.update(shared)
        m.update(_consts(j))
        in_maps.append({k: np.ascontiguousarray(v) for k, v in m.items()})
    res = run_bass_kernel_spmd(nc, in_maps, core_ids=list(range(8)))
    R = res.results
    y_p = np.zeros((2, SEQ, D), np.float32)
    lat_p = np.zeros((2, SEQ, 512), np.float32)
    kpe_p = np.zeros((2, SEQ, 64), np.float32)
    y_s = np.zeros((16, 32, D), np.float32)
    lat_s = np.zeros((16, 32, 512), np.float32)
    kpe_s = np.zeros((16, 32, 64), np.float32)
    pool_p = np.zeros((2, 15, 1024), np.float32)
    pool_s = np.zeros((16, 15, 1024), np.float32)
    mem_k_p = np.zeros((2, 256, 4, 256), np.float32)
    mem_v_p = np.zeros((2, 256, 4, 256), np.float32)
    for c in range(8):
        b, j = c // 4, c % 4
        r = R[c]
        for s in range(8):
            i = 4 * s + j
            y_p[b, i * 128:(i + 1) * 128] = r["y_own"][s * 128:(s + 1) * 128]
            lat_p[b, i * 128:(i + 1) * 128] = r["lat_own"][s * 128:(s + 1) * 128]
            kpe_p[b, i * 128:(i + 1) * 128] = r["kpe_own"][s * 128:(s + 1) * 128]
        for bi in range(2):
            y_s[2 * c + bi] = r["y_own"][1024 + 32 * bi:1056 + 32 * bi]
            lat_s[2 * c + bi] = r["lat_own"][1024 + 32 * bi:1056 + 32 * bi]
            kpe_s[2 * c + bi] = r["kpe_own"][1024 + 32 * bi:1056 + 32 * bi]
            pool_s[2 * c + bi] = r["pools_out"][bi]
        if j == 3:
            pool_p[b] = r["poolp_out"]
        if j == 0:
            mem_k_p[b] = r["memk_out"].reshape(256, 4, 256)
            mem_v_p[b] = r["memv_out"].reshape(256, 4, 256)
    return (y_p, y_s, lat_p, kpe_p, pool_p, mem_k_p, mem_v_p, lat_s, kpe_s, pool_s)
```

```python
import os
import numpy as np
from contextlib import ExitStack
import concourse.bass as bass
import concourse.mybir as mybir
from concourse.bass_utils import run_bass_kernel_spmd

F32 = mybir.dt.float32
BF16 = mybir.dt.bfloat16
AF = mybir.ActivationFunctionType
ALU = mybir.AluOpType
AX = mybir.AxisListType

D = 2048
SEQ = 4096
NB = 32
NKB = 33
TOWN = 1088
EPS = 1e-6
D_IN = 9280
OFF_U, OFF_QL, OFF_KV, OFF_KPE, OFF_MQ, OFF_G = 0, 1024, 1536, 2048, 2112, 3136
EPOCH = 30000
ARENA_BYTES = 210944

BLOCKS = [(s * 128, 128) for s in range(8)] + [(1024, 64)]


class Buf:
    def __init__(self, t, name, psum=False, ap=None):
        self.t = t
        self.name = name
        self.psum = psum
        self.w = {}
        self.r = {}
        self.dsem = None
        self.dval = 0
        self._ap = ap

    def __getitem__(self, key):
        base = self._ap if self._ap is not None else self.t
        return base[key]


class Eng:
    def __init__(self, name, obj):
        self.name = name
        self.obj = obj
        self.sem = None
        self.cnt = 0
        self.waited = {}


class Kern:
    def __init__(self, nc, stack):
        self.nc = nc
        self.stack = stack
        self.e = {n: Eng(n, o) for n, o in (('pe', nc.tensor), ('act', nc.scalar), ('dve', nc.vector),
                                            ('pool', nc.gpsimd), ('sp', nc.sync))}
        self.nsem = 0
        for n in ('pe', 'act', 'dve', 'pool'):
            self.e[n].sem = self.new_sem(n)
        self.store_toks = []
        self.dma_bufs = []
        self.ninst = 0
        self.rr = 0
        self._rec = None

    def new_sem(self, name):
        self.nsem += 1
        return self.stack.enter_context(self.nc.semaphore(f"s{self.nsem}_{name}"))

    def _need(self, en, reads, writes):
        E = self.e[en]
        need = {}

        def add(d, own_ok):
            for s, v in d.items():
                if s is E.sem and en == 'pe':
                    continue
                if need.get(s, 0) < v:
                    need[s] = v

        for b in reads:
            add(b.w, True)
            if b.psum:
                add(b.r, False)
        for b in writes:
            add(b.w, False)
            add(b.r, False)
        return need

    def _do_waits(self, en, need):
        E = self.e[en]
        for s, v in need.items():
            if E.waited.get(s, 0) >= v:
                continue
            E.obj.wait_ge(s, v)
            E.waited[s] = v
            self.ninst += 1

    def record(self):
        self._rec = []

    def stop_record(self):
        r = self._rec
        self._rec = None
        return r

    def py(self, fn):
        if self._rec is not None:
            self._rec.append(('py', (fn,), {}))
        else:
            fn()

    def replay(self, lists, width=2, stagger=0.5):
        active = []
        nxt = 0
        while nxt < len(lists) or active:
            if nxt < len(lists) and len(active) < width and (
                    not active or active[-1][1] >= stagger * len(active[-1][0])):
                active.append([lists[nxt], 0])
                nxt += 1
            for a in list(active):
                lst, pos = a
                if pos < len(lst):
                    kind, args, kw = lst[pos]
                    if kind == 'py':
                        args[0]()
                    else:
                        (self.op if kind == 'op' else self.dma)(*args, **kw)
                    a[1] += 1
                if a[1] >= len(lst):
                    active.remove(a)

    def op(self, en, fn, reads=(), writes=(), signal=True):
        if self._rec is not None:
            self._rec.append(('op', (en, fn, list(reads), list(writes), signal), {}))
            return None
        E = self.e[en]
        self._do_waits(en, self._need(en, reads, writes))
        ins = fn(E.obj)
        self.ninst += 1
        if E.cnt >= EPOCH - 200:
            E.sem = self.new_sem(en)
            E.cnt = 0
        if signal:
            E.cnt += 1
            ins.then_inc(E.sem, 1)
            tok = (E.sem, E.cnt)
        else:
            tok = (E.sem, E.cnt + 1)
        for b in reads:
            if b.r.get(tok[0], 0) < tok[1]:
                b.r[tok[0]] = tok[1]
        for b in writes:
            b.w = {tok[0]: tok[1]}
            b.r = {}
        return ins

    def dma(self, out_ap, in_ap, reads=(), writes=(), q='sp', final=False, skip_wait=False, **kw):
        if self._rec is not None:
            kw2 = dict(kw)
            kw2.update(reads=list(reads), writes=list(writes), q=q, final=final, skip_wait=skip_wait)
            self._rec.append(('dma', (out_ap, in_ap), kw2))
            return None
        E = self.e[q]
        if not skip_wait:
            self._do_waits(q, self._need(q, reads, writes))
        owner = writes[0] if writes else reads[0]
        if owner.dsem is None or owner.dval >= EPOCH:
            owner.dsem = self.new_sem('d')
            owner.dval = 0
            self.dma_bufs.append(owner)
        owner.dval += 16
        E.obj.dma_start(out=out_ap, in_=in_ap, **kw).then_inc(owner.dsem, 16)
        self.ninst += 1
        tok = (owner.dsem, owner.dval)
        for b in reads:
            if b.r.get(tok[0], 0) < tok[1]:
                b.r[tok[0]] = tok[1]
        for b in writes:
            b.w = {tok[0]: tok[1]}
            b.r = {}
        if final:
            self.store_toks.append(tok)
        return tok

    def dma_k(self, buf, dst_fn, src2d, nk, nsplit, q='pool', reads=()):
        g = nk // nsplit
        assert g * nsplit == nk
        for i in range(nsplit):
            self.dma(dst_fn(i * g, (i + 1) * g), src2d[i * g * 128:(i + 1) * g * 128, :].rearrange("(k p) n -> p k n", p=128),
                     reads=list(reads), writes=[buf], q=q, skip_wait=(i > 0))

    def barrier(self):
        names = ('pe', 'act', 'dve', 'pool')
        for a in names + ('sp',):
            need = {}
            for b in names:
                if a == b:
                    continue
                B = self.e[b]
                if B.cnt > 0:
                    need[B.sem] = B.cnt
            for b in self.dma_bufs:
                if b.dval > 0 and need.get(b.dsem, 0) < b.dval:
                    need[b.dsem] = b.dval
            self._do_waits(a, need)

    def finish(self):
        need = {}
        for s, v in self.store_toks:
            if need.get(s, 0) < v:
                need[s] = v
        for b in ('pe', 'act', 'dve', 'pool'):
            B = self.e[b]
            if B.cnt > 0:
                need[B.sem] = B.cnt
        self._do_waits('sp', need)

    def ev(self):
        self.rr ^= 1
        return 'act' if self.rr else 'dve'


class Arena:
    def __init__(self, K, nbytes):
        self.K = K
        self.t = K.stack.enter_context(K.nc.sbuf_tensor("arena", [128, nbytes // 4], F32))
        self.off = 0
        self.limit = nbytes
        self.n = 0

    def at(self, off):
        self.off = off

    def alloc(self, name, shape, dt=F32):
        nfree = int(np.prod(shape[1:]))
        esz = 2 if dt == BF16 else 4
        nb = (nfree * esz + 31) // 32 * 32
        o = self.off
        assert o % 4 == 0 and o + nb <= self.limit, (name, o, nb, self.limit)
        self.off = o + nb
        ap = self.t[0:shape[0], o // 4:(o + nb) // 4]
        if dt != F32:
            ap = ap.bitcast(dt)
        ap = ap[:, 0:nfree]
        if len(shape) == 3:
            ap = ap.rearrange("p (a b) -> p a b", a=shape[1])
        elif len(shape) == 4:
            ap = ap.rearrange("p (a b c) -> p a b c", a=shape[1], b=shape[2])
        self.n += 1
        return Buf(self.t, f"{name}_{self.n}", ap=ap)


def build_program(stage=99):
    nc = bass.Bass("TRN2", target_bir_lowering=False)

    def din(name, shape):
        return nc.dram_tensor(name, list(shape), F32, kind="ExternalInput").ap()

    def dout(name, shape):
        return nc.dram_tensor(name, list(shape), F32, kind="ExternalOutput").ap()

    xseq = din("xseq", [SEQ, D])
    xown = din("xown", [TOWN, D])
    xprev = din("xprev", [128, D])
    latc = din("latc", [2, SEQ, 512])
    kpec = din("kpec", [2, SEQ, 64])
    spool = din("spool", [32, 1024])
    memp = din("memp", [256, D])
    cmk = din("cmk", [2, 256, 1024])
    cmv = din("cmv", [2, 256, 1024])
    w_in = din("w_in", [D, D_IN])
    w_pool = din("w_pool", [4, 256, 512])
    w_qb = din("w_qb", [512, 3072])
    w_qbrot = din("w_qbrot", [512, 1024])
    w_kb = din("w_kb", [512, 2048])
    w_vb = din("w_vb", [512, 2048])
    w_mla_o = din("w_mla_o", [2048, 2048])
    w_mem_kv = din("w_mem_kv", [2048, 2048])
    w_mem_o = din("w_mem_o", [1024, 2048])
    w_out = din("w_out", [2048, 2048])
    w_up = din("w_up", [2048, 8192])
    w_down = din("w_down", [8192, 2048])
    g_mix = din("g_mix", [D])
    b_gate = din("b_gate", [6144])
    pool_scale = din("pool_scale", [D])
    g_q_lat = din("g_q_lat", [512])
    g_q_head = din("g_q_head", [192])
    g_kv_lat = din("g_kv_lat", [512])
    g_k_head = din("g_k_head", [192])
    g_mem = din("g_mem", [D])
    g_mem_q = din("g_mem_q", [256])
    g_mem_k = din("g_mem_k", [256])
    g_ff = din("g_ff", [D])
    ident = din("ident", [128, 128])
    ktab_c = din("ktab_c", [128, NKB, 64])
    ktab_s = din("ktab_s", [128, NKB, 64])
    qtab_c = din("qtab_c", [64, TOWN])
    qtab_s = din("qtab_s", [64, TOWN])
    mcur = din("mcur", [128, 8, 128])
    mprev = din("mprev", [128, 32, 128])
    mcurS = din("mcurS", [64, 4, 64])
    mprevS = din("mprevS", [32, 4, 64])
    kmask = din("kmask", [16, SEQ])
    qmask = din("qmask", [16, 1024])

    y_own = dout("y_own", [TOWN, D])
    lat_own = dout("lat_own", [TOWN, 512])
    kpe_own = dout("kpe_own", [TOWN, 64])
    poolp_out = dout("poolp_out", [15, 1024])
    pools_out = dout("pools_out", [2, 15, 1024])
    memk_out = dout("memk_out", [256, 1024])
    memv_out = dout("memv_out", [256, 1024])

    with ExitStack() as st:
        K = Kern(nc, st)
        A = Arena(K, ARENA_BYTES)
        pst = st.enter_context(nc.psum_tensor("pst", [128, 8, 512], F32))
        PB = [Buf(pst, f"bank{i}", psum=True, ap=pst[:, i, :]) for i in range(8)]

        def bfv(bank, n=128):
            return PB[bank][0:n, :].bitcast(BF16)

        A.at(0)
        ident_b = A.alloc("ident", [128, 128], BF16)
        ones_b = A.alloc("ones", [128, 128], BF16)
        ones_f = A.alloc("onesf", [128, 128])
        eps6 = A.alloc("eps6", [128, 1])
        eps192 = A.alloc("eps192", [128, 1])
        gqk = A.alloc("gqk", [128, 1])
        gq_n = A.alloc("gq_n", [128, 1])
        gk_n = A.alloc("gk_n", [128, 1])
        gq_r = A.alloc("gq_r", [64, 1])
        gq_rp = A.alloc("gq_rp", [64, 1])
        gbc = A.alloc("gbc", [128, D])
        cnew = A.alloc("cnew", [64, 512], BF16)
        kpnew = A.alloc("kpnew", [64, 64])
        sspe_new = A.alloc("sspe_new", [64, 1])
        sq_last = A.alloc("sq_last", [128, 32], BF16)
        ssp = A.alloc("ssp", [128, 9, 8])
        rstd2 = A.alloc("rstd2", [128, 9])
        ssA = A.alloc("ssA", [128, 9])
        assert A.off <= 14336
        A.at(14336)
        memkT_p = A.alloc("memkT_p", [128, 8, 256], BF16)
        memv_p = A.alloc("memv_p", [128, 2, 1024], BF16)
        assert A.off == 22528
        OT_OFF = 22528
        R1 = 57344
        R2 = 92160
        R3 = 142304

        K.dma(ident_b[:], ident, writes=[ident_b], q='pool')
        K.dma(gbc[:], g_mix.partition_broadcast(128), writes=[gbc])
        K.dma(gq_n[:], g_q_head[0:128].rearrange("(p o) -> p o", o=1), writes=[gq_n])
        K.dma(gk_n[:], g_k_head[0:128].rearrange("(p o) -> p o", o=1), writes=[gk_n])
        K.dma(gq_r[:], g_q_head[128:192].rearrange("(p o) -> p o", o=1), writes=[gq_r])
        K.dma(gq_rp[0:32, :], g_q_head[160:192].rearrange("(p o) -> p o", o=1), writes=[gq_rp])
        K.dma(gq_rp[32:64, :], g_q_head[128:160].rearrange("(p o) -> p o", o=1), writes=[gq_rp])
        K.op('dve', lambda e: e.memset(ones_f[:], 1.0), writes=[ones_f])
        K.op('dve', lambda e: e.tensor_copy(out=ones_b[:], in_=ones_f[:]), reads=[ones_f], writes=[ones_b])
        K.op('dve', lambda e: e.memset(eps6[:], EPS), writes=[eps6])
        K.op('dve', lambda e: e.memset(eps192[:], EPS * 192.0), writes=[eps192])
        K.op('dve', lambda e: e.tensor_tensor(out=gqk[:], in0=gq_n[:], in1=gk_n[:], op=ALU.mult),
             reads=[gq_n, gk_n], writes=[gqk])

        def rms_stats(src_ap, n, width, junk_ap, ss, rstd, reads, eps_b=eps6, extra_w=()):
            K.op('act', lambda e: e.activation(out=junk_ap, in_=src_ap, func=AF.Square, accum_out=ss[0:n, :]),
                 reads=reads, writes=[ss] + list(extra_w))
            K.op('act', lambda e: e.activation(out=rstd[0:n, :], in_=ss[0:n, :], func=AF.Sqrt, bias=eps_b[0:n, :],
                                               scale=1.0 / width), reads=[ss, eps_b], writes=[rstd])
            K.op('dve', lambda e: e.reciprocal(out=rstd[0:n, :], in_=rstd[0:n, :]), reads=[rstd], writes=[rstd])

        def transposes(src, src_bufs, n, nch, dst_fn, dst_bufs, banks, cw=128, pb=0, dst_grp=None):
            k = 0
            bi = 0
            while k < nch:
                g = min(4, nch - k)
                bk = banks[bi % len(banks)]
                bi += 1
                pv = bfv(bk, cw)
                for j in range(g):
                    K.op('pe', lambda e, kk=k + j, j=j, pv=pv: e.transpose(
                        out=pv[:, j * 128:j * 128 + n], in_=src[0:n, kk * cw:(kk + 1) * cw], identity=ident_b[pb:pb + n, pb:pb + n]),
                        reads=list(src_bufs) + [ident_b], writes=[PB[bk]], signal=(j == g - 1))
                if dst_grp is not None:
                    K.op(K.ev(), (lambda k0, g, pv: (lambda e: (
                        e.activation(out=dst_grp(k0, g), in_=pv[:, 0:g * 128].rearrange("p (a b) -> p a b", a=g)[:, :, 0:n], func=AF.Copy)
                        if e is nc.scalar else
                        e.tensor_copy(out=dst_grp(k0, g), in_=pv[:, 0:g * 128].rearrange("p (a b) -> p a b", a=g)[:, :, 0:n]))))(k, g, pv),
                        reads=[PB[bk]], writes=dst_bufs)
                else:
                    for j in range(g):
                        K.op(K.ev(), (lambda kk, j, pv: (lambda e: (
                            e.activation(out=dst_fn(kk), in_=pv[:, j * 128:j * 128 + n], func=AF.Copy)
                            if e is nc.scalar else e.tensor_copy(out=dst_fn(kk), in_=pv[:, j * 128:j * 128 + n]))))(k + j, j, pv),
                            reads=[PB[bk]], writes=dst_bufs)
                k += g

        def norm_block(x_dram_rows, n, xb, h, hT, ss, rstd, banks, col0=0):
            K.dma(xb[0:n, :], x_dram_rows, writes=[xb])
            rms_stats(xb[0:n, :], n, D, h[0:n, :], ss, rstd, reads=[xb], extra_w=[h])
            K.op('dve', lambda e: e.scalar_tensor_tensor(out=h[0:n, :], in0=xb[0:n, :], scalar=rstd[0:n, :], in1=gbc[0:n, :],
                                                         op0=ALU.mult, op1=ALU.mult), reads=[xb, rstd, gbc], writes=[h])
            transposes(h, [h], n, 16, lambda k: hT[:, k, col0:col0 + n], [hT], banks,
                       dst_grp=lambda k0, g: hT[:, k0:k0 + g, col0:col0 + n])

        def mm_tm(out_ap, bank, lhs_fn, rhs_fn, nk, reads):
            for k in range(nk):
                K.op('pe', lambda e, k=k: e.matmul(out_ap, lhsT=lhs_fn(k), rhs=rhs_fn(k), start=(k == 0), stop=(k == nk - 1)),
                     reads=reads, writes=[PB[bank]], signal=(k == nk - 1))

        A.at(OT_OFF)
        gmem_bc = A.alloc("gmem_bc", [128, D])
        gmk_bc = A.alloc("gmk_bc", [128, 256])
        xb0 = A.alloc("xb0", [128, D])
        h0 = A.alloc("h0", [128, D], BF16)
        memhT = A.alloc("memhT", [128, 16, 256], BF16)
        wt = [A.alloc(f"wmkv{i}", [128, 16, 512], BF16) for i in range(2)]
        mkf = A.alloc("mkf", [128, 2, 1024])
        mvf = A.alloc("mvf", [128, 2, 1024])
        mkb = A.alloc("mkb", [128, 2, 1024], BF16)
        sq4 = A.alloc("sq4", [128, 1024])
        ss4 = A.alloc("ss4", [128, 4])
        r4 = A.alloc("r4", [128, 4])
        ssm = A.alloc("ssm", [128, 1])
        rsm = A.alloc("rsm", [128, 1])
        K.dma(gmem_bc[:], g_mem.partition_broadcast(128), writes=[gmem_bc])
        K.dma(gmk_bc[:], g_mem_k.partition_broadcast(128), writes=[gmk_bc])
        for mb in range(2):
            K.dma(xb0[:], memp[mb * 128:(mb + 1) * 128, :], writes=[xb0])
            rms_stats(xb0[:], 128, D, h0[:], ssm, rsm, reads=[xb0], extra_w=[h0])
            K.op('dve', lambda e: e.scalar_tensor_tensor(out=h0[:], in0=xb0[:], scalar=rsm[:], in1=gmem_bc[:],
                                                         op0=ALU.mult, op1=ALU.mult), reads=[xb0, rsm, gmem_bc], writes=[h0])
            transposes(h0, [h0], 128, 16, lambda k, mb=mb: memhT[:, k, mb * 128:(mb + 1) * 128], [memhT], [0, 1])
        for ct in range(4):
            wtile = wt[ct % 2]
            K.dma_k(wtile, lambda a, b, wtile=wtile: wtile[:, a:b, :], w_mem_kv[:, ct * 512:(ct + 1) * 512], 16, 4)
            for mb in range(2):
                bank = 2 + (ct * 2 + mb) % 4
                mm_tm(PB[bank][:, :], bank, lambda k, mb=mb: memhT[:, k, mb * 128:(mb + 1) * 128],
                      lambda k, wtile=wtile: wtile[:, k, :], 16, [memhT, wtile])
                dstf = mkf if ct < 2 else mvf
                cc = (ct % 2) * 512
                K.op(K.ev(), (lambda bank, dstf, mb, cc: (lambda e: (
                    e.activation(out=dstf[:, mb, cc:cc + 512], in_=PB[bank][:, :], func=AF.Copy) if e is nc.scalar
                    else e.tensor_copy(out=dstf[:, mb, cc:cc + 512], in_=PB[bank][:, :]))))(bank, dstf, mb, cc),
                    reads=[PB[bank]], writes=[dstf])
        for mb in range(2):
            K.op('dve', lambda e, mb=mb: e.tensor_tensor(out=sq4[:], in0=mkf[:, mb, :], in1=mkf[:, mb, :], op=ALU.mult),
                 reads=[mkf], writes=[sq4])
            K.op('dve', lambda e: e.tensor_reduce(out=ss4[:], in_=sq4[:].rearrange("p (h d) -> p h d", h=4), axis=AX.X, op=ALU.add),
                 reads=[sq4], writes=[ss4])
            K.op('act', lambda e: e.activation(out=r4[:], in_=ss4[:], func=AF.Sqrt, bias=eps6[:], scale=1.0 / 256),
                 reads=[ss4, eps6], writes=[r4])
            K.op('dve', lambda e: e.reciprocal(out=r4[:], in_=r4[:]), reads=[r4], writes=[r4])
            for hh in range(4):
                K.op('dve', lambda e, mb=mb, hh=hh: e.scalar_tensor_tensor(
                    out=mkf[:, mb, hh * 256:(hh + 1) * 256], in0=mkf[:, mb, hh * 256:(hh + 1) * 256], scalar=r4[:, hh:hh + 1],
                    in1=gmk_bc[:], op0=ALU.mult, op1=ALU.mult), reads=[mkf, r4, gmk_bc], writes=[mkf])
            K.op('act', lambda e, mb=mb: e.activation(out=mkb[:, mb, :], in_=mkf[:, mb, :], func=AF.Copy), reads=[mkf], writes=[mkb])
            K.op('dve', lambda e, mb=mb: e.tensor_copy(out=memv_p[:, mb, :], in_=mvf[:, mb, :]), reads=[mvf], writes=[memv_p])
            K.dma(memk_out[mb * 128:(mb + 1) * 128, :], mkf[:, mb, :], reads=[mkf], final=True)
            K.dma(memv_out[mb * 128:(mb + 1) * 128, :], mvf[:, mb, :], reads=[mvf], final=True)

        def build_memkT(src_b, dst):
            for mb in range(2):
                transposes(src_b[:, mb, :], [src_b], 128, 8,
                           lambda c, mb=mb: dst[:, c, mb * 128:(mb + 1) * 128], [dst], [0, 1])

        build_memkT(mkb, memkT_p)
        K.barrier()
        if stage <= 0:
            K.finish()
            return nc

        A.at(OT_OFF)
        oT_all = A.alloc("oT_all", [128, 16, TOWN], BF16)
        assert A.off == R1
        Wa = A.alloc("Wa", [128, 16, 1088], BF16)
        assert A.off == R2
        ckvT = A.alloc("ckvT", [128, 4, 4128], BF16)
        kropeT = A.alloc("kropeT", [128, 4128], BF16)
        sspe = A.alloc("sspe", [128, NKB])
        qlatT = A.alloc("qlatT", [128, 4, TOWN], BF16)
        assert A.off <= R3, A.off
        A.at(R3)
        ktc = A.alloc("ktc", [128, NKB, 64])
        kts = A.alloc("kts", [128, NKB, 64])
        gkr_bc = A.alloc("gkr_bc", [128, 64])
        CH = [dict(), dict(), dict()]
        for c in range(3):
            CH[c]['kpg'] = A.alloc("kpg", [128, 64])
            CH[c]['kt1'] = A.alloc("kt1", [128, 64])
            CH[c]['kt2'] = A.alloc("kt2", [128, 64])
            CH[c]['krb'] = A.alloc("krb", [128, 64], BF16)
            CH[c]['bT'], CH[c]['bL'], CH[c]['bS'], CH[c]['bC'] = [(0, 1, 2, 2), (3, 4, 5, 5), (6, 7, 6, 6)][c]
        PBOFF = A.off
        for c in range(3):
            if c == 2:
                sv_off = A.off
                A.at(OT_OFF)
            CH[c]['xb'] = A.alloc("xb", [128, D])
            CH[c]['hb'] = A.alloc("hb", [128, D], BF16)
            CH[c]['hT'] = A.alloc("hT", [128, 16, 128], BF16)
            CH[c]['cf'] = A.alloc("cf", [128, 512])
            CH[c]['cb'] = A.alloc("cb", [128, 512], BF16)
            CH[c]['qlb'] = A.alloc("qlb", [128, 512], BF16)
            CH[c]['kpf'] = A.alloc("kpf", [128, 64])
            CH[c]['ss'] = [A.alloc("st_ss", [128, 1]) for i in range(3)]
            CH[c]['r'] = [A.alloc("st_r", [128, 1]) for i in range(3)]
            if c == 2:
                assert A.off <= R1
                A.at(sv_off)
        gkv_bc = A.alloc("gkv_bc", [128, 512])
        gql_bc = A.alloc("gql_bc", [128, 512])
        hscr_t = nc.dram_tensor("hT_scr", [128, 16, TOWN + 128], BF16, kind="Internal").ap()
        hscr = Buf(None, "hscr")
        K.op('dve', lambda e: e.memset(sspe[:], 1.0), writes=[sspe])
        K.op('dve', lambda e: e.memset(kropeT[64:128, :], 0.0), writes=[kropeT])
        K.dma(kropeT[64:80, 0:SEQ], kmask, writes=[kropeT], q='pool')
        K.dma_k(Wa, lambda a, b: Wa[:, a:b, 512:1088], w_in[:, OFF_KV:OFF_KV + 576], 16, 4)
        K.dma_k(Wa, lambda a, b: Wa[:, a:b, 0:512], w_in[:, OFF_QL:OFF_QL + 512], 16, 4)
        K.dma(ktc[:], ktab_c, writes=[ktc])
        K.dma(kts[:], ktab_s, writes=[kts])
        K.dma(gkv_bc[:], g_kv_lat.partition_broadcast(128), writes=[gkv_bc])
        K.dma(gql_bc[:], g_q_lat.partition_broadcast(128), writes=[gql_bc])
        K.dma(gkr_bc[:], g_k_head[128:192].partition_broadcast(128), writes=[gkr_bc])

        def key_side(ch, c_src_ap, c_reads, kpe_src_ap, kpe_reads, n, blk, col0, sspe_dst_ap, sspe_dst_buf, pb=0):
            P = slice(pb, pb + n)
            kpg, kt1, kt2, krb, bS = ch['kpg'], ch['kt1'], ch['kt2'], ch['krb'], ch['bS']
            transposes(c_src_ap, c_reads, n, 4, lambda k: ckvT[:, k, col0:col0 + n], [ckvT], [ch['bC']], pb=pb,
                       dst_grp=lambda k0, g: ckvT[:, k0:k0 + g, col0:col0 + n])
            K.op('act', lambda e: e.activation(out=kt1[P, :], in_=kpe_src_ap, func=AF.Square, accum_out=sspe_dst_ap),
                 reads=kpe_reads, writes=[kt1, sspe_dst_buf])
            K.op('dve', lambda e: e.tensor_tensor(out=kpg[P, :], in0=kpe_src_ap, in1=gkr_bc[P, :], op=ALU.mult),
                 reads=kpe_reads + [gkr_bc], writes=[kpg])
            K.op('dve', lambda e: e.tensor_tensor(out=kt1[P, :], in0=kpg[P, :], in1=ktc[P, blk, :], op=ALU.mult),
                 reads=[kpg, ktc], writes=[kt1])
            K.op('dve', lambda e: e.tensor_tensor(out=kt2[P, 0:32], in0=kpg[P, 32:64], in1=kts[P, blk, 0:32], op=ALU.mult),
                 reads=[kpg, kts], writes=[kt2])
            K.op('dve', lambda e: e.tensor_tensor(out=kt2[P, 32:64], in0=kpg[P, 0:32], in1=kts[P, blk, 32:64], op=ALU.mult),
                 reads=[kpg, kts], writes=[kt2])
            K.op('dve', lambda e: e.tensor_tensor(out=krb[P, :], in0=kt1[P, :], in1=kt2[P, :], op=ALU.add),
                 reads=[kt1, kt2], writes=[krb])
            K.op('pe', lambda e: e.transpose(out=bfv(bS, 64)[:, 512:512 + n], in_=krb[P, :], identity=ident_b[P, P]),
                 reads=[krb, ident_b], writes=[PB[bS]])
            K.op('act', lambda e: e.activation(out=kropeT[0:64, col0:col0 + n], in_=bfv(bS, 64)[:, 512:512 + n], func=AF.Copy),
                 reads=[PB[bS]], writes=[kropeT])

        def latents(ch, n):
            hT, bL, bS, cf, cb, kpf = ch['hT'], ch['bL'], ch['bS'], ch['cf'], ch['cb'], ch['kpf']
            mm_tm(PB[bL][0:n, :], bL, lambda k: hT[:, k, 0:n], lambda k: Wa[:, k, 512:1024], 16, [hT, Wa])
            mm_tm(PB[bS][0:n, 0:64], bS, lambda k: hT[:, k, 0:n], lambda k: Wa[:, k, 1024:1088], 16, [hT, Wa])
            rms_stats(PB[bL][0:n, :], n, 512, cb[0:n, :], ch['ss'][1], ch['r'][1], reads=[PB[bL]], extra_w=[cb])
            K.op('dve', lambda e: e.scalar_tensor_tensor(out=cf[0:n, :], in0=PB[bL][0:n, :], scalar=ch['r'][1][0:n, :], in1=gkv_bc[0:n, :],
                                                         op0=ALU.mult, op1=ALU.mult), reads=[PB[bL], ch['r'][1], gkv_bc], writes=[cf])
            K.op('act', lambda e: e.activation(out=cb[0:n, :], in_=cf[0:n, :], func=AF.Copy), reads=[cf], writes=[cb])
            K.op('dve', lambda e: e.tensor_copy(out=kpf[0:n, :], in_=PB[bS][0:n, 0:64]), reads=[PB[bS]], writes=[kpf])

        blks = [(xseq[blk * 128:(blk + 1) * 128, :], 128) for blk in range(NB)]
        blks += [(xown[r0:r0 + n, :], n) for (r0, n) in BLOCKS]
        blks += [(xprev[:, :], 128)]
        NBLK = len(blks)
        normed = set()

        def load_x(i):
            rows, n = blks[i]
            xb = CH[i % 3]['xb']
            K.dma(xb[0:n, :], rows, writes=[xb])

        def norm_a(i):
            rows, n = blks[i]
            ch = CH[i % 3]
            xb, h = ch['xb'], ch['hb']
            rms_stats(xb[0:n, :], n, D, h[0:n, :], ch['ss'][0], ch['r'][0], reads=[xb], extra_w=[h])
            K.op('dve', lambda e: e.scalar_tensor_tensor(out=h[0:n, :], in0=xb[0:n, :], scalar=ch['r'][0][0:n, :], in1=gbc[0:n, :],
                                                         op0=ALU.mult, op1=ALU.mult), reads=[xb, ch['r'][0], gbc], writes=[h])
            K.py(lambda: normed.add(i))
            if i + 3 < NBLK:
                load_x(i + 3)

        def head(i):
            rows, n = blks[i]
            ch = CH[i % 3]
            hT = ch['hT']

            def chk():
                assert i in normed, i
            K.py(chk)
            transposes(ch['hb'], [ch['hb']], n, 16, lambda k: hT[:, k, 0:n], [hT], [ch['bT']],
                       dst_grp=lambda k0, g: hT[:, k0:k0 + g, 0:n])
            if i + 3 < NBLK:
                norm_a(i + 3)

        for i in range(3):
            load_x(i)
        for i in range(3):
            norm_a(i)
        recs = []
        for blk in range(NB):
            ch = CH[len(recs) % 3]
            K.record()
            head(len(recs))
            latents(ch, 128)
            key_side(ch, ch['cb'][0:128, :], [ch['cb']], ch['kpf'][0:128, :], [ch['kpf']], 128, blk, blk * 128, sspe[:, blk:blk + 1], sspe)
            recs.append(K.stop_record())

        for bi, (r0, n) in enumerate(BLOCKS):
            ch = CH[len(recs) % 3]
            hT, cf, cb, kpf, qlb, bT, bC = ch['hT'], ch['cf'], ch['cb'], ch['kpf'], ch['qlb'], ch['bT'], ch['bC']
            K.record()
            head(len(recs))
            K.dma(hscr_t[:, :, r0:r0 + n], hT[:, :, 0:n], reads=[hT], writes=[hscr])
            latents(ch, n)
            mm_tm(PB[bT][0:n, :], bT, lambda k, hT=hT, n=n: hT[:, k, 0:n], lambda k: Wa[:, k, 0:512], 16, [hT, Wa])
            K.dma(lat_own[r0:r0 + n, :], cf[0:n, :], reads=[cf], final=True)
            K.dma(kpe_own[r0:r0 + n, :], kpf[0:n, :], reads=[kpf], final=True)
            if bi == 8:
                K.op('dve', lambda e, cb=cb: e.tensor_copy(out=cnew[:, :], in_=cb[0:64, :]), reads=[cb], writes=[cnew])
                K.op('dve', lambda e, kpf=kpf: e.tensor_copy(out=kpnew[:, :], in_=kpf[0:64, :]), reads=[kpf], writes=[kpnew])
            rms_stats(PB[bT][0:n, :], n, 512, qlb[0:n, :], ch['ss'][2], ch['r'][2], reads=[PB[bT]], extra_w=[qlb])
            K.op('dve', lambda e, ch=ch, qlb=qlb, bT=bT, n=n: e.scalar_tensor_tensor(
                out=qlb[0:n, :], in0=PB[bT][0:n, :], scalar=ch['r'][2][0:n, :], in1=gql_bc[0:n, :],
                op0=ALU.mult, op1=ALU.mult), reads=[PB[bT], ch['r'][2], gql_bc], writes=[qlb])
            transposes(qlb, [qlb], n, 4, lambda k, r0=r0, n=n: qlatT[:, k, r0:r0 + n], [qlatT], [bC],
                       dst_grp=lambda k0, g, r0=r0, n=n: qlatT[:, k0:k0 + g, r0:r0 + n])
            recs.append(K.stop_record())
        ch = CH[len(recs) % 3]
        K.record()
        head(len(recs))
        K.dma(hscr_t[:, :, TOWN:TOWN + 128], ch['hT'][:, :, :], reads=[ch['hT']], writes=[hscr])
        recs.append(K.stop_record())
        assert len(recs) == NBLK
        K.replay(recs, width=3, stagger=0.33)
        K.barrier()
        if stage <= 1:
            K.finish()
            return nc

        A.at(R1)
        V4 = A.alloc("V4", [128, NKB, 512], BF16)
        A.at(PBOFF)
        knT = A.alloc("knT", [128, 4128], BF16)
        qnT = A.alloc("qnT", [128, TOWN], BF16)
        qrT = A.alloc("qrT", [128, TOWN], BF16)
        Tc = A.alloc("Tc", [64, TOWN])
        Ts = A.alloc("Ts", [64, TOWN])
        pTs = [A.alloc(f"pT{i}", [128, 512], BF16) for i in range(3)]
        sqall = Buf(A.t, "sqall", ap=gbc._ap.bitcast(BF16))
        sqn = A.alloc("sqn", [128, 512], BF16)
        sqr = A.alloc("sqr", [64, 512], BF16)
        off_scr = A.off
        rrep = A.alloc("rrep", [128, 512])
        rrec = A.alloc("rrec", [128, 512])
        t1 = A.alloc("t1", [64, 512])
        t2 = A.alloc("t2", [64, 512])
        assert A.off == off_scr + 8192
        lat4 = [Buf(A.t, f"lat4_{i}", ap=A.t[0:128, (off_scr + i * 4096) // 4:(off_scr + (i + 1) * 4096) // 4].bitcast(BF16)
                    .rearrange("p (a b) -> p a b", a=4)) for i in range(2)]
        kpg4, kt1_4, kt2_4 = [Buf(A.t, f"ks4_{i}", ap=pTs[i]._ap.bitcast(F32).rearrange("p (a b) -> p a b", a=4)) for i in range(3)]
        rks = A.alloc("rks", [128, NKB])
        rkt = A.alloc("rkt", [128, NKB])
        KSPL = 24
        rksP = [Buf(A.t, f"rks{i}", ap=rks._ap) for i in range(2)]
        rktP = [Buf(A.t, f"rkt{i}", ap=rkt._ap) for i in range(2)]
        sqP = [Buf(A.t, f"sqP{i}", ap=sqall._ap) for i in range(2)]
        recip = A.alloc("recip", [128, 512])
        recips = [recip, rrep]
        wq_h = A.alloc("wq_h", [128, 4, 192], BF16)
        wrot_h = A.alloc("wrot_h", [128, 4, 64], BF16)
        wk_h = A.alloc("wk_h", [128, 4, 128], BF16)
        wv_g = A.alloc("wv_g", [128, 4, 512], BF16)
        kp4 = [A.alloc(f"kp4_{i}", [128, 4, 64]) for i in range(2)]
        krb4 = [A.alloc(f"krb4_{i}", [128, 4, 64], BF16) for i in range(2)]
        K.op('dve', lambda e: e.memset(qrT[64:128, :], 0.0), writes=[qrT])
        K.dma(qrT[64:80, 0:1024], qmask, writes=[qrT], q='pool')
        K.dma(Tc[:], qtab_c, writes=[Tc])
        K.dma(Ts[:], qtab_s, writes=[Ts])
        K.op('dve', lambda e: e.tensor_scalar_mul(out=Tc[:], in0=Tc[:], scalar1=gq_r[:, 0:1]), reads=[Tc, gq_r], writes=[Tc])
        K.op('dve', lambda e: e.tensor_scalar_mul(out=Ts[:], in0=Ts[:], scalar1=gq_rp[:, 0:1]), reads=[Ts, gq_rp], writes=[Ts])

        pT_i = [0]
        sc_i = [0]

        def attention_head(h, prob):
            hh = h % 4
            if prob == 0:
                nkb, nkeys = NB, SEQ
                qchunks = [(0, 512), (512, 512)]
            else:
                nkb, nkeys = NKB, 4128
                qchunks = [(1024 + 32 * (prob - 1), 32)]
            def v_build():
                for blk in range(nkb):
                    n = min(128, nkeys - blk * 128)
                    bank = 5 + blk % 2
                    mm_tm(PB[bank][0:n, :], bank, lambda k, blk=blk, n=n: ckvT[:, k, blk * 128:blk * 128 + n],
                          lambda k: wv_g[:, k, :], 4, [ckvT, wv_g])
                    K.op(K.ev(), (lambda bank, blk, n: (lambda e: (
                        e.activation(out=V4[0:n, blk, :], in_=PB[bank][0:n, :], func=AF.Copy) if e is nc.scalar
                        else e.tensor_copy(out=V4[0:n, blk, :], in_=PB[bank][0:n, :]))))(bank, blk, n), reads=[PB[bank]], writes=[V4])
                g2 = (h // 4 + 1) % 4
                K.dma(wv_g[:], w_vb[:, g2 * 512:(g2 + 1) * 512].rearrange("(k p) n -> p k n", p=128), writes=[wv_g], q='pool')

            def q_mm(c0, w):
                mm_tm(PB[0][:, 0:w], 0, lambda k: wq_h[:, k, 0:128], lambda k: qlatT[:, k, c0:c0 + w], 4, [wq_h, qlatT])
                mm_tm(PB[1][0:64, 0:w], 1, lambda k: wq_h[:, k, 128:192], lambda k: qlatT[:, k, c0:c0 + w], 4, [wq_h, qlatT])
                mm_tm(PB[2][0:64, 0:w], 2, lambda k: wrot_h[:, k, :], lambda k: qlatT[:, k, c0:c0 + w], 4, [wrot_h, qlatT])
                K.op('act', lambda e: e.activation(out=sqn[:, 0:w], in_=PB[0][:, 0:w], func=AF.Square), reads=[PB[0]], writes=[sqn])
                K.op('act', lambda e: e.activation(out=sqr[:, 0:w], in_=PB[1][0:64, 0:w], func=AF.Square), reads=[PB[1]], writes=[sqr])

            def q_fin(c0, w):
                K.op('pe', lambda e: e.matmul(PB[3][:, 0:w], lhsT=ones_b[:, :], rhs=sqn[:, 0:w], start=True, stop=False),
                     reads=[ones_b, sqn], writes=[PB[3]], signal=False)
                K.op('pe', lambda e: e.matmul(PB[3][:, 0:w], lhsT=ones_b[0:64, :], rhs=sqr[:, 0:w], start=False, stop=True),
                     reads=[ones_b, sqr], writes=[PB[3]])
                K.op('act', lambda e: e.activation(out=rrep[:, 0:w], in_=PB[3][:, 0:w], func=AF.Ln, bias=eps6[:], scale=1.0 / 192),
                     reads=[PB[3], eps6], writes=[rrep])
                K.op('act', lambda e: e.activation(out=rrec[:, 0:w], in_=rrep[:, 0:w], func=AF.Exp, scale=-0.5), reads=[rrep], writes=[rrec])
                K.op('dve', lambda e: e.scalar_tensor_tensor(out=qnT[:, c0:c0 + w], in0=PB[0][:, 0:w], scalar=gqk[:, 0:1],
                                                             in1=rrec[:, 0:w], op0=ALU.mult, op1=ALU.mult),
                     reads=[PB[0], gqk, rrec], writes=[qnT])
                K.op('dve', lambda e: e.tensor_tensor(out=t1[:, 0:w], in0=PB[1][0:64, 0:w], in1=Tc[:, c0:c0 + w], op=ALU.mult),
                     reads=[PB[1], Tc], writes=[t1])
                K.op('dve', lambda e: e.tensor_tensor(out=t2[:, 0:w], in0=PB[2][0:64, 0:w], in1=Ts[:, c0:c0 + w], op=ALU.mult),
                     reads=[PB[2], Ts], writes=[t2])
                K.op('dve', lambda e: e.tensor_tensor(out=t1[:, 0:w], in0=t1[:, 0:w], in1=t2[:, 0:w], op=ALU.add),
                     reads=[t1, t2], writes=[t1])
                K.op('dve', lambda e: e.tensor_tensor(out=qrT[0:64, c0:c0 + w], in0=t1[:, 0:w], in1=rrec[0:64, 0:w], op=ALU.mult),
                     reads=[t1, rrec], writes=[qrT])

            nkc = (nkeys + 511) // 512

            KBANKS = [4, 5, 6]

            def k_mm(kc):
                c0 = kc * 512
                w = min(512, nkeys - c0)
                bank = KBANKS[kc % 3]
                mm_tm(PB[bank][:, 0:w], bank, lambda k: wk_h[:, k, :], lambda k: ckvT[:, k, c0:c0 + w], 4, [wk_h, ckvT])

            def k_fin(kc):
                c0 = kc * 512
                w = min(512, nkeys - c0)
                bank = KBANKS[kc % 3]
                sqb, sqd = (sqP[0 if kc < KSPL // 4 else 1], sqall[:, c0:c0 + w]) if kc < 8 else (sq_last, sq_last[:, 0:w])
                K.op('act', lambda e: e.activation(out=sqd, in_=PB[bank][:, 0:w], func=AF.Square), reads=[PB[bank]], writes=[sqb])
                K.op('dve', lambda e: e.tensor_copy(out=knT[:, c0:c0 + w], in_=PB[bank][:, 0:w]), reads=[PB[bank]], writes=[knT])

            def k_ss(b_lo, b_hi):
                for blk in range(b_lo, b_hi):
                    nb_ = min(128, nkeys - blk * 128)
                    sqb, src = (sqP[0 if blk < KSPL else 1], sqall[:, blk * 128:blk * 128 + nb_]) if blk < 32 else (sq_last, sq_last[:, 0:nb_])
                    K.op('pe', lambda e, nb_=nb_, blk=blk, src=src: e.matmul(
                        PB[7][0:nb_, blk:blk + 1], lhsT=src, rhs=ones_b[:, 0:1], start=True, stop=True),
                        reads=[sqb, ones_b], writes=[PB[7]], signal=(blk == b_hi - 1))

            def rk_chain(pr, c_lo, c_hi, part):
                rkt_, rks_ = rktP[part], rksP[part]
                K.op('dve', lambda e: e.tensor_tensor(
                    out=rkt[0:pr, c_lo:c_hi], in0=PB[7][0:pr, c_lo:c_hi], in1=sspe[0:pr, c_lo:c_hi], op=ALU.add),
                    reads=[PB[7], sspe], writes=[rkt_])
                K.op('act', lambda e: e.activation(
                    out=rkt[0:pr, c_lo:c_hi], in_=rkt[0:pr, c_lo:c_hi], func=AF.Ln, bias=eps192[0:pr, :], scale=1.0),
                    reads=[rkt_, eps192], writes=[rkt_])
                K.op('act', lambda e: e.activation(
                    out=rks[0:pr, c_lo:c_hi], in_=rkt[0:pr, c_lo:c_hi], func=AF.Exp, scale=-0.5), reads=[rkt_], writes=[rks_])

            def k_seq(a, b):
                if a >= b:
                    return
                k_mm(a)
                for kc in range(a, b):
                    if kc + 1 < b:
                        k_mm(kc + 1)
                    k_fin(kc)

            k_seq(0, 3)
            q_mm(*qchunks[0])
            k_seq(3, 6)
            q_fin(*qchunks[0])
            k_seq(6, nkc)
            k_ss(0, KSPL)
            k_ss(KSPL, nkb)
            rk_chain(128, 0, KSPL, 0)
            rk_chain(128, KSPL, NB, 1)
            if nkb == NKB:
                rk_chain(32, NB, NKB, 1)
            if hh == 0:
                v_build()
            for qc in qchunks[1:]:
                q_mm(*qc)
                q_fin(*qc)
            h2 = (h + 1) % 16
            K.dma(wk_h[:], w_kb[:, h2 * 128:(h2 + 1) * 128].rearrange("(k p) n -> p k n", p=128), writes=[wk_h], q='pool')
            K.dma(wq_h[:], w_qb[:, h2 * 192:(h2 + 1) * 192].rearrange("(k p) n -> p k n", p=128), writes=[wq_h], q='pool')
            K.dma(wrot_h[:], w_qbrot[:, h2 * 64:(h2 + 1) * 64].rearrange("(k p) n -> p k n", p=128), writes=[wrot_h], q='pool')
            DEPTH = 2
            scbanks = [4, 7, 6]
            if prob == 0:
                last_kb = [15, 31]
                pairs = []
                for kb in range(NB):
                    s0 = kb // 4
                    for ci in range(2):
                        lo = max(s0 * 128, ci * 512)
                        hi = (ci + 1) * 512
                        if lo >= hi:
                            continue
                        pairs.append((kb, ci, lo, hi, lo - ci * 512, (ci * 512 <= s0 * 128 < hi), s0))

                def a_scores(i):
                    kb, ci, lo, hi, lr, diag, s0 = pairs[i]
                    scb = scbanks[i % 3]
                    K.op('pe', lambda e: e.matmul(PB[scb][:, lr:512], lhsT=knT[:, kb * 128:(kb + 1) * 128], rhs=qnT[:, lo:hi],
                                                  start=True, stop=False), reads=[knT, qnT], writes=[PB[scb]], signal=False)
                    K.op('pe', lambda e: e.matmul(PB[scb][:, lr:512], lhsT=kropeT[:, kb * 128:(kb + 1) * 128], rhs=qrT[:, lo:hi],
                                                  start=False, stop=True), reads=[kropeT, qrT], writes=[PB[scb]])

                def a_rest(i):
                    kb, ci, lo, hi, lr, diag, s0 = pairs[i]
                    scb = scbanks[i % 3]
                    pT = pTs[i % 3]
                    K.op('act', lambda e: e.activation(out=pT[:, lr:512], in_=PB[scb][:, lr:512], func=AF.Exp, scale=rks[:, kb:kb + 1]),
                         reads=[PB[scb], rksP[0 if kb < KSPL else 1]], writes=[pT])
                    K.op('pe', lambda e: e.matmul(PB[ci][:, lr:512], lhsT=V4[:, kb, hh * 128:(hh + 1) * 128], rhs=pT[:, lr:512],
                                                  start=(kb == 0), stop=(kb == last_kb[ci])), reads=[V4, pT], writes=[PB[ci]], signal=False)
                    K.op('pe', lambda e: e.matmul(PB[2 + ci][:, lr:512], lhsT=ones_b[:, :], rhs=pT[:, lr:512],
                                                  start=(kb == 0), stop=(kb == last_kb[ci])), reads=[ones_b, pT], writes=[PB[2 + ci]])

                npairs = len(pairs)
                for i in range(min(DEPTH, npairs)):
                    a_scores(i)
                for i in range(npairs):
                    if i + DEPTH < npairs:
                        a_scores(i + DEPTH)
                    a_rest(i)
                for ci in range(2):
                    rb_ = recips[ci]
                    K.op('act', lambda e, ci=ci, rb_=rb_: e.activation(out=rb_[:, :], in_=PB[2 + ci][:, :], func=AF.Ln), reads=[PB[2 + ci]], writes=[rb_])
                    K.op('act', lambda e, rb_=rb_: e.activation(out=rb_[:, :], in_=rb_[:, :], func=AF.Exp, scale=-1.0), reads=[rb_], writes=[rb_])
                    K.op('dve', lambda e, ci=ci, rb_=rb_: e.tensor_tensor(out=oT_all[:, h, ci * 512:(ci + 1) * 512], in0=PB[ci][:, :], in1=rb_[:, :],
                                                                 op=ALU.mult), reads=[PB[ci], rb_], writes=[oT_all])
            else:
                c0 = 1024 + 32 * (prob - 1)

                def s_scores(kb):
                    nk = min(128, 4128 - kb * 128)
                    scb = scbanks[kb % 3]
                    K.op('pe', lambda e: e.matmul(PB[scb][0:nk, 0:32], lhsT=knT[:, kb * 128:kb * 128 + nk], rhs=qnT[:, c0:c0 + 32],
                                                  start=True, stop=False), reads=[knT, qnT], writes=[PB[scb]], signal=False)
                    K.op('pe', lambda e: e.matmul(PB[scb][0:nk, 0:32], lhsT=kropeT[:, kb * 128:kb * 128 + nk], rhs=qrT[:, c0:c0 + 32],
                                                  start=False, stop=True), reads=[kropeT, qrT], writes=[PB[scb]])

                def s_rest(kb):
                    nk = min(128, 4128 - kb * 128)
                    scb = scbanks[kb % 3]
                    pT = pTs[kb % 3]
                    K.op('act', lambda e: e.activation(out=pT[0:nk, 0:32], in_=PB[scb][0:nk, 0:32], func=AF.Exp, scale=rks[0:nk, kb:kb + 1]),
                         reads=[PB[scb], rksP[0 if kb < KSPL else 1]], writes=[pT])
                    K.op('pe', lambda e: e.matmul(PB[0][:, 0:32], lhsT=V4[0:nk, kb, hh * 128:(hh + 1) * 128], rhs=pT[0:nk, 0:32],
                                                  start=(kb == 0), stop=(kb == NKB - 1)), reads=[V4, pT], writes=[PB[0]], signal=False)
                    K.op('pe', lambda e: e.matmul(PB[2][:, 0:32], lhsT=ones_b[0:nk, :], rhs=pT[0:nk, 0:32],
                                                  start=(kb == 0), stop=(kb == NKB - 1)), reads=[ones_b, pT], writes=[PB[2]])

                for kb in range(DEPTH):
                    s_scores(kb)
                for kb in range(NKB):
                    if kb + DEPTH < NKB:
                        s_scores(kb + DEPTH)
                    s_rest(kb)
                K.op('act', lambda e: e.activation(out=recip[:, 0:32], in_=PB[2][:, 0:32], func=AF.Ln), reads=[PB[2]], writes=[recip])
                K.op('act', lambda e: e.activation(out=recip[:, 0:32], in_=recip[:, 0:32], func=AF.Exp, scale=-1.0), reads=[recip], writes=[recip])
                K.op('dve', lambda e: e.tensor_tensor(out=oT_all[:, h, c0:c0 + 32], in0=PB[0][:, 0:32], in1=recip[:, 0:32], op=ALU.mult),
                     reads=[PB[0], recip], writes=[oT_all])

        def key_side4a(bi, s4):
            c0 = s4 * 512
            b0 = s4 * 4
            lt, kp, kr = lat4[s4 % 2], kp4[s4 % 2], krb4[s4 % 2]
            K.dma(lt[:], latc[bi, c0:c0 + 512, :].rearrange("(a p) n -> p a n", p=128), writes=[lt], q='pool')
            K.dma(kp[:], kpec[bi, c0:c0 + 512, :].rearrange("(a p) n -> p a n", p=128), writes=[kp])
            for k in range(4):
                pv = bfv(k)
                for a in range(4):
                    K.op('pe', lambda e, k=k, a=a, pv=pv: e.transpose(out=pv[:, a * 128:(a + 1) * 128], in_=lt[:, a, k * 128:(k + 1) * 128],
                                                                     identity=ident_b[:, :]),
                         reads=[lt, ident_b], writes=[PB[k]], signal=(a == 3))
                K.op(K.ev(), (lambda k, pv: (lambda e: (
                    e.activation(out=ckvT[:, k, c0:c0 + 512], in_=pv[:, 0:512], func=AF.Copy) if e is nc.scalar
                    else e.tensor_copy(out=ckvT[:, k, c0:c0 + 512], in_=pv[:, 0:512]))))(k, pv), reads=[PB[k]], writes=[ckvT])
            for a in range(4):
                K.op('act', lambda e, a=a: e.activation(out=kt2_4[:, a, :], in_=kp[:, a, :], func=AF.Square, accum_out=sspe[:, b0 + a:b0 + a + 1]),
                     reads=[kp], writes=[kt2_4, sspe])
            for a in range(4):
                K.op('dve', lambda e, a=a: e.tensor_tensor(out=kpg4[:, a, :], in0=kp[:, a, :], in1=gkr_bc[:, :], op=ALU.mult),
                     reads=[kp, gkr_bc], writes=[kpg4])
            K.op('dve', lambda e: e.tensor_tensor(out=kt1_4[:, :, :], in0=kpg4[:, :, :], in1=ktc[:, b0:b0 + 4, :], op=ALU.mult),
                 reads=[kpg4, ktc], writes=[kt1_4])
            K.op('dve', lambda e: e.tensor_tensor(out=kt2_4[:, :, 0:32], in0=kpg4[:, :, 32:64], in1=kts[:, b0:b0 + 4, 0:32], op=ALU.mult),
                 reads=[kpg4, kts], writes=[kt2_4])
            K.op('dve', lambda e: e.tensor_tensor(out=kt2_4[:, :, 32:64], in0=kpg4[:, :, 0:32], in1=kts[:, b0:b0 + 4, 32:64], op=ALU.mult),
                 reads=[kpg4, kts], writes=[kt2_4])
            K.op('dve', lambda e: e.tensor_tensor(out=kr[:, :, :], in0=kt1_4[:, :, :], in1=kt2_4[:, :, :], op=ALU.add),
                 reads=[kt1_4, kt2_4], writes=[kr])

        def key_side4b(s4):
            c0 = s4 * 512
            kr = krb4[s4 % 2]
            pr = bfv(4, 64)
            for a in range(4):
                K.op('pe', lambda e, a=a: e.transpose(out=pr[:, a * 128:(a + 1) * 128], in_=kr[:, a, :], identity=ident_b[:, :]),
                     reads=[kr, ident_b], writes=[PB[4]], signal=(a == 3))
            K.op('act', lambda e: e.activation(out=kropeT[0:64, c0:c0 + 512], in_=pr[:, 0:512], func=AF.Copy), reads=[PB[4]], writes=[kropeT])

        K.dma(wv_g[:], w_vb[:, 0:512].rearrange("(k p) n -> p k n", p=128), writes=[wv_g], q='pool')
        K.dma(wk_h[:], w_kb[:, 0:128].rearrange("(k p) n -> p k n", p=128), writes=[wk_h], q='pool')
        K.dma(wq_h[:], w_qb[:, 0:192].rearrange("(k p) n -> p k n", p=128), writes=[wq_h], q='pool')
        K.dma(wrot_h[:], w_qbrot[:, 0:64].rearrange("(k p) n -> p k n", p=128), writes=[wrot_h], q='pool')
        for prob in range(3):
            if prob > 0:
                bi = prob - 1
                K.barrier()
                for s4 in range(8):
                    key_side4a(bi, s4)
                    if s4 >= 1:
                        key_side4b(s4 - 1)
                key_side4b(7)
                pb = 32 * bi
                key_side(CH[0], cnew[pb:pb + 32, :], [cnew], kpnew[pb:pb + 32, :], [kpnew], 32, 32, 4096,
                         sspe_new[pb:pb + 32, 0:1], sspe_new, pb=pb)
                K.dma(sspe[0:32, 32:33], sspe_new[pb:pb + 32, 0:1], reads=[sspe_new], writes=[sspe])
                K.barrier()
            for h in range(16):
                attention_head(h, prob)
        K.barrier()
        if stage <= 2:
            K.finish()
            return nc
        A.at(R1)
        hT_own = A.alloc("hT_own", [128, 16, TOWN], BF16)
        assert A.off == R2
        dT = A.alloc("dT", [128, 8, TOWN], BF16)
        a_memT = A.alloc("a_memT", [128, 8, TOWN], BF16)
        C_T = A.off
        assert C_T == 126976
        A.at(C_T)
        u_tm = A.alloc("u_tm", [128, 10, 1024], BF16)
        hT_prev = A.alloc("hT_prev", [128, 16, 128], BF16)
        spool_b = A.alloc("spool_b", [32, 1024], BF16)
        mcur_b = A.alloc("mcur_b", [128, 8, 128], BF16)
        mprev_b = A.alloc("mprev_b", [128, 32, 128], BF16)
        mcurS_b = A.alloc("mcurS_b", [64, 4, 64], BF16)
        mprevS_b = A.alloc("mprevS_b", [32, 4, 64], BF16)
        wpool_b = A.alloc("wpool_b", [128, 8, 512], BF16)
        uf = A.alloc("uf", [128, 2, 1024])
        c_ss = A.alloc("c_ss", [128, 1])
        c_r = A.alloc("c_r", [128, 1])
        C1_T = A.off
        xbc = [A.alloc(f"xbc{i}", [128, D]) for i in range(2)]
        hbc = A.alloc("hbc", [128, D], BF16)
        for i in range(4):
            K.dma(hT_own[:, 4 * i:4 * i + 4, :], hscr_t[:, 4 * i:4 * i + 4, 0:TOWN], reads=[hscr], writes=[hT_own], skip_wait=(i > 0))
        K.dma(hT_prev[:, :, :], hscr_t[:, :, TOWN:TOWN + 128], reads=[hscr], writes=[hT_prev])
        A.at(C1_T)
        wu = [A.alloc(f"wu{i}", [128, 16, 256], BF16) for i in range(2)]
        tblocks = [(s, r0, n) for s, (r0, n) in enumerate(BLOCKS)] + [(9, None, 128)]

        def c1_consts():
            K.dma(spool_b[:], spool, writes=[spool_b], q='pool')
            K.dma(mcur_b[:], mcur, writes=[mcur_b], q='pool')
            K.dma(mprev_b[:], mprev, writes=[mprev_b], q='pool')
            K.dma(mcurS_b[:], mcurS, writes=[mcurS_b], q='pool')
            K.dma(mprevS_b[:], mprevS, writes=[mprevS_b], q='pool')
        for ct in range(4):
            wt_ = wu[ct % 2]
            K.dma_k(wt_, lambda a, b, wt_=wt_: wt_[:, a:b, :], w_in[:, OFF_U + ct * 256:OFF_U + (ct + 1) * 256], 16, 2)
            if ct == 1:
                c1_consts()
            for ti, (slot, r0, n) in enumerate(tblocks):
                bank = 2 + ti % 4
                if slot == 9:
                    lf = lambda k: hT_prev[:, k, :]
                    rd = [hT_prev, wt_]
                else:
                    lf = lambda k, r0=r0, n=n: hT_own[:, k, r0:r0 + n]
                    rd = [hT_own, wt_]
                mm_tm(PB[bank][0:n, 0:256], bank, lf, lambda k, wt_=wt_: wt_[:, k, :], 16, rd)
                K.op('act', lambda e, bank=bank, n=n, slot=slot, ct=ct: e.activation(
                    out=u_tm[0:n, slot, ct * 256:(ct + 1) * 256], in_=PB[bank][0:n, 0:256], func=AF.Copy), reads=[PB[bank]], writes=[u_tm])
                if slot in (7, 8):
                    K.op('dve', lambda e, bank=bank, n=n, slot=slot, ct=ct: e.tensor_copy(
                        out=uf[0:n, slot - 7, ct * 256:(ct + 1) * 256], in_=PB[bank][0:n, 0:256]), reads=[PB[bank]], writes=[uf])
        K.dma(poolp_out[:, :], uf[113:128, 0, :], reads=[uf], final=True)
        K.dma(pools_out[0], uf[17:32, 1, :], reads=[uf], final=True)
        K.dma(pools_out[1], uf[49:64, 1, :], reads=[uf], final=True)
        for s, (r0, n) in enumerate(BLOCKS):
            for half in range(2):
                bank = 6 + (s * 2 + half) % 2
                for q4 in range(4):
                    c8 = half * 4 + q4
                    g = c8 // 2
                    o = PB[bank][:, q4 * 128:q4 * 128 + n]
                    if s < 8:
                        idx = g if s == 0 else 4 + g
                        K.op('pe', lambda e, o=o, s=s, c8=c8, idx=idx: e.matmul(
                            o, lhsT=u_tm[:, s, c8 * 128:(c8 + 1) * 128], rhs=mcur_b[:, idx, :], start=True, stop=False),
                            reads=[u_tm, mcur_b], writes=[PB[bank]], signal=False)
                        K.op('pe', lambda e, o=o, s=s, c8=c8, g=g: e.matmul(
                            o, lhsT=u_tm[:, 9, c8 * 128:(c8 + 1) * 128], rhs=mprev_b[:, s * 4 + g, :], start=False, stop=True),
                            reads=[u_tm, mprev_b], writes=[PB[bank]], signal=(q4 == 3))
                    else:
                        K.op('pe', lambda e, o=o, c8=c8, g=g: e.matmul(
                            o, lhsT=u_tm[0:64, 8, c8 * 128:(c8 + 1) * 128], rhs=mcurS_b[:, g, :], start=True, stop=False),
                            reads=[u_tm, mcurS_b], writes=[PB[bank]], signal=False)
                        K.op('pe', lambda e, o=o, c8=c8, g=g: e.matmul(
                            o, lhsT=spool_b[:, c8 * 128:(c8 + 1) * 128], rhs=mprevS_b[:, g, :], start=False, stop=True),
                            reads=[spool_b, mprevS_b], writes=[PB[bank]], signal=(q4 == 3))
                K.op(K.ev(), (lambda bank, half, r0, n: (lambda e: (
                    e.activation(out=dT[:, half * 4:half * 4 + 4, r0:r0 + n],
                                 in_=PB[bank][:, :].rearrange("p (a b) -> p a b", a=4)[:, :, 0:n], func=AF.Copy) if e is nc.scalar
                    else e.tensor_copy(out=dT[:, half * 4:half * 4 + 4, r0:r0 + n],
                                       in_=PB[bank][:, :].rearrange("p (a b) -> p a b", a=4)[:, :, 0:n]))))(bank, half, r0, n),
                    reads=[PB[bank]], writes=[dT])
        K.barrier()
        A.at(C_T)
        wmqs = [A.alloc(f"wmq{i}", [128, 16, 512], BF16) for i in range(2)]
        qmT_all = A.alloc("qmT_all", [128, 8, TOWN], BF16)
        sqh = [[A.alloc(f"sqh{i}{j}", [128, 512], BF16) for j in range(2)] for i in range(2)]
        rrc = [A.alloc(f"rrc{i}", [128, 512]) for i in range(2)]
        pTm = A.alloc("pTm", [128, 2, 512], BF16)
        recm = A.alloc("recm", [128, 512])
        gmqT = A.alloc("gmqT", [128, 2])
        cmk_b = A.alloc("cmk_b", [128, 2, 1024], BF16)
        memkT_s = [A.alloc(f"memkT_s{i}", [128, 8, 256], BF16) for i in range(2)]
        memv_s = [A.alloc(f"memv_s{i}", [128, 2, 1024], BF16) for i in range(2)]
        for i in range(2):
            K.dma_k(wmqs[i], lambda a, b, i=i: wmqs[i][:, a:b, :], w_in[:, OFF_MQ + i * 512:OFF_MQ + (i + 1) * 512], 16, 4)
        with nc.allow_non_contiguous_dma(reason="tiny per-partition gain vector"):
            K.dma(gmqT[:], g_mem_q.rearrange("(c p) -> p c", p=128), writes=[gmqT])
        for bi in range(2):
            K.dma(cmk_b[:], cmk[bi].rearrange("(a p) n -> p a n", p=128), writes=[cmk_b], q='pool')
            K.dma(memv_s[bi][:], cmv[bi].rearrange("(a p) n -> p a n", p=128), writes=[memv_s[bi]], q='pool')
            build_memkT(cmk_b, memkT_s[bi])
        MEM_SCALE = 1.0 / 16.0
        c2chunks = [(0, 512), (512, 512), (1024, 64)]
        ui = 0
        for hh in range(4):
            wm = wmqs[hh // 2]
            for (c0, w) in c2chunks:
                st_ = ui % 2
                ui += 1
                bA = (0, 1) if st_ == 0 else (3, 4)
                bS = 2 if st_ == 0 else 5
                for dc in range(2):
                    col0 = (hh % 2) * 256 + dc * 128
                    mm_tm(PB[bA[dc]][:, 0:w], bA[dc], lambda k, wm=wm, col0=col0: wm[:, k, col0:col0 + 128],
                          lambda k, c0=c0, w=w: hT_own[:, k, c0:c0 + w], 16, [wm, hT_own])
                    K.op('act', lambda e, dc=dc, st_=st_, w=w, bA=bA: e.activation(out=sqh[st_][dc][:, 0:w], in_=PB[bA[dc]][:, 0:w], func=AF.Square),
                         reads=[PB[bA[dc]]], writes=[sqh[st_][dc]])
                for dc in range(2):
                    K.op('pe', lambda e, dc=dc, st_=st_, w=w, bS=bS: e.matmul(PB[bS][:, 0:w], lhsT=ones_b[:, :], rhs=sqh[st_][dc][:, 0:w],
                                                                             start=(dc == 0), stop=(dc == 1)),
                         reads=[ones_b, sqh[st_][dc]], writes=[PB[bS]], signal=(dc == 1))
                K.op('act', lambda e, st_=st_, w=w, bS=bS: e.activation(out=rrc[st_][:, 0:w], in_=PB[bS][:, 0:w], func=AF.Ln, bias=eps6[:], scale=1.0 / 256),
                     reads=[PB[bS], eps6], writes=[rrc[st_]])
                K.op('act', lambda e, st_=st_, w=w: e.activation(out=rrc[st_][:, 0:w], in_=rrc[st_][:, 0:w], func=AF.Exp, scale=-0.5),
                     reads=[rrc[st_]], writes=[rrc[st_]])
                for dc in range(2):
                    K.op('dve', lambda e, dc=dc, st_=st_, w=w, c0=c0, hh=hh, bA=bA: e.scalar_tensor_tensor(
                        out=qmT_all[:, hh * 2 + dc, c0:c0 + w], in0=PB[bA[dc]][:, 0:w], scalar=gmqT[:, dc:dc + 1], in1=rrc[st_][:, 0:w],
                        op0=ALU.mult, op1=ALU.mult), reads=[PB[bA[dc]], gmqT, rrc[st_]], writes=[qmT_all])
        units = [(memkT_p, memv_p, hh, c0, 512) for hh in range(4) for c0 in (0, 512)]
        units += [(memkT_s[bi], memv_s[bi], hh, 1024 + 32 * bi, 32) for bi in range(2) for hh in range(4)]

        def m_scores(u):
            mkT, mv, hh, c0, w = units[u]
            sc = (0, 1) if u % 2 == 0 else (2, 3)
            for mb in range(2):
                for dc in range(2):
                    K.op('pe', lambda e, mb=mb, dc=dc: e.matmul(
                        PB[sc[mb]][:, 0:w], lhsT=mkT[:, hh * 2 + dc, mb * 128:(mb + 1) * 128], rhs=qmT_all[:, hh * 2 + dc, c0:c0 + w],
                        start=(dc == 0), stop=(dc == 1)), reads=[mkT, qmT_all], writes=[PB[sc[mb]]], signal=(dc == 1))

        def m_rest(u):
            mkT, mv, hh, c0, w = units[u]
            sc = (0, 1) if u % 2 == 0 else (2, 3)
            for mb in range(2):
                K.op('act', lambda e, mb=mb: e.activation(out=pTm[:, mb, 0:w], in_=PB[sc[mb]][:, 0:w], func=AF.Exp, scale=MEM_SCALE),
                     reads=[PB[sc[mb]]], writes=[pTm])
            for dvc in range(2):
                for mb in range(2):
                    K.op('pe', lambda e, dvc=dvc, mb=mb: e.matmul(
                        PB[4 + dvc][:, 0:w], lhsT=mv[:, mb, hh * 256 + dvc * 128:hh * 256 + (dvc + 1) * 128], rhs=pTm[:, mb, 0:w],
                        start=(mb == 0), stop=(mb == 1)), reads=[mv, pTm], writes=[PB[4 + dvc]], signal=(mb == 1))
            for mb in range(2):
                K.op('pe', lambda e, mb=mb: e.matmul(PB[6][:, 0:w], lhsT=ones_b[:, :], rhs=pTm[:, mb, 0:w], start=(mb == 0), stop=(mb == 1)),
                     reads=[ones_b, pTm], writes=[PB[6]], signal=(mb == 1))
            K.op('act', lambda e: e.activation(out=recm[:, 0:w], in_=PB[6][:, 0:w], func=AF.Ln), reads=[PB[6]], writes=[recm])
            K.op('act', lambda e: e.activation(out=recm[:, 0:w], in_=recm[:, 0:w], func=AF.Exp, scale=-1.0), reads=[recm], writes=[recm])
            for dvc in range(2):
                K.op('dve', lambda e, dvc=dvc: e.tensor_tensor(out=a_memT[:, hh * 2 + dvc, c0:c0 + w], in0=PB[4 + dvc][:, 0:w], in1=recm[:, 0:w],
                                                               op=ALU.mult), reads=[PB[4 + dvc], recm], writes=[a_memT])

        m_scores(0)
        for u in range(len(units)):
            if u + 1 < len(units):
                m_scores(u + 1)
            m_rest(u)
        K.barrier()
        A.at(C_T)
        mergedT = A.alloc("mergedT", [128, 16, TOWN], BF16)
        C3_T = A.off
        wg = [A.alloc(f"wg{i}", [128, 3, 16, 128], BF16) for i in range(2)]
        wmo = [A.alloc(f"wmo{i}", [128, 16, 128], BF16) for i in range(2)]
        wme = [A.alloc(f"wme{i}", [128, 8, 128], BF16) for i in range(2)]
        wpo = [A.alloc(f"wpo{i}", [128, 2, 128], BF16) for i in range(2)]
        gs = A.alloc("gs", [128, 512])
        acc = A.alloc("acc", [128, 512])
        tmpc = A.alloc("tmpc", [128, 512])
        bgT = A.alloc("bgT", [128, 48])
        psT = A.alloc("psT", [128, 16])
        with nc.allow_non_contiguous_dma(reason="tiny per-partition bias/scale vectors"):
            K.dma(bgT[:], b_gate.rearrange("(c p) -> p c", p=128), writes=[bgT])
            K.dma(psT[:], pool_scale.rearrange("(c p) -> p c", p=128), writes=[psT])
        tchunks = [(0, 512), (512, 512), (1024, 64)]
        for cg in range(16):
            x_ = cg % 2
            for br in range(3):
                K.dma(wg[x_][:, br, :, :], w_in[:, OFF_G + br * 2048 + cg * 128:OFF_G + br * 2048 + (cg + 1) * 128].rearrange(
                    "(k p) n -> p k n", p=128), writes=[wg[x_]], q='pool')
            K.dma(wmo[x_][:], w_mla_o[:, cg * 128:(cg + 1) * 128].rearrange("(k p) n -> p k n", p=128), writes=[wmo[x_]], q='pool')
            K.dma(wme[x_][:], w_mem_o[:, cg * 128:(cg + 1) * 128].rearrange("(k p) n -> p k n", p=128), writes=[wme[x_]], q='pool')
            gi = cg // 4
            K.dma(wpo[x_][:], w_pool[gi, :, (cg % 4) * 128:(cg % 4 + 1) * 128].rearrange("(k p) n -> p k n", p=128), writes=[wpo[x_]], q='pool')
            for (c0, w) in tchunks:
                for br in range(3):
                    mm_tm(PB[br][:, 0:w], br, lambda k, br=br: wg[x_][:, br, k, :], lambda k, c0=c0, w=w: hT_own[:, k, c0:c0 + w], 16,
                          [wg[x_], hT_own])
                mm_tm(PB[3][:, 0:w], 3, lambda k: wpo[x_][:, k, :], lambda k, c0=c0, w=w: dT[:, gi * 2 + k, c0:c0 + w], 2, [wpo[x_], dT])
                mm_tm(PB[4][:, 0:w], 4, lambda k: wmo[x_][:, k, :], lambda k, c0=c0, w=w: oT_all[:, k, c0:c0 + w], 16, [wmo[x_], oT_all])
                mm_tm(PB[5][:, 0:w], 5, lambda k: wme[x_][:, k, :], lambda k, c0=c0, w=w: a_memT[:, k, c0:c0 + w], 8, [wme[x_], a_memT])
                K.op('act', lambda e, w=w: e.activation(out=gs[:, 0:w], in_=PB[0][:, 0:w], func=AF.Sigmoid, bias=bgT[:, cg:cg + 1], scale=1.0),
                     reads=[PB[0], bgT], writes=[gs])
                K.op('dve', lambda e, w=w: e.scalar_tensor_tensor(out=acc[:, 0:w], in0=PB[3][:, 0:w], scalar=psT[:, cg:cg + 1], in1=gs[:, 0:w],
                                                                  op0=ALU.mult, op1=ALU.mult), reads=[PB[3], psT, gs], writes=[acc])
                K.op('act', lambda e, w=w: e.activation(out=gs[:, 0:w], in_=PB[1][:, 0:w], func=AF.Sigmoid, bias=bgT[:, 16 + cg:17 + cg], scale=1.0),
                     reads=[PB[1], bgT], writes=[gs])
                K.op('dve', lambda e, w=w: e.tensor_tensor(out=tmpc[:, 0:w], in0=PB[4][:, 0:w], in1=gs[:, 0:w], op=ALU.mult),
                     reads=[PB[4], gs], writes=[tmpc])
                K.op('dve', lambda e, w=w: e.tensor_tensor(out=acc[:, 0:w], in0=acc[:, 0:w], in1=tmpc[:, 0:w], op=ALU.add),
                     reads=[acc, tmpc], writes=[acc])
                K.op('act', lambda e, w=w: e.activation(out=gs[:, 0:w], in_=PB[2][:, 0:w], func=AF.Sigmoid, bias=bgT[:, 32 + cg:33 + cg], scale=1.0),
                     reads=[PB[2], bgT], writes=[gs])
                K.op('dve', lambda e, w=w: e.tensor_tensor(out=tmpc[:, 0:w], in0=PB[5][:, 0:w], in1=gs[:, 0:w], op=ALU.mult),
                     reads=[PB[5], gs], writes=[tmpc])
                K.op('dve', lambda e, c0=c0, w=w: e.tensor_tensor(out=mergedT[:, cg, c0:c0 + w], in0=acc[:, 0:w], in1=tmpc[:, 0:w], op=ALU.add),
                     reads=[acc, tmpc], writes=[mergedT])
        K.barrier()
        A.at(OT_OFF)
        y_acc = A.alloc("y_acc", [128, 9, D])
        assert A.off == 96256
        yb = [Buf(A.t, f"yacc{s}", ap=y_acc._ap[:, s, :]) for s in range(9)]
        wo = [A.alloc(f"wo{i}", [128, 16, 256], BF16) for i in range(2)]
        hbq = [A.alloc(f"hbq{i}", [128, 256], BF16) for i in range(3)]
        sqj = A.alloc("sqj", [128, 256], BF16)
        assert A.off <= C_T
        A.at(C3_T)
        h2T = A.alloc("h2T", [128, 16, TOWN], BF16)
        wup1 = A.alloc("wup1", [128, 16, 256], BF16)
        K.dma(gbc[:], g_ff.partition_broadcast(128), writes=[gbc])
        K.op('dve', lambda e: e.memset(ssp[:], 1.0), writes=[ssp])

        def load_wo(ct):
            K.dma_k(wo[ct % 2], lambda a, b: wo[ct % 2][:, a:b, :], w_out[:, ct * 256:(ct + 1) * 256], 16, 2)

        def load_xres(s):
            r0, n = BLOCKS[s]
            K.dma(yb[s][0:n, :], xown[r0:r0 + n, :], writes=[yb[s]], q='pool')

        load_wo(0)
        for s in range(4):
            load_xres(s)
        load_wo(1)
        for s in range(4, 9):
            load_xres(s)
        K.dma(wup1[:], w_up[:, 0:256].rearrange("(k p) n -> p k n", p=128), writes=[wup1], q='pool')

        groups = [(ct, s) for ct in range(8) for s in range(9)]

        def c4_tail(gi):
            ct, s = groups[gi]
            r0, n = BLOCKS[s]
            bank = 4 + gi % 4
            hb = hbq[gi % 3]
            pv = bfv(bank)
            for j in range(2):
                K.op('pe', lambda e, j=j: e.transpose(out=pv[:, j * 128:j * 128 + n], in_=hb[0:n, j * 128:(j + 1) * 128],
                                                      identity=ident_b[0:n, 0:n]), reads=[hb, ident_b], writes=[PB[bank]], signal=(j == 1))
            K.op('act', lambda e: e.activation(out=h2T[:, 2 * ct:2 * ct + 2, r0:r0 + n],
                                               in_=pv[:, 0:256].rearrange("p (a b) -> p a b", a=2)[:, :, 0:n], func=AF.Copy),
                 reads=[PB[bank]], writes=[h2T])

        for gi, (ct, s) in enumerate(groups):
            r0, n = BLOCKS[s]
            wt_ = wo[ct % 2]
            if s == 0 and ct >= 2:
                load_wo(ct)
            bank = s % 4
            cs = slice(ct * 256, (ct + 1) * 256)
            mm_tm(PB[bank][0:n, 0:256], bank, lambda k, r0=r0, n=n: mergedT[:, k, r0:r0 + n], lambda k, wt_=wt_: wt_[:, k, :], 16,
                  [mergedT, wt_])
            K.op('dve', lambda e, bank=bank, n=n, s=s, cs=cs: e.tensor_tensor(
                out=yb[s][0:n, cs], in0=PB[bank][0:n, 0:256], in1=yb[s][0:n, cs], op=ALU.add),
                reads=[PB[bank], yb[s]], writes=[yb[s]])
            K.op('act', lambda e, n=n, s=s, cs=cs, ct=ct: e.activation(out=sqj[0:n, :], in_=yb[s][0:n, cs], func=AF.Square,
                                                                       accum_out=ssp[0:n, s, ct:ct + 1]),
                 reads=[yb[s]], writes=[sqj, ssp])
            K.op('dve', lambda e, n=n, s=s, cs=cs, gi=gi: e.tensor_tensor(out=hbq[gi % 3][0:n, :], in0=yb[s][0:n, cs], in1=gbc[0:n, cs],
                                                                          op=ALU.mult), reads=[yb[s], gbc], writes=[hbq[gi % 3]])
            if gi >= 2:
                c4_tail(gi - 2)
        c4_tail(len(groups) - 2)
        c4_tail(len(groups) - 1)
        K.op('dve', lambda e: e.tensor_reduce(out=ssA[:], in_=ssp[:], axis=AX.X, op=ALU.add), reads=[ssp], writes=[ssA])
        K.op('act', lambda e: e.activation(out=ssA[:], in_=ssA[:], func=AF.Sqrt, bias=eps6[:], scale=1.0 / D), reads=[ssA, eps6], writes=[ssA])
        K.op('dve', lambda e: e.reciprocal(out=ssA[:], in_=ssA[:]), reads=[ssA], writes=[ssA])
        K.op('dve', lambda e: e.tensor_tensor(out=rstd2[:], in0=ssA[:], in1=ssA[:], op=ALU.mult), reads=[ssA], writes=[rstd2])
        K.barrier()
        A.at(96256)
        fT = A.alloc("fT", [128, 8, TOWN], BF16)
        wdn = [A.alloc(f"wdn{i}", [128, D], BF16) for i in range(8)]
        wup0 = A.alloc("wup0", [128, 16, 256], BF16)
        rl = [A.alloc(f"rl{i}", [128, 512]) for i in range(2)]
        assert A.off <= C3_T, A.off
        wup = [wup1, wup0]
        NT = 32

        def load_wup(T):
            K.dma(wup[T % 2][:], w_up[:, T * 256:(T + 1) * 256].rearrange("(k p) n -> p k n", p=128), writes=[wup[T % 2]], q='pool')

        def load_wdn(G):
            for hc in range(8):
                f_ = G * 8 + hc
                K.dma(wdn[hc][:], w_down[f_ * 128:(f_ + 1) * 128, :], writes=[wdn[hc]], q='pool')

        load_wup(1)
        load_wdn(0)
        ui = 0
        for G in range(8):
            for q4 in range(4):
                T = G * 4 + q4
                wt_ = wup[T % 2]
                for sub in range(2):
                    hc = 2 * q4 + sub
                    for (c0, w) in tchunks:
                        bank = ui % 4
                        r_ = rl[ui % 2]
                        ui += 1
                        mm_tm(PB[bank][:, 0:w], bank, lambda k, wt_=wt_, sub=sub: wt_[:, k, sub * 128:(sub + 1) * 128],
                              lambda k, c0=c0, w=w: h2T[:, k, c0:c0 + w], 16, [wt_, h2T])
                        K.op('act', lambda e, bank=bank, w=w, r_=r_: e.activation(out=r_[:, 0:w], in_=PB[bank][:, 0:w], func=AF.Relu),
                             reads=[PB[bank]], writes=[r_])
                        K.op('dve', lambda e, w=w, r_=r_, hc=hc, c0=c0: e.tensor_tensor(out=fT[:, hc, c0:c0 + w], in0=r_[:, 0:w], in1=r_[:, 0:w],
                                                                                       op=ALU.mult), reads=[r_], writes=[fT])
                if T + 2 < NT:
                    load_wup(T + 2)
            for s, (r0, n) in enumerate(BLOCKS):
                for c4 in range(4):
                    bank = 4 + c4
                    mm_tm(PB[bank][0:n, :], bank, lambda k, r0=r0, n=n: fT[:, k, r0:r0 + n], lambda k, c4=c4: wdn[k][:, c4 * 512:(c4 + 1) * 512], 8,
                          [fT] + wdn)
                    K.op('dve', lambda e, bank=bank, n=n, s=s, c4=c4: e.scalar_tensor_tensor(
                        out=yb[s][0:n, c4 * 512:(c4 + 1) * 512], in0=PB[bank][0:n, :], scalar=rstd2[0:n, s:s + 1],
                        in1=yb[s][0:n, c4 * 512:(c4 + 1) * 512], op0=ALU.mult, op1=ALU.add),
                        reads=[PB[bank], yb[s], rstd2], writes=[yb[s]])
                if G == 7:
                    K.dma(y_own[r0:r0 + n, :], yb[s][0:n, :], reads=[yb[s]], final=True)
            if G + 1 < 8:
                load_wdn(G + 1)
        K.finish()
        print("program built: ninst", K.ninst, "nsem", K.nsem, flush=True)
        return nc


_PROG = {}
POOL_WINDOWS = (2, 4, 8, 16)


def _rope_tables():
    half = 32
    inv = (1.0 / (np.float32(10000.0) ** (np.arange(half, dtype=np.float32) * np.float32(2.0 / 64)))).astype(np.float32)
    return inv


def _consts(j):
    inv = _rope_tables()
    c = {}
    c["ident"] = np.eye(128, dtype=np.float32)
    pos = (np.arange(NKB)[None, :] * 128 + np.arange(128)[:, None]).astype(np.float32)
    pos[:, 32] = 4096 + (np.arange(128) % 32)
    ang = pos[:, :, None] * inv[None, None, :]
    cs, sn = np.cos(ang).astype(np.float32), np.sin(ang).astype(np.float32)
    c["ktab_c"] = np.concatenate([cs, cs], axis=-1)
    c["ktab_s"] = np.concatenate([-sn, sn], axis=-1)
    qpos = np.zeros(TOWN, np.float32)
    for s in range(8):
        qpos[s * 128:(s + 1) * 128] = (4 * s + j) * 128 + np.arange(128)
    qpos[1024:1056] = 4096 + np.arange(32)
    qpos[1056:1088] = 4096 + np.arange(32)
    qa = qpos[None, :] * inv[:, None]
    qc, qs = np.cos(qa).astype(np.float32), np.sin(qa).astype(np.float32)
    c["qtab_c"] = np.concatenate([qc, qc], axis=0)
    c["qtab_s"] = np.concatenate([-qs, qs], axis=0)
    mcur = np.zeros((128, 8, 128), np.float32)
    for g, win in enumerate(POOL_WINDOWS):
        for t in range(128):
            for first in (True, False):
                cnt = min(t + 1, win) if first else win
                idx = g if first else 4 + g
                lo = max(0, t - win + 1)
                mcur[lo:t + 1, idx, t] += 1.0 / cnt
                mcur[t, idx, t] -= 1.0
    if j != 0:
        mcur[:, 0:4, :] = mcur[:, 4:8, :]
    c["mcur"] = mcur
    mprev = np.zeros((128, 32, 128), np.float32)
    for s in range(8):
        for g, win in enumerate(POOL_WINDOWS):
            for t in range(min(win - 1, 128)):
                for ps in range(t - win + 1, 0):
                    mprev[16 * s + 16 + ps, s * 4 + g, t] += 1.0 / win
    if j == 0:
        mprev[:, 0:4, :] = 0.0
    c["mprev"] = mprev
    mcs = np.zeros((64, 4, 64), np.float32)
    mps = np.zeros((32, 4, 64), np.float32)
    for bi in range(2):
        for g, win in enumerate(POOL_WINDOWS):
            for t in range(32):
                lo = max(0, t - win + 1)
                mcs[bi * 32 + lo:bi * 32 + t + 1, g, bi * 32 + t] += 1.0 / win
                mcs[bi * 32 + t, g, bi * 32 + t] -= 1.0
                for ps in range(t - win + 1, 0):
                    mps[bi * 16 + 16 + ps, g, bi * 32 + t] += 1.0 / win
    c["mcurS"] = mcs
    c["mprevS"] = mps
    km = np.zeros((16, SEQ), np.float32)
    qm = np.zeros((16, 1024), np.float32)
    for s in range(8):
        qm[2 * s, s * 128:s * 128 + 64] = -30000.0
        qm[2 * s + 1, s * 128 + 64:(s + 1) * 128] = -30000.0
        for d in range(4):
            kb = 4 * s + d
            if d > j:
                km[2 * s, kb * 128:(kb + 1) * 128] = 1.0
                km[2 * s + 1, kb * 128:(kb + 1) * 128] = 1.0
            elif d == j:
                km[2 * s, kb * 128 + 64:(kb + 1) * 128] = 1.0
    c["kmask"] = km
    c["qmask"] = qm
    return c


def kernel(x_prompt, mem_prompt, x_sample, cache_mla_latent, cache_mla_kpe, state_pool,
           cache_mem_k, cache_mem_v, g_mix, w_in, b_gate, w_pool, pool_scale, g_q_lat, w_qb,
           g_q_head, g_kv_lat, w_kb, w_vb, g_k_head, w_mla_o, g_mem, w_mem_kv, g_mem_q,
           g_mem_k, w_mem_o, w_out, g_ff, w_up, w_down):
    stage = int(os.environ.get("MK_STAGE", "99"))
    f = lambda a: np.ascontiguousarray(np.asarray(a, dtype=np.float32))
    x_prompt, mem_prompt, x_sample = f(x_prompt), f(mem_prompt), f(x_sample)
    cache_mla_latent, cache_mla_kpe, state_pool = f(cache_mla_latent), f(cache_mla_kpe), f(state_pool)
    cache_mem_k, cache_mem_v = f(cache_mem_k), f(cache_mem_v)
    if stage not in _PROG:
        _PROG[stage] = build_program(stage)
    nc = _PROG[stage]
    w_qb = f(w_qb)
    wq3 = w_qb.reshape(512, 16, 192)
    w_qbrot = np.ascontiguousarray(np.concatenate([wq3[:, :, 160:192], wq3[:, :, 128:160]], axis=-1).reshape(512, 1024))
    shared = dict(w_in=f(w_in), w_pool=f(w_pool), w_qb=w_qb, w_qbrot=w_qbrot, w_kb=f(w_kb), w_vb=f(w_vb),
                  w_mla_o=f(w_mla_o), w_mem_kv=f(w_mem_kv), w_mem_o=f(w_mem_o), w_out=f(w_out), w_up=f(w_up),
                  w_down=f(w_down), g_mix=f(g_mix), b_gate=f(b_gate), pool_scale=f(pool_scale), g_q_lat=f(g_q_lat),
                  g_q_head=f(g_q_head), g_kv_lat=f(g_kv_lat), g_k_head=f(g_k_head), g_mem=f(g_mem),
                  g_mem_q=f(g_mem_q), g_mem_k=f(g_mem_k), g_ff=f(g_ff))
    in_maps = []
    for c in range(8):
        b, j = c // 4, c % 4
        xs = x_prompt[b]
        xown = np.empty((TOWN, D), np.float32)
        xprev = np.zeros((128, D), np.float32)
        for s in range(8):
            i = 4 * s + j
            xown[s * 128:(s + 1) * 128] = xs[i * 128:(i + 1) * 128]
            if i > 0:
                xprev[16 * s:16 * s + 16] = xs[i * 128 - 16:i * 128]
        xown[1024:1056] = x_sample[2 * c]
        xown[1056:1088] = x_sample[2 * c + 1]
        sp = np.zeros((32, 1024), np.float32)
        sp[1:16] = state_pool[2 * c]
        sp[17:32] = state_pool[2 * c + 1]
        m = dict(xseq=xs, xown=xown, xprev=xprev,
                 latc=cache_mla_latent[2 * c:2 * c + 2], kpec=cache_mla_kpe[2 * c:2 * c + 2], spool=sp,
                 memp=mem_prompt[b], cmk=cache_mem_k[2 * c:2 * c + 2].reshape(2, 256, 1024),
                 cmv=cache_mem_v[2 * c:2 * c + 2].reshape(2, 256, 1024))
        m.update(shared)
        m.update(_consts(j))
        in_maps.append({k: np.ascontiguousarray(v) for k, v in m.items()})
    res = run_bass_kernel_spmd(nc, in_maps, core_ids=list(range(8)))
    R = res.results
    y_p = np.zeros((2, SEQ, D), np.float32)
    lat_p = np.zeros((2, SEQ, 512), np.float32)
    kpe_p = np.zeros((2, SEQ, 64), np.float32)
    y_s = np.zeros((16, 32, D), np.float32)
    lat_s = np.zeros((16, 32, 512), np.float32)
    kpe_s = np.zeros((16, 32, 64), np.float32)
    pool_p = np.zeros((2, 15, 1024), np.float32)
    pool_s = np.zeros((16, 15, 1024), np.float32)
    mem_k_p = np.zeros((2, 256, 4, 256), np.float32)
    mem_v_p = np.zeros((2, 256, 4, 256), np.float32)
    for c in range(8):
        b, j = c // 4, c % 4
        r = R[c]
        for s in range(8):
            i = 4 * s + j
            y_p[b, i * 128:(i + 1) * 128] = r["y_own"][s * 128:(s + 1) * 128]
            lat_p[b, i * 128:(i + 1) * 128] = r["lat_own"][s * 128:(s + 1) * 128]
            kpe_p[b, i * 128:(i + 1) * 128] = r["kpe_own"][s * 128:(s + 1) * 128]
        for bi in range(2):
            y_s[2 * c + bi] = r["y_own"][1024 + 32 * bi:1056 + 32 * bi]
            lat_s[2 * c + bi] = r["lat_own"][1024 + 32 * bi:1056 + 32 * bi]
            kpe_s[2 * c + bi] = r["kpe_own"][1024 + 32 * bi:1056 + 32 * bi]
            pool_s[2 * c + bi] = r["pools_out"][bi]
        if j == 3:
            pool_p[b] = r["poolp_out"]
        if j == 0:
            mem_k_p[b] = r["memk_out"].reshape(256, 4, 256)
            mem_v_p[b] = r["memv_out"].reshape(256, 4, 256)
    return (y_p, y_s, lat_p, kpe_p, pool_p, mem_k_p, mem_v_p, lat_s, kpe_s, pool_s)
```

```python
import os
import numpy as np
from contextlib import ExitStack
import concourse.bass as bass
import concourse.mybir as mybir
from concourse.bass_utils import run_bass_kernel_spmd

F32 = mybir.dt.float32
BF16 = mybir.dt.bfloat16
AF = mybir.ActivationFunctionType
ALU = mybir.AluOpType
AX = mybir.AxisListType

D = 2048
SEQ = 4096
NB = 32
NKB = 33
TOWN = 1088
EPS = 1e-6
D_IN = 9280
OFF_U, OFF_QL, OFF_KV, OFF_KPE, OFF_MQ, OFF_G = 0, 1024, 1536, 2048, 2112, 3136
EPOCH = 30000
ARENA_BYTES = 210944

BLOCKS = [(s * 128, 128) for s in range(8)] + [(1024, 64)]


class Buf:
    def __init__(self, t, name, psum=False, ap=None):
        self.t = t
        self.name = name
        self.psum = psum
        self.w = {}
        self.r = {}
        self.dsem = None
        self.dval = 0
        self._ap = ap

    def __getitem__(self, key):
        base = self._ap if self._ap is not None else self.t
        return base[key]


class Eng:
    def __init__(self, name, obj):
        self.name = name
        self.obj = obj
        self.sem = None
        self.cnt = 0
        self.waited = {}


class Kern:
    def __init__(self, nc, stack):
        self.nc = nc
        self.stack = stack
        self.e = {n: Eng(n, o) for n, o in (('pe', nc.tensor), ('act', nc.scalar), ('dve', nc.vector),
                                            ('pool', nc.gpsimd), ('sp', nc.sync))}
        self.nsem = 0
        for n in ('pe', 'act', 'dve', 'pool'):
            self.e[n].sem = self.new_sem(n)
        self.store_toks = []
        self.dma_bufs = []
        self.ninst = 0
        self.rr = 0
        self._rec = None

    def new_sem(self, name):
        self.nsem += 1
        return self.stack.enter_context(self.nc.semaphore(f"s{self.nsem}_{name}"))

    def _need(self, en, reads, writes):
        E = self.e[en]
        need = {}

        def add(d, own_ok):
            for s, v in d.items():
                if s is E.sem and en == 'pe':
                    continue
                if need.get(s, 0) < v:
                    need[s] = v

        for b in reads:
            add(b.w, True)
            if b.psum:
                add(b.r, False)
        for b in writes:
            add(b.w, False)
            add(b.r, False)
        return need

    def _do_waits(self, en, need):
        E = self.e[en]
        for s, v in need.items():
            if E.waited.get(s, 0) >= v:
                continue
            E.obj.wait_ge(s, v)
            E.waited[s] = v
            self.ninst += 1

    def record(self):
        self._rec = []

    def stop_record(self):
        r = self._rec
        self._rec = None
        return r

    def py(self, fn):
        if self._rec is not None:
            self._rec.append(('py', (fn,), {}))
        else:
            fn()

    def replay(self, lists, width=2, stagger=0.5):
        active = []
        nxt = 0
        while nxt < len(lists) or active:
            if nxt < len(lists) and len(active) < width and (
                    not active or active[-1][1] >= stagger * len(active[-1][0])):
                active.append([lists[nxt], 0])
                nxt += 1
            for a in list(active):
                lst, pos = a
                if pos < len(lst):
                    kind, args, kw = lst[pos]
                    if kind == 'py':
                        args[0]()
                    else:
                        (self.op if kind == 'op' else self.dma)(*args, **kw)
                    a[1] += 1
                if a[1] >= len(lst):
                    active.remove(a)

    def op(self, en, fn, reads=(), writes=(), signal=True):
        if self._rec is not None:
            self._rec.append(('op', (en, fn, list(reads), list(writes), signal), {}))
            return None
        E = self.e[en]
        self._do_waits(en, self._need(en, reads, writes))
        ins = fn(E.obj)
        self.ninst += 1
        if E.cnt >= EPOCH - 200:
            E.sem = self.new_sem(en)
            E.cnt = 0
        if signal:
            E.cnt += 1
            ins.then_inc(E.sem, 1)
            tok = (E.sem, E.cnt)
        else:
            tok = (E.sem, E.cnt + 1)
        for b in reads:
            if b.r.get(tok[0], 0) < tok[1]:
                b.r[tok[0]] = tok[1]
        for b in writes:
            b.w = {tok[0]: tok[1]}
            b.r = {}
        return ins

    def dma(self, out_ap, in_ap, reads=(), writes=(), q='sp', final=False, skip_wait=False, **kw):
        if self._rec is not None:
            kw2 = dict(kw)
            kw2.update(reads=list(reads), writes=list(writes), q=q, final=final, skip_wait=skip_wait)
            self._rec.append(('dma', (out_ap, in_ap), kw2))
            return None
        E = self.e[q]
        if not skip_wait:
            self._do_waits(q, self._need(q, reads, writes))
        owner = writes[0] if writes else reads[0]
        if owner.dsem is None or owner.dval >= EPOCH:
            owner.dsem = self.new_sem('d')
            owner.dval = 0
            self.dma_bufs.append(owner)
        owner.dval += 16
        E.obj.dma_start(out=out_ap, in_=in_ap, **kw).then_inc(owner.dsem, 16)
        self.ninst += 1
        tok = (owner.dsem, owner.dval)
        for b in reads:
            if b.r.get(tok[0], 0) < tok[1]:
                b.r[tok[0]] = tok[1]
        for b in writes:
            b.w = {tok[0]: tok[1]}
            b.r = {}
        if final:
            self.store_toks.append(tok)
        return tok

    def dma_k(self, buf, dst_fn, src2d, nk, nsplit, q='pool', reads=()):
        g = nk // nsplit
        assert g * nsplit == nk
        for i in range(nsplit):
            self.dma(dst_fn(i * g, (i + 1) * g), src2d[i * g * 128:(i + 1) * g * 128, :].rearrange("(k p) n -> p k n", p=128),
                     reads=list(reads), writes=[buf], q=q, skip_wait=(i > 0))

    def barrier(self):
        names = ('pe', 'act', 'dve', 'pool')
        for a in names + ('sp',):
            need = {}
            for b in names:
                if a == b:
                    continue
                B = self.e[b]
                if B.cnt > 0:
                    need[B.sem] = B.cnt
            for b in self.dma_bufs:
                if b.dval > 0 and need.get(b.dsem, 0) < b.dval:
                    need[b.dsem] = b.dval
            self._do_waits(a, need)

    def finish(self):
        need = {}
        for s, v in self.store_toks:
            if need.get(s, 0) < v:
                need[s] = v
        for b in ('pe', 'act', 'dve', 'pool'):
            B = self.e[b]
            if B.cnt > 0:
                need[B.sem] = B.cnt
        self._do_waits('sp', need)

    def ev(self):
        self.rr ^= 1
        return 'act' if self.rr else 'dve'


class Arena:
    def __init__(self, K, nbytes):
        self.K = K
        self.t = K.stack.enter_context(K.nc.sbuf_tensor("arena", [128, nbytes // 4], F32))
        self.off = 0
        self.limit = nbytes
        self.n = 0

    def at(self, off):
        self.off = off

    def alloc(self, name, shape, dt=F32):
        nfree = int(np.prod(shape[1:]))
        esz = 2 if dt == BF16 else 4
        nb = (nfree * esz + 31) // 32 * 32
        o = self.off
        assert o % 4 == 0 and o + nb <= self.limit, (name, o, nb, self.limit)
        self.off = o + nb
        ap = self.t[0:shape[0], o // 4:(o + nb) // 4]
        if dt != F32:
            ap = ap.bitcast(dt)
        ap = ap[:, 0:nfree]
        if len(shape) == 3:
            ap = ap.rearrange("p (a b) -> p a b", a=shape[1])
        elif len(shape) == 4:
            ap = ap.rearrange("p (a b c) -> p a b c", a=shape[1], b=shape[2])
        self.n += 1
        return Buf(self.t, f"{name}_{self.n}", ap=ap)


def build_program(stage=99):
    nc = bass.Bass("TRN2", target_bir_lowering=False)

    def din(name, shape):
        return nc.dram_tensor(name, list(shape), F32, kind="ExternalInput").ap()

    def dout(name, shape):
        return nc.dram_tensor(name, list(shape), F32, kind="ExternalOutput").ap()

    xseq = din("xseq", [SEQ, D])
    xown = din("xown", [TOWN, D])
    xprev = din("xprev", [128, D])
    latc = din("latc", [2, SEQ, 512])
    kpec = din("kpec", [2, SEQ, 64])
    spool = din("spool", [32, 1024])
    memp = din("memp", [256, D])
    cmk = din("cmk", [2, 256, 1024])
    cmv = din("cmv", [2, 256, 1024])
    w_in = din("w_in", [D, D_IN])
    w_pool = din("w_pool", [4, 256, 512])
    w_qb = din("w_qb", [512, 3072])
    w_qbrot = din("w_qbrot", [512, 1024])
    w_kb = din("w_kb", [512, 2048])
    w_vb = din("w_vb", [512, 2048])
    w_mla_o = din("w_mla_o", [2048, 2048])
    w_mem_kv = din("w_mem_kv", [2048, 2048])
    w_mem_o = din("w_mem_o", [1024, 2048])
    w_out = din("w_out", [2048, 2048])
    w_up = din("w_up", [2048, 8192])
    w_down = din("w_down", [8192, 2048])
    g_mix = din("g_mix", [D])
    b_gate = din("b_gate", [6144])
    pool_scale = din("pool_scale", [D])
    g_q_lat = din("g_q_lat", [512])
    g_q_head = din("g_q_head", [192])
    g_kv_lat = din("g_kv_lat", [512])
    g_k_head = din("g_k_head", [192])
    g_mem = din("g_mem", [D])
    g_mem_q = din("g_mem_q", [256])
    g_mem_k = din("g_mem_k", [256])
    g_ff = din("g_ff", [D])
    ident = din("ident", [128, 128])
    ktab_c = din("ktab_c", [128, NKB, 64])
    ktab_s = din("ktab_s", [128, NKB, 64])
    qtab_c = din("qtab_c", [64, TOWN])
    qtab_s = din("qtab_s", [64, TOWN])
    mcur = din("mcur", [128, 8, 128])
    mprev = din("mprev", [128, 32, 128])
    mcurS = din("mcurS", [64, 4, 64])
    mprevS = din("mprevS", [32, 4, 64])
    kmask = din("kmask", [16, SEQ])
    qmask = din("qmask", [16, 1024])

    y_own = dout("y_own", [TOWN, D])
    lat_own = dout("lat_own", [TOWN, 512])
    kpe_own = dout("kpe_own", [TOWN, 64])
    poolp_out = dout("poolp_out", [15, 1024])
    pools_out = dout("pools_out", [2, 15, 1024])
    memk_out = dout("memk_out", [256, 1024])
    memv_out = dout("memv_out", [256, 1024])

    with ExitStack() as st:
        K = Kern(nc, st)
        A = Arena(K, ARENA_BYTES)
        pst = st.enter_context(nc.psum_tensor("pst", [128, 8, 512], F32))
        PB = [Buf(pst, f"bank{i}", psum=True, ap=pst[:, i, :]) for i in range(8)]

        def bfv(bank, n=128):
            return PB[bank][0:n, :].bitcast(BF16)

        A.at(0)
        ident_b = A.alloc("ident", [128, 128], BF16)
        ones_b = A.alloc("ones", [128, 128], BF16)
        ones_f = A.alloc("onesf", [128, 128])
        eps6 = A.alloc("eps6", [128, 1])
        eps192 = A.alloc("eps192", [128, 1])
        gqk = A.alloc("gqk", [128, 1])
        gq_n = A.alloc("gq_n", [128, 1])
        gk_n = A.alloc("gk_n", [128, 1])
        gq_r = A.alloc("gq_r", [64, 1])
        gq_rp = A.alloc("gq_rp", [64, 1])
        gbc = A.alloc("gbc", [128, D])
        cnew = A.alloc("cnew", [64, 512], BF16)
        kpnew = A.alloc("kpnew", [64, 64])
        sspe_new = A.alloc("sspe_new", [64, 1])
        sq_last = A.alloc("sq_last", [128, 32], BF16)
        ssp = A.alloc("ssp", [128, 9, 8])
        rstd2 = A.alloc("rstd2", [128, 9])
        ssA = A.alloc("ssA", [128, 9])
        assert A.off <= 14336
        A.at(14336)
        memkT_p = A.alloc("memkT_p", [128, 8, 256], BF16)
        memv_p = A.alloc("memv_p", [128, 2, 1024], BF16)
        assert A.off == 22528
        OT_OFF = 22528
        R1 = 57344
        R2 = 92160
        R3 = 142304

        K.dma(ident_b[:], ident, writes=[ident_b], q='pool')
        def late_consts():
            K.dma(gbc[:], g_mix.partition_broadcast(128), writes=[gbc])
            K.dma(gq_n[:], g_q_head[0:128].rearrange("(p o) -> p o", o=1), writes=[gq_n])
            K.dma(gk_n[:], g_k_head[0:128].rearrange("(p o) -> p o", o=1), writes=[gk_n])
            K.dma(gq_r[:], g_q_head[128:192].rearrange("(p o) -> p o", o=1), writes=[gq_r])
            K.dma(gq_rp[0:32, :], g_q_head[160:192].rearrange("(p o) -> p o", o=1), writes=[gq_rp])
            K.dma(gq_rp[32:64, :], g_q_head[128:160].rearrange("(p o) -> p o", o=1), writes=[gq_rp], skip_wait=True)
            K.op('dve', lambda e: e.tensor_tensor(out=gqk[:], in0=gq_n[:], in1=gk_n[:], op=ALU.mult),
                 reads=[gq_n, gk_n], writes=[gqk])

        K.op('dve', lambda e: e.memset(ones_f[:], 1.0), writes=[ones_f])
        K.op('dve', lambda e: e.tensor_copy(out=ones_b[:], in_=ones_f[:]), reads=[ones_f], writes=[ones_b])
        K.op('dve', lambda e: e.memset(eps6[:], EPS), writes=[eps6])
        K.op('dve', lambda e: e.memset(eps192[:], EPS * 192.0), writes=[eps192])

        def rms_stats(src_ap, n, width, junk_ap, ss, rstd, reads, eps_b=eps6, extra_w=()):
            K.op('act', lambda e: e.activation(out=junk_ap, in_=src_ap, func=AF.Square, accum_out=ss[0:n, :]),
                 reads=reads, writes=[ss] + list(extra_w))
            K.op('act', lambda e: e.activation(out=rstd[0:n, :], in_=ss[0:n, :], func=AF.Sqrt, bias=eps_b[0:n, :],
                                               scale=1.0 / width), reads=[ss, eps_b], writes=[rstd])
            K.op('dve', lambda e: e.reciprocal(out=rstd[0:n, :], in_=rstd[0:n, :]), reads=[rstd], writes=[rstd])

        def transposes(src, src_bufs, n, nch, dst_fn, dst_bufs, banks, cw=128, pb=0, dst_grp=None):
            k = 0
            bi = 0
            while k < nch:
                g = min(4, nch - k)
                bk = banks[bi % len(banks)]
                bi += 1
                pv = bfv(bk, cw)
                for j in range(g):
                    K.op('pe', lambda e, kk=k + j, j=j, pv=pv: e.transpose(
                        out=pv[:, j * 128:j * 128 + n], in_=src[0:n, kk * cw:(kk + 1) * cw], identity=ident_b[pb:pb + n, pb:pb + n]),
                        reads=list(src_bufs) + [ident_b], writes=[PB[bk]], signal=(j == g - 1))
                if dst_grp is not None:
                    K.op(K.ev(), (lambda k0, g, pv: (lambda e: (
                        e.activation(out=dst_grp(k0, g), in_=pv[:, 0:g * 128].rearrange("p (a b) -> p a b", a=g)[:, :, 0:n], func=AF.Copy)
                        if e is nc.scalar else
                        e.tensor_copy(out=dst_grp(k0, g), in_=pv[:, 0:g * 128].rearrange("p (a b) -> p a b", a=g)[:, :, 0:n]))))(k, g, pv),
                        reads=[PB[bk]], writes=dst_bufs)
                else:
                    for j in range(g):
                        K.op(K.ev(), (lambda kk, j, pv: (lambda e: (
                            e.activation(out=dst_fn(kk), in_=pv[:, j * 128:j * 128 + n], func=AF.Copy)
                            if e is nc.scalar else e.tensor_copy(out=dst_fn(kk), in_=pv[:, j * 128:j * 128 + n]))))(k + j, j, pv),
                            reads=[PB[bk]], writes=dst_bufs)
                k += g

        def norm_block(x_dram_rows, n, xb, h, hT, ss, rstd, banks, col0=0):
            K.dma(xb[0:n, :], x_dram_rows, writes=[xb])
            rms_stats(xb[0:n, :], n, D, h[0:n, :], ss, rstd, reads=[xb], extra_w=[h])
            K.op('dve', lambda e: e.scalar_tensor_tensor(out=h[0:n, :], in0=xb[0:n, :], scalar=rstd[0:n, :], in1=gbc[0:n, :],
                                                         op0=ALU.mult, op1=ALU.mult), reads=[xb, rstd, gbc], writes=[h])
            transposes(h, [h], n, 16, lambda k: hT[:, k, col0:col0 + n], [hT], banks,
                       dst_grp=lambda k0, g: hT[:, k0:k0 + g, col0:col0 + n])

        def mm_tm(out_ap, bank, lhs_fn, rhs_fn, nk, reads):
            for k in range(nk):
                K.op('pe', lambda e, k=k: e.matmul(out_ap, lhsT=lhs_fn(k), rhs=rhs_fn(k), start=(k == 0), stop=(k == nk - 1)),
                     reads=reads, writes=[PB[bank]], signal=(k == nk - 1))

        A.at(OT_OFF)
        gmem_bc = A.alloc("gmem_bc", [128, D])
        gmk_bc = A.alloc("gmk_bc", [128, 256])
        xb0s = [A.alloc(f"xb0{i}", [128, D]) for i in range(2)]
        h0 = A.alloc("h0", [128, D], BF16)
        memhT = A.alloc("memhT", [128, 16, 256], BF16)
        wt = [A.alloc(f"wmkv{i}", [128, 16, 512], BF16) for i in range(2)]
        mkf = A.alloc("mkf", [128, 2, 1024])
        mvf = A.alloc("mvf", [128, 2, 1024])
        mkb = A.alloc("mkb", [128, 2, 1024], BF16)
        sq4 = A.alloc("sq4", [128, 1024])
        ss4 = A.alloc("ss4", [128, 4])
        r4 = A.alloc("r4", [128, 4])
        ssm = A.alloc("ssm", [128, 1])
        rsm = A.alloc("rsm", [128, 1])
        for mb in range(2):
            K.dma(xb0s[mb][:], memp[mb * 128:(mb + 1) * 128, :], writes=[xb0s[mb]])
        K.dma(gmem_bc[:], g_mem.partition_broadcast(128), writes=[gmem_bc])
        K.dma(gmk_bc[:], g_mem_k.partition_broadcast(128), writes=[gmk_bc])
        late_consts()
        for mb in range(2):
            xb0 = xb0s[mb]
            rms_stats(xb0[:], 128, D, h0[:], ssm, rsm, reads=[xb0], extra_w=[h0])
            K.op('dve', lambda e: e.scalar_tensor_tensor(out=h0[:], in0=xb0[:], scalar=rsm[:], in1=gmem_bc[:],
                                                         op0=ALU.mult, op1=ALU.mult), reads=[xb0, rsm, gmem_bc], writes=[h0])
            transposes(h0, [h0], 128, 16, lambda k, mb=mb: memhT[:, k, mb * 128:(mb + 1) * 128], [memhT], [0, 1])
        for ct in range(4):
            wtile = wt[ct % 2]
            K.dma_k(wtile, lambda a, b, wtile=wtile: wtile[:, a:b, :], w_mem_kv[:, ct * 512:(ct + 1) * 512], 16, 4)
            for mb in range(2):
                bank = 2 + (ct * 2 + mb) % 4
                mm_tm(PB[bank][:, :], bank, lambda k, mb=mb: memhT[:, k, mb * 128:(mb + 1) * 128],
                      lambda k, wtile=wtile: wtile[:, k, :], 16, [memhT, wtile])
                dstf = mkf if ct < 2 else mvf
                cc = (ct % 2) * 512
                K.op(K.ev(), (lambda bank, dstf, mb, cc: (lambda e: (
                    e.activation(out=dstf[:, mb, cc:cc + 512], in_=PB[bank][:, :], func=AF.Copy) if e is nc.scalar
                    else e.tensor_copy(out=dstf[:, mb, cc:cc + 512], in_=PB[bank][:, :]))))(bank, dstf, mb, cc),
                    reads=[PB[bank]], writes=[dstf])
        for mb in range(2):
            K.op('dve', lambda e, mb=mb: e.tensor_tensor(out=sq4[:], in0=mkf[:, mb, :], in1=mkf[:, mb, :], op=ALU.mult),
                 reads=[mkf], writes=[sq4])
            K.op('dve', lambda e: e.tensor_reduce(out=ss4[:], in_=sq4[:].rearrange("p (h d) -> p h d", h=4), axis=AX.X, op=ALU.add),
                 reads=[sq4], writes=[ss4])
            K.op('act', lambda e: e.activation(out=r4[:], in_=ss4[:], func=AF.Sqrt, bias=eps6[:], scale=1.0 / 256),
                 reads=[ss4, eps6], writes=[r4])
            K.op('dve', lambda e: e.reciprocal(out=r4[:], in_=r4[:]), reads=[r4], writes=[r4])
            for hh in range(4):
                K.op('dve', lambda e, mb=mb, hh=hh: e.scalar_tensor_tensor(
                    out=mkf[:, mb, hh * 256:(hh + 1) * 256], in0=mkf[:, mb, hh * 256:(hh + 1) * 256], scalar=r4[:, hh:hh + 1],
                    in1=gmk_bc[:], op0=ALU.mult, op1=ALU.mult), reads=[mkf, r4, gmk_bc], writes=[mkf])
            K.op('act', lambda e, mb=mb: e.activation(out=mkb[:, mb, :], in_=mkf[:, mb, :], func=AF.Copy), reads=[mkf], writes=[mkb])
            K.op('dve', lambda e, mb=mb: e.tensor_copy(out=memv_p[:, mb, :], in_=mvf[:, mb, :]), reads=[mvf], writes=[memv_p])
            K.dma(memk_out[mb * 128:(mb + 1) * 128, :], mkf[:, mb, :], reads=[mkf], final=True)
            K.dma(memv_out[mb * 128:(mb + 1) * 128, :], mvf[:, mb, :], reads=[mvf], final=True)

        def build_memkT(src_b, dst):
            for mb in range(2):
                transposes(src_b[:, mb, :], [src_b], 128, 8,
                           lambda c, mb=mb: dst[:, c, mb * 128:(mb + 1) * 128], [dst], [0, 1])

        build_memkT(mkb, memkT_p)
        K.barrier()
        if stage <= 0:
            K.finish()
            return nc

        A.at(OT_OFF)
        oT_all = A.alloc("oT_all", [128, 16, TOWN], BF16)
        assert A.off == R1
        Wa = A.alloc("Wa", [128, 16, 1088], BF16)
        assert A.off == R2
        ckvT = A.alloc("ckvT", [128, 4, 4128], BF16)
        kropeT = A.alloc("kropeT", [128, 4128], BF16)
        sspe = A.alloc("sspe", [128, NKB])
        qlatT = A.alloc("qlatT", [128, 4, TOWN], BF16)
        assert A.off <= R3, A.off
        A.at(R3)
        ktc = A.alloc("ktc", [128, NKB, 64])
        kts = A.alloc("kts", [128, NKB, 64])
        gkr_bc = A.alloc("gkr_bc", [128, 64])
        CH = [dict(), dict(), dict()]
        for c in range(3):
            CH[c]['kpg'] = A.alloc("kpg", [128, 64])
            CH[c]['kt1'] = A.alloc("kt1", [128, 64])
            CH[c]['kt2'] = A.alloc("kt2", [128, 64])
            CH[c]['krb'] = A.alloc("krb", [128, 64], BF16)
            CH[c]['bT'], CH[c]['bL'], CH[c]['bS'], CH[c]['bC'] = [(0, 1, 2, 2), (3, 4, 5, 5), (6, 7, 6, 6)][c]
        PBOFF = A.off
        for c in range(3):
            if c == 2:
                sv_off = A.off
                A.at(OT_OFF)
            CH[c]['xb'] = A.alloc("xb", [128, D])
            CH[c]['hb'] = A.alloc("hb", [128, D], BF16)
            CH[c]['hT'] = A.alloc("hT", [128, 16, 128], BF16)
            CH[c]['cf'] = A.alloc("cf", [128, 512])
            CH[c]['cb'] = A.alloc("cb", [128, 512], BF16)
            CH[c]['qlb'] = A.alloc("qlb", [128, 512], BF16)
            CH[c]['kpf'] = A.alloc("kpf", [128, 64])
            CH[c]['ss'] = [A.alloc("st_ss", [128, 1]) for i in range(3)]
            CH[c]['r'] = [A.alloc("st_r", [128, 1]) for i in range(3)]
            if c == 2:
                assert A.off <= R1
                A.at(sv_off)
        gkv_bc = A.alloc("gkv_bc", [128, 512])
        gql_bc = A.alloc("gql_bc", [128, 512])
        hscr_t = nc.dram_tensor("hT_scr", [128, 16, TOWN + 128], BF16, kind="Internal").ap()
        hscr = Buf(None, "hscr")
        K.op('dve', lambda e: e.memset(sspe[:], 1.0), writes=[sspe])
        K.op('dve', lambda e: e.memset(kropeT[64:128, :], 0.0), writes=[kropeT])
        K.dma(kropeT[64:80, 0:SEQ], kmask, writes=[kropeT], q='pool')
        WaKV = Buf(A.t, "WaKV", ap=Wa._ap)
        WaQL = Buf(A.t, "WaQL", ap=Wa._ap)
        K.dma_k(WaKV, lambda a, b: Wa[:, a:b, 512:1088], w_in[:, OFF_KV:OFF_KV + 576], 16, 4)
        K.dma_k(WaQL, lambda a, b: Wa[:, a:b, 0:512], w_in[:, OFF_QL:OFF_QL + 512], 16, 4)
        K.dma(ktc[:], ktab_c, writes=[ktc])
        K.dma(kts[:], ktab_s, writes=[kts])
        K.dma(gkv_bc[:], g_kv_lat.partition_broadcast(128), writes=[gkv_bc])
        K.dma(gql_bc[:], g_q_lat.partition_broadcast(128), writes=[gql_bc])
        K.dma(gkr_bc[:], g_k_head[128:192].partition_broadcast(128), writes=[gkr_bc])

        def key_side(ch, c_src_ap, c_reads, kpe_src_ap, kpe_reads, n, blk, col0, sspe_dst_ap, sspe_dst_buf, pb=0):
            P = slice(pb, pb + n)
            kpg, kt1, kt2, krb, bS = ch['kpg'], ch['kt1'], ch['kt2'], ch['krb'], ch['bS']
            transposes(c_src_ap, c_reads, n, 4, lambda k: ckvT[:, k, col0:col0 + n], [ckvT], [ch['bC']], pb=pb,
                       dst_grp=lambda k0, g: ckvT[:, k0:k0 + g, col0:col0 + n])
            K.op('act', lambda e: e.activation(out=kt1[P, :], in_=kpe_src_ap, func=AF.Square, accum_out=sspe_dst_ap),
                 reads=kpe_reads, writes=[kt1, sspe_dst_buf])
            K.op('dve', lambda e: e.tensor_tensor(out=kpg[P, :], in0=kpe_src_ap, in1=gkr_bc[P, :], op=ALU.mult),
                 reads=kpe_reads + [gkr_bc], writes=[kpg])
            K.op('dve', lambda e: e.tensor_tensor(out=kt1[P, :], in0=kpg[P, :], in1=ktc[P, blk, :], op=ALU.mult),
                 reads=[kpg, ktc], writes=[kt1])
            K.op('dve', lambda e: e.tensor_tensor(out=kt2[P, 0:32], in0=kpg[P, 32:64], in1=kts[P, blk, 0:32], op=ALU.mult),
                 reads=[kpg, kts], writes=[kt2])
            K.op('dve', lambda e: e.tensor_tensor(out=kt2[P, 32:64], in0=kpg[P, 0:32], in1=kts[P, blk, 32:64], op=ALU.mult),
                 reads=[kpg, kts], writes=[kt2])
            K.op('dve', lambda e: e.tensor_tensor(out=krb[P, :], in0=kt1[P, :], in1=kt2[P, :], op=ALU.add),
                 reads=[kt1, kt2], writes=[krb])
            K.op('pe', lambda e: e.transpose(out=bfv(bS, 64)[:, 512:512 + n], in_=krb[P, :], identity=ident_b[P, P]),
                 reads=[krb, ident_b], writes=[PB[bS]])
            K.op('act', lambda e: e.activation(out=kropeT[0:64, col0:col0 + n], in_=bfv(bS, 64)[:, 512:512 + n], func=AF.Copy),
                 reads=[PB[bS]], writes=[kropeT])

        def latents(ch, n):
            hT, bL, bS, cf, cb, kpf = ch['hT'], ch['bL'], ch['bS'], ch['cf'], ch['cb'], ch['kpf']
            mm_tm(PB[bL][0:n, :], bL, lambda k: hT[:, k, 0:n], lambda k: Wa[:, k, 512:1024], 16, [hT, WaKV])
            mm_tm(PB[bS][0:n, 0:64], bS, lambda k: hT[:, k, 0:n], lambda k: Wa[:, k, 1024:1088], 16, [hT, WaKV])
            rms_stats(PB[bL][0:n, :], n, 512, cb[0:n, :], ch['ss'][1], ch['r'][1], reads=[PB[bL]], extra_w=[cb])
            K.op('dve', lambda e: e.scalar_tensor_tensor(out=cf[0:n, :], in0=PB[bL][0:n, :], scalar=ch['r'][1][0:n, :], in1=gkv_bc[0:n, :],
                                                         op0=ALU.mult, op1=ALU.mult), reads=[PB[bL], ch['r'][1], gkv_bc], writes=[cf])
            K.op('act', lambda e: e.activation(out=cb[0:n, :], in_=cf[0:n, :], func=AF.Copy), reads=[cf], writes=[cb])
            K.op('dve', lambda e: e.tensor_copy(out=kpf[0:n, :], in_=PB[bS][0:n, 0:64]), reads=[PB[bS]], writes=[kpf])

        blks = [(xseq[blk * 128:(blk + 1) * 128, :], 128) for blk in range(NB)]
        blks += [(xown[r0:r0 + n, :], n) for (r0, n) in BLOCKS]
        blks += [(xprev[:, :], 128)]
        NBLK = len(blks)
        normed = set()

        def load_x(i):
            rows, n = blks[i]
            xb = CH[i % 3]['xb']
            K.dma(xb[0:n, :], rows, writes=[xb])

        def norm_a(i):
            rows, n = blks[i]
            ch = CH[i % 3]
            xb, h = ch['xb'], ch['hb']
            rms_stats(xb[0:n, :], n, D, h[0:n, :], ch['ss'][0], ch['r'][0], reads=[xb], extra_w=[h])
            K.op('dve', lambda e: e.scalar_tensor_tensor(out=h[0:n, :], in0=xb[0:n, :], scalar=ch['r'][0][0:n, :], in1=gbc[0:n, :],
                                                         op0=ALU.mult, op1=ALU.mult), reads=[xb, ch['r'][0], gbc], writes=[h])
            K.py(lambda: normed.add(i))
            if i + 3 < NBLK:
                load_x(i + 3)

        def head(i):
            rows, n = blks[i]
            ch = CH[i % 3]
            hT = ch['hT']

            def chk():
                assert i in normed, i
            K.py(chk)
            transposes(ch['hb'], [ch['hb']], n, 16, lambda k: hT[:, k, 0:n], [hT], [ch['bT']],
                       dst_grp=lambda k0, g: hT[:, k0:k0 + g, 0:n])
            if i + 3 < NBLK:
                norm_a(i + 3)

        for i in range(3):
            load_x(i)
        for i in range(3):
            norm_a(i)
        recs = []
        for blk in range(NB):
            ch = CH[len(recs) % 3]
            K.record()
            head(len(recs))
            latents(ch, 128)
            key_side(ch, ch['cb'][0:128, :], [ch['cb']], ch['kpf'][0:128, :], [ch['kpf']], 128, blk, blk * 128, sspe[:, blk:blk + 1], sspe)
            recs.append(K.stop_record())

        for bi, (r0, n) in enumerate(BLOCKS):
            ch = CH[len(recs) % 3]
            hT, cf, cb, kpf, qlb, bT, bC = ch['hT'], ch['cf'], ch['cb'], ch['kpf'], ch['qlb'], ch['bT'], ch['bC']
            K.record()
            head(len(recs))
            K.dma(hscr_t[:, :, r0:r0 + n], hT[:, :, 0:n], reads=[hT], writes=[hscr])
            latents(ch, n)
            mm_tm(PB[bT][0:n, :], bT, lambda k, hT=hT, n=n: hT[:, k, 0:n], lambda k: Wa[:, k, 0:512], 16, [hT, WaQL])
            K.dma(lat_own[r0:r0 + n, :], cf[0:n, :], reads=[cf], final=True)
            K.dma(kpe_own[r0:r0 + n, :], kpf[0:n, :], reads=[kpf], final=True)
            if bi == 8:
                K.op('dve', lambda e, cb=cb: e.tensor_copy(out=cnew[:, :], in_=cb[0:64, :]), reads=[cb], writes=[cnew])
                K.op('dve', lambda e, kpf=kpf: e.tensor_copy(out=kpnew[:, :], in_=kpf[0:64, :]), reads=[kpf], writes=[kpnew])
            rms_stats(PB[bT][0:n, :], n, 512, qlb[0:n, :], ch['ss'][2], ch['r'][2], reads=[PB[bT]], extra_w=[qlb])
            K.op('dve', lambda e, ch=ch, qlb=qlb, bT=bT, n=n: e.scalar_tensor_tensor(
                out=qlb[0:n, :], in0=PB[bT][0:n, :], scalar=ch['r'][2][0:n, :], in1=gql_bc[0:n, :],
                op0=ALU.mult, op1=ALU.mult), reads=[PB[bT], ch['r'][2], gql_bc], writes=[qlb])
            transposes(qlb, [qlb], n, 4, lambda k, r0=r0, n=n: qlatT[:, k, r0:r0 + n], [qlatT], [bC],
                       dst_grp=lambda k0, g, r0=r0, n=n: qlatT[:, k0:k0 + g, r0:r0 + n])
            recs.append(K.stop_record())
        ch = CH[len(recs) % 3]
        K.record()
        head(len(recs))
        K.dma(hscr_t[:, :, TOWN:TOWN + 128], ch['hT'][:, :, :], reads=[ch['hT']], writes=[hscr])
        recs.append(K.stop_record())
        assert len(recs) == NBLK
        K.replay(recs, width=3, stagger=0.33)
        K.barrier()
        if stage <= 1:
            K.finish()
            return nc

        A.at(R1)
        V4 = A.alloc("V4", [128, NKB, 512], BF16)
        A.at(PBOFF)
        knT = A.alloc("knT", [128, 4128], BF16)
        qnT = A.alloc("qnT", [128, TOWN], BF16)
        qrT = A.alloc("qrT", [128, TOWN], BF16)
        Tc = A.alloc("Tc", [64, TOWN])
        Ts = A.alloc("Ts", [64, TOWN])
        pTs = [A.alloc(f"pT{i}", [128, 512], BF16) for i in range(3)]
        sqall = Buf(A.t, "sqall", ap=gbc._ap.bitcast(BF16))
        sqn = A.alloc("sqn", [128, 512], BF16)
        sqr = A.alloc("sqr", [64, 512], BF16)
        off_scr = A.off
        rrep = A.alloc("rrep", [128, 512])
        rrec = A.alloc("rrec", [128, 512])
        t1 = A.alloc("t1", [64, 512])
        t2 = A.alloc("t2", [64, 512])
        assert A.off == off_scr + 8192
        lat4 = [Buf(A.t, f"lat4_{i}", ap=A.t[0:128, (off_scr + i * 4096) // 4:(off_scr + (i + 1) * 4096) // 4].bitcast(BF16)
                    .rearrange("p (a b) -> p a b", a=4)) for i in range(2)]
        kpg4, kt1_4, kt2_4 = [Buf(A.t, f"ks4_{i}", ap=pTs[i]._ap.bitcast(F32).rearrange("p (a b) -> p a b", a=4)) for i in range(3)]
        rks = A.alloc("rks", [128, NKB])
        rkt = A.alloc("rkt", [128, NKB])
        KSPL = 24
        rksP = [Buf(A.t, f"rks{i}", ap=rks._ap) for i in range(2)]
        rktP = [Buf(A.t, f"rkt{i}", ap=rkt._ap) for i in range(2)]
        sqP = [Buf(A.t, f"sqP{i}", ap=sqall._ap) for i in range(2)]
        recip = A.alloc("recip", [128, 512])
        recips = [recip, rrep]
        wq_h = A.alloc("wq_h", [128, 4, 192], BF16)
        wrot_h = A.alloc("wrot_h", [128, 4, 64], BF16)
        wk_h = A.alloc("wk_h", [128, 4, 128], BF16)
        wv_g = A.alloc("wv_g", [128, 4, 512], BF16)
        kp4 = [A.alloc(f"kp4_{i}", [128, 4, 64]) for i in range(2)]
        krb4 = [A.alloc(f"krb4_{i}", [128, 4, 64], BF16) for i in range(2)]
        K.op('dve', lambda e: e.memset(qrT[64:128, :], 0.0), writes=[qrT])
        K.dma(qrT[64:80, 0:1024], qmask, writes=[qrT], q='pool')
        K.dma(Tc[:], qtab_c, writes=[Tc])
        K.dma(Ts[:], qtab_s, writes=[Ts])
        K.op('dve', lambda e: e.tensor_scalar_mul(out=Tc[:], in0=Tc[:], scalar1=gq_r[:, 0:1]), reads=[Tc, gq_r], writes=[Tc])
        K.op('dve', lambda e: e.tensor_scalar_mul(out=Ts[:], in0=Ts[:], scalar1=gq_rp[:, 0:1]), reads=[Ts, gq_rp], writes=[Ts])

        pT_i = [0]
        sc_i = [0]

        def attention_head(h, prob):
            hh = h % 4
            if prob == 0:
                nkb, nkeys = NB, SEQ
                qchunks = [(0, 512), (512, 512)]
            else:
                nkb, nkeys = NKB, 4128
                qchunks = [(1024 + 32 * (prob - 1), 32)]
            def v_build():
                for blk in range(nkb):
                    n = min(128, nkeys - blk * 128)
                    bank = 5 + blk % 2
                    mm_tm(PB[bank][0:n, :], bank, lambda k, blk=blk, n=n: ckvT[:, k, blk * 128:blk * 128 + n],
                          lambda k: wv_g[:, k, :], 4, [ckvT, wv_g])
                    K.op(K.ev(), (lambda bank, blk, n: (lambda e: (
                        e.activation(out=V4[0:n, blk, :], in_=PB[bank][0:n, :], func=AF.Copy) if e is nc.scalar
                        else e.tensor_copy(out=V4[0:n, blk, :], in_=PB[bank][0:n, :]))))(bank, blk, n), reads=[PB[bank]], writes=[V4])
                g2 = (h // 4 + 1) % 4
                K.dma(wv_g[:], w_vb[:, g2 * 512:(g2 + 1) * 512].rearrange("(k p) n -> p k n", p=128), writes=[wv_g], q='pool')

            def q_mm(c0, w):
                mm_tm(PB[0][:, 0:w], 0, lambda k: wq_h[:, k, 0:128], lambda k: qlatT[:, k, c0:c0 + w], 4, [wq_h, qlatT])
                mm_tm(PB[1][0:64, 0:w], 1, lambda k: wq_h[:, k, 128:192], lambda k: qlatT[:, k, c0:c0 + w], 4, [wq_h, qlatT])
                mm_tm(PB[2][0:64, 0:w], 2, lambda k: wrot_h[:, k, :], lambda k: qlatT[:, k, c0:c0 + w], 4, [wrot_h, qlatT])
                K.op('act', lambda e: e.activation(out=sqn[:, 0:w], in_=PB[0][:, 0:w], func=AF.Square), reads=[PB[0]], writes=[sqn])
                K.op('act', lambda e: e.activation(out=sqr[:, 0:w], in_=PB[1][0:64, 0:w], func=AF.Square), reads=[PB[1]], writes=[sqr])

            def q_fin(c0, w):
                K.op('pe', lambda e: e.matmul(PB[3][:, 0:w], lhsT=ones_b[:, :], rhs=sqn[:, 0:w], start=True, stop=False),
                     reads=[ones_b, sqn], writes=[PB[3]], signal=False)
                K.op('pe', lambda e: e.matmul(PB[3][:, 0:w], lhsT=ones_b[0:64, :], rhs=sqr[:, 0:w], start=False, stop=True),
                     reads=[ones_b, sqr], writes=[PB[3]])
                K.op('act', lambda e: e.activation(out=rrep[:, 0:w], in_=PB[3][:, 0:w], func=AF.Ln, bias=eps6[:], scale=1.0 / 192),
                     reads=[PB[3], eps6], writes=[rrep])
                K.op('act', lambda e: e.activation(out=rrec[:, 0:w], in_=rrep[:, 0:w], func=AF.Exp, scale=-0.5), reads=[rrep], writes=[rrec])
                K.op('dve', lambda e: e.scalar_tensor_tensor(out=qnT[:, c0:c0 + w], in0=PB[0][:, 0:w], scalar=gqk[:, 0:1],
                                                             in1=rrec[:, 0:w], op0=ALU.mult, op1=ALU.mult),
                     reads=[PB[0], gqk, rrec], writes=[qnT])
                K.op('dve', lambda e: e.tensor_tensor(out=t1[:, 0:w], in0=PB[1][0:64, 0:w], in1=Tc[:, c0:c0 + w], op=ALU.mult),
                     reads=[PB[1], Tc], writes=[t1])
                K.op('dve', lambda e: e.tensor_tensor(out=t2[:, 0:w], in0=PB[2][0:64, 0:w], in1=Ts[:, c0:c0 + w], op=ALU.mult),
                     reads=[PB[2], Ts], writes=[t2])
                K.op('dve', lambda e: e.tensor_tensor(out=t1[:, 0:w], in0=t1[:, 0:w], in1=t2[:, 0:w], op=ALU.add),
                     reads=[t1, t2], writes=[t1])
                K.op('dve', lambda e: e.tensor_tensor(out=qrT[0:64, c0:c0 + w], in0=t1[:, 0:w], in1=rrec[0:64, 0:w], op=ALU.mult),
                     reads=[t1, rrec], writes=[qrT])

            nkc = (nkeys + 511) // 512

            KBANKS = [4, 5, 6]

            def k_mm(kc):
                c0 = kc * 512
                w = min(512, nkeys - c0)
                bank = KBANKS[kc % 3]
                mm_tm(PB[bank][:, 0:w], bank, lambda k: wk_h[:, k, :], lambda k: ckvT[:, k, c0:c0 + w], 4, [wk_h, ckvT])

            def k_fin(kc):
                c0 = kc * 512
                w = min(512, nkeys - c0)
                bank = KBANKS[kc % 3]
                sqb, sqd = (sqP[0 if kc < KSPL // 4 else 1], sqall[:, c0:c0 + w]) if kc < 8 else (sq_last, sq_last[:, 0:w])
                K.op('act', lambda e: e.activation(out=sqd, in_=PB[bank][:, 0:w], func=AF.Square), reads=[PB[bank]], writes=[sqb])
                K.op('dve', lambda e: e.tensor_copy(out=knT[:, c0:c0 + w], in_=PB[bank][:, 0:w]), reads=[PB[bank]], writes=[knT])

            def k_ss(b_lo, b_hi):
                for blk in range(b_lo, b_hi):
                    nb_ = min(128, nkeys - blk * 128)
                    sqb, src = (sqP[0 if blk < KSPL else 1], sqall[:, blk * 128:blk * 128 + nb_]) if blk < 32 else (sq_last, sq_last[:, 0:nb_])
                    K.op('pe', lambda e, nb_=nb_, blk=blk, src=src: e.matmul(
                        PB[7][0:nb_, blk:blk + 1], lhsT=src, rhs=ones_b[:, 0:1], start=True, stop=True),
                        reads=[sqb, ones_b], writes=[PB[7]], signal=(blk == b_hi - 1))

            def rk_chain(pr, c_lo, c_hi, part):
                rkt_, rks_ = rktP[part], rksP[part]
                K.op('dve', lambda e: e.tensor_tensor(
                    out=rkt[0:pr, c_lo:c_hi], in0=PB[7][0:pr, c_lo:c_hi], in1=sspe[0:pr, c_lo:c_hi], op=ALU.add),
                    reads=[PB[7], sspe], writes=[rkt_])
                K.op('act', lambda e: e.activation(
                    out=rkt[0:pr, c_lo:c_hi], in_=rkt[0:pr, c_lo:c_hi], func=AF.Ln, bias=eps192[0:pr, :], scale=1.0),
                    reads=[rkt_, eps192], writes=[rkt_])
                K.op('act', lambda e: e.activation(
                    out=rks[0:pr, c_lo:c_hi], in_=rkt[0:pr, c_lo:c_hi], func=AF.Exp, scale=-0.5), reads=[rkt_], writes=[rks_])

            def k_seq(a, b):
                if a >= b:
                    return
                k_mm(a)
                for kc in range(a, b):
                    if kc + 1 < b:
                        k_mm(kc + 1)
                    k_fin(kc)

            k_seq(0, 3)
            q_mm(*qchunks[0])
            k_seq(3, 6)
            q_fin(*qchunks[0])
            k_seq(6, nkc)
            k_ss(0, KSPL)
            k_ss(KSPL, nkb)
            rk_chain(128, 0, KSPL, 0)
            rk_chain(128, KSPL, NB, 1)
            if nkb == NKB:
                rk_chain(32, NB, NKB, 1)
            if hh == 0:
                v_build()
            for qc in qchunks[1:]:
                q_mm(*qc)
                q_fin(*qc)
            h2 = (h + 1) % 16
            K.dma(wk_h[:], w_kb[:, h2 * 128:(h2 + 1) * 128].rearrange("(k p) n -> p k n", p=128), writes=[wk_h], q='pool')
            K.dma(wq_h[:], w_qb[:, h2 * 192:(h2 + 1) * 192].rearrange("(k p) n -> p k n", p=128), writes=[wq_h], q='pool')
            K.dma(wrot_h[:], w_qbrot[:, h2 * 64:(h2 + 1) * 64].rearrange("(k p) n -> p k n", p=128), writes=[wrot_h], q='pool')
            DEPTH = 2
            scbanks = [4, 7, 6]
            if prob == 0:
                last_kb = [15, 31]
                pairs = []
                for kb in range(NB):
                    s0 = kb // 4
                    for ci in range(2):
                        lo = max(s0 * 128, ci * 512)
                        hi = (ci + 1) * 512
                        if lo >= hi:
                            continue
                        pairs.append((kb, ci, lo, hi, lo - ci * 512, (ci * 512 <= s0 * 128 < hi), s0))

                def a_scores(i):
                    kb, ci, lo, hi, lr, diag, s0 = pairs[i]
                    scb = scbanks[i % 3]
                    K.op('pe', lambda e: e.matmul(PB[scb][:, lr:512], lhsT=knT[:, kb * 128:(kb + 1) * 128], rhs=qnT[:, lo:hi],
                                                  start=True, stop=False), reads=[knT, qnT], writes=[PB[scb]], signal=False)
                    K.op('pe', lambda e: e.matmul(PB[scb][:, lr:512], lhsT=kropeT[:, kb * 128:(kb + 1) * 128], rhs=qrT[:, lo:hi],
                                                  start=False, stop=True), reads=[kropeT, qrT], writes=[PB[scb]])

                def a_rest(i):
                    kb, ci, lo, hi, lr, diag, s0 = pairs[i]
                    scb = scbanks[i % 3]
                    pT = pTs[i % 3]
                    K.op('act', lambda e: e.activation(out=pT[:, lr:512], in_=PB[scb][:, lr:512], func=AF.Exp, scale=rks[:, kb:kb + 1]),
                         reads=[PB[scb], rksP[0 if kb < KSPL else 1]], writes=[pT])
                    K.op('pe', lambda e: e.matmul(PB[ci][:, lr:512], lhsT=V4[:, kb, hh * 128:(hh + 1) * 128], rhs=pT[:, lr:512],
                                                  start=(kb == 0), stop=(kb == last_kb[ci])), reads=[V4, pT], writes=[PB[ci]], signal=False)
                    K.op('pe', lambda e: e.matmul(PB[2 + ci][:, lr:512], lhsT=ones_b[:, :], rhs=pT[:, lr:512],
                                                  start=(kb == 0), stop=(kb == last_kb[ci])), reads=[ones_b, pT], writes=[PB[2 + ci]])

                npairs = len(pairs)
                for i in range(min(DEPTH, npairs)):
                    a_scores(i)
                for i in range(npairs):
                    if i + DEPTH < npairs:
                        a_scores(i + DEPTH)
                    a_rest(i)
                for ci in range(2):
                    rb_ = recips[ci]
                    K.op('act', lambda e, ci=ci, rb_=rb_: e.activation(out=rb_[:, :], in_=PB[2 + ci][:, :], func=AF.Ln), reads=[PB[2 + ci]], writes=[rb_])
                    K.op('act', lambda e, rb_=rb_: e.activation(out=rb_[:, :], in_=rb_[:, :], func=AF.Exp, scale=-1.0), reads=[rb_], writes=[rb_])
                    K.op('dve', lambda e, ci=ci, rb_=rb_: e.tensor_tensor(out=oT_all[:, h, ci * 512:(ci + 1) * 512], in0=PB[ci][:, :], in1=rb_[:, :],
                                                                 op=ALU.mult), reads=[PB[ci], rb_], writes=[oT_all])
            else:
                c0 = 1024 + 32 * (prob - 1)

                def s_scores(kb):
                    nk = min(128, 4128 - kb * 128)
                    scb = scbanks[kb % 3]
                    K.op('pe', lambda e: e.matmul(PB[scb][0:nk, 0:32], lhsT=knT[:, kb * 128:kb * 128 + nk], rhs=qnT[:, c0:c0 + 32],
                                                  start=True, stop=False), reads=[knT, qnT], writes=[PB[scb]], signal=False)
                    K.op('pe', lambda e: e.matmul(PB[scb][0:nk, 0:32], lhsT=kropeT[:, kb * 128:kb * 128 + nk], rhs=qrT[:, c0:c0 + 32],
                                                  start=False, stop=True), reads=[kropeT, qrT], writes=[PB[scb]])

                def s_rest(kb):
                    nk = min(128, 4128 - kb * 128)
                    scb = scbanks[kb % 3]
                    pT = pTs[kb % 3]
                    K.op('act', lambda e: e.activation(out=pT[0:nk, 0:32], in_=PB[scb][0:nk, 0:32], func=AF.Exp, scale=rks[0:nk, kb:kb + 1]),
                         reads=[PB[scb], rksP[0 if kb < KSPL else 1]], writes=[pT])
                    K.op('pe', lambda e: e.matmul(PB[0][:, 0:32], lhsT=V4[0:nk, kb, hh * 128:(hh + 1) * 128], rhs=pT[0:nk, 0:32],
                                                  start=(kb == 0), stop=(kb == NKB - 1)), reads=[V4, pT], writes=[PB[0]], signal=False)
                    K.op('pe', lambda e: e.matmul(PB[2][:, 0:32], lhsT=ones_b[0:nk, :], rhs=pT[0:nk, 0:32],
                                                  start=(kb == 0), stop=(kb == NKB - 1)), reads=[ones_b, pT], writes=[PB[2]])

                for kb in range(DEPTH):
                    s_scores(kb)
                for kb in range(NKB):
                    if kb + DEPTH < NKB:
                        s_scores(kb + DEPTH)
                    s_rest(kb)
                K.op('act', lambda e: e.activation(out=recip[:, 0:32], in_=PB[2][:, 0:32], func=AF.Ln), reads=[PB[2]], writes=[recip])
                K.op('act', lambda e: e.activation(out=recip[:, 0:32], in_=recip[:, 0:32], func=AF.Exp, scale=-1.0), reads=[recip], writes=[recip])
                K.op('dve', lambda e: e.tensor_tensor(out=oT_all[:, h, c0:c0 + 32], in0=PB[0][:, 0:32], in1=recip[:, 0:32], op=ALU.mult),
                     reads=[PB[0], recip], writes=[oT_all])

        def key_side4a(bi, s4):
            c0 = s4 * 512
            b0 = s4 * 4
            lt, kp, kr = lat4[s4 % 2], kp4[s4 % 2], krb4[s4 % 2]
            K.dma(lt[:], latc[bi, c0:c0 + 512, :].rearrange("(a p) n -> p a n", p=128), writes=[lt], q='pool')
            K.dma(kp[:], kpec[bi, c0:c0 + 512, :].rearrange("(a p) n -> p a n", p=128), writes=[kp])
            for k in range(4):
                pv = bfv(k)
                for a in range(4):
                    K.op('pe', lambda e, k=k, a=a, pv=pv: e.transpose(out=pv[:, a * 128:(a + 1) * 128], in_=lt[:, a, k * 128:(k + 1) * 128],
                                                                     identity=ident_b[:, :]),
                         reads=[lt, ident_b], writes=[PB[k]], signal=(a == 3))
                K.op(K.ev(), (lambda k, pv: (lambda e: (
                    e.activation(out=ckvT[:, k, c0:c0 + 512], in_=pv[:, 0:512], func=AF.Copy) if e is nc.scalar
                    else e.tensor_copy(out=ckvT[:, k, c0:c0 + 512], in_=pv[:, 0:512]))))(k, pv), reads=[PB[k]], writes=[ckvT])
            for a in range(4):
                K.op('act', lambda e, a=a: e.activation(out=kt2_4[:, a, :], in_=kp[:, a, :], func=AF.Square, accum_out=sspe[:, b0 + a:b0 + a + 1]),
                     reads=[kp], writes=[kt2_4, sspe])
            for a in range(4):
                K.op('dve', lambda e, a=a: e.tensor_tensor(out=kpg4[:, a, :], in0=kp[:, a, :], in1=gkr_bc[:, :], op=ALU.mult),
                     reads=[kp, gkr_bc], writes=[kpg4])
            K.op('dve', lambda e: e.tensor_tensor(out=kt1_4[:, :, :], in0=kpg4[:, :, :], in1=ktc[:, b0:b0 + 4, :], op=ALU.mult),
                 reads=[kpg4, ktc], writes=[kt1_4])
            K.op('dve', lambda e: e.tensor_tensor(out=kt2_4[:, :, 0:32], in0=kpg4[:, :, 32:64], in1=kts[:, b0:b0 + 4, 0:32], op=ALU.mult),
                 reads=[kpg4, kts], writes=[kt2_4])
            K.op('dve', lambda e: e.tensor_tensor(out=kt2_4[:, :, 32:64], in0=kpg4[:, :, 0:32], in1=kts[:, b0:b0 + 4, 32:64], op=ALU.mult),
                 reads=[kpg4, kts], writes=[kt2_4])
            K.op('dve', lambda e: e.tensor_tensor(out=kr[:, :, :], in0=kt1_4[:, :, :], in1=kt2_4[:, :, :], op=ALU.add),
                 reads=[kt1_4, kt2_4], writes=[kr])

        def key_side4b(s4):
            c0 = s4 * 512
            kr = krb4[s4 % 2]
            pr = bfv(4, 64)
            for a in range(4):
                K.op('pe', lambda e, a=a: e.transpose(out=pr[:, a * 128:(a + 1) * 128], in_=kr[:, a, :], identity=ident_b[:, :]),
                     reads=[kr, ident_b], writes=[PB[4]], signal=(a == 3))
            K.op('act', lambda e: e.activation(out=kropeT[0:64, c0:c0 + 512], in_=pr[:, 0:512], func=AF.Copy), reads=[PB[4]], writes=[kropeT])

        K.dma(wv_g[:], w_vb[:, 0:512].rearrange("(k p) n -> p k n", p=128), writes=[wv_g], q='pool')
        K.dma(wk_h[:], w_kb[:, 0:128].rearrange("(k p) n -> p k n", p=128), writes=[wk_h], q='pool')
        K.dma(wq_h[:], w_qb[:, 0:192].rearrange("(k p) n -> p k n", p=128), writes=[wq_h], q='pool')
        K.dma(wrot_h[:], w_qbrot[:, 0:64].rearrange("(k p) n -> p k n", p=128), writes=[wrot_h], q='pool')
        for prob in range(3):
            if prob > 0:
                bi = prob - 1
                K.barrier()
                for s4 in range(8):
                    key_side4a(bi, s4)
                    if s4 >= 1:
                        key_side4b(s4 - 1)
                key_side4b(7)
                pb = 32 * bi
                key_side(CH[0], cnew[pb:pb + 32, :], [cnew], kpnew[pb:pb + 32, :], [kpnew], 32, 32, 4096,
                         sspe_new[pb:pb + 32, 0:1], sspe_new, pb=pb)
                K.dma(sspe[0:32, 32:33], sspe_new[pb:pb + 32, 0:1], reads=[sspe_new], writes=[sspe])
                K.barrier()
            for h in range(16):
                attention_head(h, prob)
        K.barrier()
        if stage <= 2:
            K.finish()
            return nc
        A.at(R1)
        hT_own = A.alloc("hT_own", [128, 16, TOWN], BF16)
        assert A.off == R2
        dT = A.alloc("dT", [128, 8, TOWN], BF16)
        a_memT = A.alloc("a_memT", [128, 8, TOWN], BF16)
        C_T = A.off
        assert C_T == 126976
        A.at(C_T)
        u_tm = A.alloc("u_tm", [128, 10, 1024], BF16)
        hT_prev = A.alloc("hT_prev", [128, 16, 128], BF16)
        spool_b = A.alloc("spool_b", [32, 1024], BF16)
        mcur_b = A.alloc("mcur_b", [128, 8, 128], BF16)
        mprev_b = A.alloc("mprev_b", [128, 32, 128], BF16)
        mcurS_b = A.alloc("mcurS_b", [64, 4, 64], BF16)
        mprevS_b = A.alloc("mprevS_b", [32, 4, 64], BF16)
        wpool_b = A.alloc("wpool_b", [128, 8, 512], BF16)
        uf = A.alloc("uf", [128, 2, 1024])
        c_ss = A.alloc("c_ss", [128, 1])
        c_r = A.alloc("c_r", [128, 1])
        C1_T = A.off
        xbc = [A.alloc(f"xbc{i}", [128, D]) for i in range(2)]
        hbc = A.alloc("hbc", [128, D], BF16)
        for i in range(4):
            K.dma(hT_own[:, 4 * i:4 * i + 4, :], hscr_t[:, 4 * i:4 * i + 4, 0:TOWN], reads=[hscr], writes=[hT_own], skip_wait=(i > 0))
        K.dma(hT_prev[:, :, :], hscr_t[:, :, TOWN:TOWN + 128], reads=[hscr], writes=[hT_prev])
        A.at(C1_T)
        wu = [A.alloc(f"wu{i}", [128, 16, 256], BF16) for i in range(2)]
        tblocks = [(s, r0, n) for s, (r0, n) in enumerate(BLOCKS)] + [(9, None, 128)]

        def c1_consts():
            K.dma(spool_b[:], spool, writes=[spool_b], q='pool')
            K.dma(mcur_b[:], mcur, writes=[mcur_b], q='pool')
            K.dma(mprev_b[:], mprev, writes=[mprev_b], q='pool')
            K.dma(mcurS_b[:], mcurS, writes=[mcurS_b], q='pool')
            K.dma(mprevS_b[:], mprevS, writes=[mprevS_b], q='pool')
        for ct in range(4):
            wt_ = wu[ct % 2]
            K.dma_k(wt_, lambda a, b, wt_=wt_: wt_[:, a:b, :], w_in[:, OFF_U + ct * 256:OFF_U + (ct + 1) * 256], 16, 2)
            if ct == 1:
                c1_consts()
            for ti, (slot, r0, n) in enumerate(tblocks):
                bank = 2 + ti % 4
                if slot == 9:
                    lf = lambda k: hT_prev[:, k, :]
                    rd = [hT_prev, wt_]
                else:
                    lf = lambda k, r0=r0, n=n: hT_own[:, k, r0:r0 + n]
                    rd = [hT_own, wt_]
                mm_tm(PB[bank][0:n, 0:256], bank, lf, lambda k, wt_=wt_: wt_[:, k, :], 16, rd)
                K.op('act', lambda e, bank=bank, n=n, slot=slot, ct=ct: e.activation(
                    out=u_tm[0:n, slot, ct * 256:(ct + 1) * 256], in_=PB[bank][0:n, 0:256], func=AF.Copy), reads=[PB[bank]], writes=[u_tm])
                if slot in (7, 8):
                    K.op('dve', lambda e, bank=bank, n=n, slot=slot, ct=ct: e.tensor_copy(
                        out=uf[0:n, slot - 7, ct * 256:(ct + 1) * 256], in_=PB[bank][0:n, 0:256]), reads=[PB[bank]], writes=[uf])
        K.dma(poolp_out[:, :], uf[113:128, 0, :], reads=[uf], final=True)
        K.dma(pools_out[0], uf[17:32, 1, :], reads=[uf], final=True)
        K.dma(pools_out[1], uf[49:64, 1, :], reads=[uf], final=True)
        for s, (r0, n) in enumerate(BLOCKS):
            for half in range(2):
                bank = 6 + (s * 2 + half) % 2
                for q4 in range(4):
                    c8 = half * 4 + q4
                    g = c8 // 2
                    o = PB[bank][:, q4 * 128:q4 * 128 + n]
                    if s < 8:
                        idx = g if s == 0 else 4 + g
                        K.op('pe', lambda e, o=o, s=s, c8=c8, idx=idx: e.matmul(
                            o, lhsT=u_tm[:, s, c8 * 128:(c8 + 1) * 128], rhs=mcur_b[:, idx, :], start=True, stop=False),
                            reads=[u_tm, mcur_b], writes=[PB[bank]], signal=False)
                        K.op('pe', lambda e, o=o, s=s, c8=c8, g=g: e.matmul(
                            o, lhsT=u_tm[:, 9, c8 * 128:(c8 + 1) * 128], rhs=mprev_b[:, s * 4 + g, :], start=False, stop=True),
                            reads=[u_tm, mprev_b], writes=[PB[bank]], signal=(q4 == 3))
                    else:
                        K.op('pe', lambda e, o=o, c8=c8, g=g: e.matmul(
                            o, lhsT=u_tm[0:64, 8, c8 * 128:(c8 + 1) * 128], rhs=mcurS_b[:, g, :], start=True, stop=False),
                            reads=[u_tm, mcurS_b], writes=[PB[bank]], signal=False)
                        K.op('pe', lambda e, o=o, c8=c8, g=g: e.matmul(
                            o, lhsT=spool_b[:, c8 * 128:(c8 + 1) * 128], rhs=mprevS_b[:, g, :], start=False, stop=True),
                            reads=[spool_b, mprevS_b], writes=[PB[bank]], signal=(q4 == 3))
                K.op(K.ev(), (lambda bank, half, r0, n: (lambda e: (
                    e.activation(out=dT[:, half * 4:half * 4 + 4, r0:r0 + n],
                                 in_=PB[bank][:, :].rearrange("p (a b) -> p a b", a=4)[:, :, 0:n], func=AF.Copy) if e is nc.scalar
                    else e.tensor_copy(out=dT[:, half * 4:half * 4 + 4, r0:r0 + n],
                                       in_=PB[bank][:, :].rearrange("p (a b) -> p a b", a=4)[:, :, 0:n]))))(bank, half, r0, n),
                    reads=[PB[bank]], writes=[dT])
        K.barrier()
        A.at(C_T)
        wmqs = [A.alloc(f"wmq{i}", [128, 16, 512], BF16) for i in range(2)]
        qmT_all = A.alloc("qmT_all", [128, 8, TOWN], BF16)
        sqh = [[A.alloc(f"sqh{i}{j}", [128, 512], BF16) for j in range(2)] for i in range(2)]
        rrc = [A.alloc(f"rrc{i}", [128, 512]) for i in range(2)]
        pTm = A.alloc("pTm", [128, 2, 512], BF16)
        recm = A.alloc("recm", [128, 512])
        gmqT = A.alloc("gmqT", [128, 2])
        cmk_b = A.alloc("cmk_b", [128, 2, 1024], BF16)
        memkT_s = [A.alloc(f"memkT_s{i}", [128, 8, 256], BF16) for i in range(2)]
        memv_s = [A.alloc(f"memv_s{i}", [128, 2, 1024], BF16) for i in range(2)]
        for i in range(2):
            K.dma_k(wmqs[i], lambda a, b, i=i: wmqs[i][:, a:b, :], w_in[:, OFF_MQ + i * 512:OFF_MQ + (i + 1) * 512], 16, 4)
        with nc.allow_non_contiguous_dma(reason="tiny per-partition gain vector"):
            K.dma(gmqT[:], g_mem_q.rearrange("(c p) -> p c", p=128), writes=[gmqT])
        for bi in range(2):
            K.dma(cmk_b[:], cmk[bi].rearrange("(a p) n -> p a n", p=128), writes=[cmk_b], q='pool')
            K.dma(memv_s[bi][:], cmv[bi].rearrange("(a p) n -> p a n", p=128), writes=[memv_s[bi]], q='pool')
            build_memkT(cmk_b, memkT_s[bi])
        MEM_SCALE = 1.0 / 16.0
        c2chunks = [(0, 512), (512, 512), (1024, 64)]
        ui = 0
        for hh in range(4):
            wm = wmqs[hh // 2]
            for (c0, w) in c2chunks:
                st_ = ui % 2
                ui += 1
                bA = (0, 1) if st_ == 0 else (3, 4)
                bS = 2 if st_ == 0 else 5
                for dc in range(2):
                    col0 = (hh % 2) * 256 + dc * 128
                    mm_tm(PB[bA[dc]][:, 0:w], bA[dc], lambda k, wm=wm, col0=col0: wm[:, k, col0:col0 + 128],
                          lambda k, c0=c0, w=w: hT_own[:, k, c0:c0 + w], 16, [wm, hT_own])
                    K.op('act', lambda e, dc=dc, st_=st_, w=w, bA=bA: e.activation(out=sqh[st_][dc][:, 0:w], in_=PB[bA[dc]][:, 0:w], func=AF.Square),
                         reads=[PB[bA[dc]]], writes=[sqh[st_][dc]])
                for dc in range(2):
                    K.op('pe', lambda e, dc=dc, st_=st_, w=w, bS=bS: e.matmul(PB[bS][:, 0:w], lhsT=ones_b[:, :], rhs=sqh[st_][dc][:, 0:w],
                                                                             start=(dc == 0), stop=(dc == 1)),
                         reads=[ones_b, sqh[st_][dc]], writes=[PB[bS]], signal=(dc == 1))
                K.op('act', lambda e, st_=st_, w=w, bS=bS: e.activation(out=rrc[st_][:, 0:w], in_=PB[bS][:, 0:w], func=AF.Ln, bias=eps6[:], scale=1.0 / 256),
                     reads=[PB[bS], eps6], writes=[rrc[st_]])
                K.op('act', lambda e, st_=st_, w=w: e.activation(out=rrc[st_][:, 0:w], in_=rrc[st_][:, 0:w], func=AF.Exp, scale=-0.5),
                     reads=[rrc[st_]], writes=[rrc[st_]])
                for dc in range(2):
                    K.op('dve', lambda e, dc=dc, st_=st_, w=w, c0=c0, hh=hh, bA=bA: e.scalar_tensor_tensor(
                        out=qmT_all[:, hh * 2 + dc, c0:c0 + w], in0=PB[bA[dc]][:, 0:w], scalar=gmqT[:, dc:dc + 1], in1=rrc[st_][:, 0:w],
                        op0=ALU.mult, op1=ALU.mult), reads=[PB[bA[dc]], gmqT, rrc[st_]], writes=[qmT_all])
        units = [(memkT_p, memv_p, hh, c0, 512) for hh in range(4) for c0 in (0, 512)]
        units += [(memkT_s[bi], memv_s[bi], hh, 1024 + 32 * bi, 32) for bi in range(2) for hh in range(4)]

        def m_scores(u):
            mkT, mv, hh, c0, w = units[u]
            sc = (0, 1) if u % 2 == 0 else (2, 3)
            for mb in range(2):
                for dc in range(2):
                    K.op('pe', lambda e, mb=mb, dc=dc: e.matmul(
                        PB[sc[mb]][:, 0:w], lhsT=mkT[:, hh * 2 + dc, mb * 128:(mb + 1) * 128], rhs=qmT_all[:, hh * 2 + dc, c0:c0 + w],
                        start=(dc == 0), stop=(dc == 1)), reads=[mkT, qmT_all], writes=[PB[sc[mb]]], signal=(dc == 1))

        def m_rest(u):
            mkT, mv, hh, c0, w = units[u]
            sc = (0, 1) if u % 2 == 0 else (2, 3)
            for mb in range(2):
                K.op('act', lambda e, mb=mb: e.activation(out=pTm[:, mb, 0:w], in_=PB[sc[mb]][:, 0:w], func=AF.Exp, scale=MEM_SCALE),
                     reads=[PB[sc[mb]]], writes=[pTm])
            for dvc in range(2):
                for mb in range(2):
                    K.op('pe', lambda e, dvc=dvc, mb=mb: e.matmul(
                        PB[4 + dvc][:, 0:w], lhsT=mv[:, mb, hh * 256 + dvc * 128:hh * 256 + (dvc + 1) * 128], rhs=pTm[:, mb, 0:w],
                        start=(mb == 0), stop=(mb == 1)), reads=[mv, pTm], writes=[PB[4 + dvc]], signal=(mb == 1))
            for mb in range(2):
                K.op('pe', lambda e, mb=mb: e.matmul(PB[6][:, 0:w], lhsT=ones_b[:, :], rhs=pTm[:, mb, 0:w], start=(mb == 0), stop=(mb == 1)),
                     reads=[ones_b, pTm], writes=[PB[6]], signal=(mb == 1))
            K.op('act', lambda e: e.activation(out=recm[:, 0:w], in_=PB[6][:, 0:w], func=AF.Ln), reads=[PB[6]], writes=[recm])
            K.op('act', lambda e: e.activation(out=recm[:, 0:w], in_=recm[:, 0:w], func=AF.Exp, scale=-1.0), reads=[recm], writes=[recm])
            for dvc in range(2):
                K.op('dve', lambda e, dvc=dvc: e.tensor_tensor(out=a_memT[:, hh * 2 + dvc, c0:c0 + w], in0=PB[4 + dvc][:, 0:w], in1=recm[:, 0:w],
                                                               op=ALU.mult), reads=[PB[4 + dvc], recm], writes=[a_memT])

        m_scores(0)
        for u in range(len(units)):
            if u + 1 < len(units):
                m_scores(u + 1)
            m_rest(u)
        K.barrier()
        A.at(C_T)
        mergedT = A.alloc("mergedT", [128, 16, TOWN], BF16)
        C3_T = A.off
        wg = [A.alloc(f"wg{i}", [128, 3, 16, 128], BF16) for i in range(2)]
        wmo = [A.alloc(f"wmo{i}", [128, 16, 128], BF16) for i in range(2)]
        wme = [A.alloc(f"wme{i}", [128, 8, 128], BF16) for i in range(2)]
        wpo = [A.alloc(f"wpo{i}", [128, 2, 128], BF16) for i in range(2)]
        gs = A.alloc("gs", [128, 512])
        acc = A.alloc("acc", [128, 512])
        tmpc = A.alloc("tmpc", [128, 512])
        bgT = A.alloc("bgT", [128, 48])
        psT = A.alloc("psT", [128, 16])
        with nc.allow_non_contiguous_dma(reason="tiny per-partition bias/scale vectors"):
            K.dma(bgT[:], b_gate.rearrange("(c p) -> p c", p=128), writes=[bgT])
            K.dma(psT[:], pool_scale.rearrange("(c p) -> p c", p=128), writes=[psT])
        tchunks = [(0, 512), (512, 512), (1024, 64)]
        for cg in range(16):
            x_ = cg % 2
            for br in range(3):
                K.dma(wg[x_][:, br, :, :], w_in[:, OFF_G + br * 2048 + cg * 128:OFF_G + br * 2048 + (cg + 1) * 128].rearrange(
                    "(k p) n -> p k n", p=128), writes=[wg[x_]], q='pool')
            K.dma(wmo[x_][:], w_mla_o[:, cg * 128:(cg + 1) * 128].rearrange("(k p) n -> p k n", p=128), writes=[wmo[x_]], q='pool')
            K.dma(wme[x_][:], w_mem_o[:, cg * 128:(cg + 1) * 128].rearrange("(k p) n -> p k n", p=128), writes=[wme[x_]], q='pool')
            gi = cg // 4
            K.dma(wpo[x_][:], w_pool[gi, :, (cg % 4) * 128:(cg % 4 + 1) * 128].rearrange("(k p) n -> p k n", p=128), writes=[wpo[x_]], q='pool')
            for (c0, w) in tchunks:
                for br in range(3):
                    mm_tm(PB[br][:, 0:w], br, lambda k, br=br: wg[x_][:, br, k, :], lambda k, c0=c0, w=w: hT_own[:, k, c0:c0 + w], 16,
                          [wg[x_], hT_own])
                mm_tm(PB[3][:, 0:w], 3, lambda k: wpo[x_][:, k, :], lambda k, c0=c0, w=w: dT[:, gi * 2 + k, c0:c0 + w], 2, [wpo[x_], dT])
                mm_tm(PB[4][:, 0:w], 4, lambda k: wmo[x_][:, k, :], lambda k, c0=c0, w=w: oT_all[:, k, c0:c0 + w], 16, [wmo[x_], oT_all])
                mm_tm(PB[5][:, 0:w], 5, lambda k: wme[x_][:, k, :], lambda k, c0=c0, w=w: a_memT[:, k, c0:c0 + w], 8, [wme[x_], a_memT])
                K.op('act', lambda e, w=w: e.activation(out=gs[:, 0:w], in_=PB[0][:, 0:w], func=AF.Sigmoid, bias=bgT[:, cg:cg + 1], scale=1.0),
                     reads=[PB[0], bgT], writes=[gs])
                K.op('dve', lambda e, w=w: e.scalar_tensor_tensor(out=acc[:, 0:w], in0=PB[3][:, 0:w], scalar=psT[:, cg:cg + 1], in1=gs[:, 0:w],
                                                                  op0=ALU.mult, op1=ALU.mult), reads=[PB[3], psT, gs], writes=[acc])
                K.op('act', lambda e, w=w: e.activation(out=gs[:, 0:w], in_=PB[1][:, 0:w], func=AF.Sigmoid, bias=bgT[:, 16 + cg:17 + cg], scale=1.0),
                     reads=[PB[1], bgT], writes=[gs])
                K.op('dve', lambda e, w=w: e.tensor_tensor(out=tmpc[:, 0:w], in0=PB[4][:, 0:w], in1=gs[:, 0:w], op=ALU.mult),
                     reads=[PB[4], gs], writes=[tmpc])
                K.op('dve', lambda e, w=w: e.tensor_tensor(out=acc[:, 0:w], in0=acc[:, 0:w], in1=tmpc[:, 0:w], op=ALU.add),
                     reads=[acc, tmpc], writes=[acc])
                K.op('act', lambda e, w=w: e.activation(out=gs[:, 0:w], in_=PB[2][:, 0:w], func=AF.Sigmoid, bias=bgT[:, 32 + cg:33 + cg], scale=1.0),
                     reads=[PB[2], bgT], writes=[gs])
                K.op('dve', lambda e, w=w: e.tensor_tensor(out=tmpc[:, 0:w], in0=PB[5][:, 0:w], in1=gs[:, 0:w], op=ALU.mult),
                     reads=[PB[5], gs], writes=[tmpc])
                K.op('dve', lambda e, c0=c0, w=w: e.tensor_tensor(out=mergedT[:, cg, c0:c0 + w], in0=acc[:, 0:w], in1=tmpc[:, 0:w], op=ALU.add),
                     reads=[acc, tmpc], writes=[mergedT])
        K.barrier()
        A.at(OT_OFF)
        y_acc = A.alloc("y_acc", [128, 9, D])
        assert A.off == 96256
        yb = [Buf(A.t, f"yacc{s}", ap=y_acc._ap[:, s, :]) for s in range(9)]
        wo = [A.alloc(f"wo{i}", [128, 16, 256], BF16) for i in range(2)]
        hbq = [A.alloc(f"hbq{i}", [128, 256], BF16) for i in range(3)]
        sqj = A.alloc("sqj", [128, 256], BF16)
        assert A.off <= C_T
        A.at(C3_T)
        h2T = A.alloc("h2T", [128, 16, TOWN], BF16)
        wup1 = A.alloc("wup1", [128, 16, 256], BF16)
        K.dma(gbc[:], g_ff.partition_broadcast(128), writes=[gbc])
        K.op('dve', lambda e: e.memset(ssp[:], 1.0), writes=[ssp])

        def load_wo(ct):
            K.dma_k(wo[ct % 2], lambda a, b: wo[ct % 2][:, a:b, :], w_out[:, ct * 256:(ct + 1) * 256], 16, 2)

        def load_xres(s):
            r0, n = BLOCKS[s]
            K.dma(yb[s][0:n, :], xown[r0:r0 + n, :], writes=[yb[s]], q='pool')

        load_wo(0)
        for s in range(4):
            load_xres(s)
        load_wo(1)
        for s in range(4, 9):
            load_xres(s)
        K.dma(wup1[:], w_up[:, 0:256].rearrange("(k p) n -> p k n", p=128), writes=[wup1], q='pool')

        groups = [(ct, s) for ct in range(8) for s in range(9)]

        def c4_tail(gi):
            ct, s = groups[gi]
            r0, n = BLOCKS[s]
            bank = 4 + gi % 4
            hb = hbq[gi % 3]
            pv = bfv(bank)
            for j in range(2):
                K.op('pe', lambda e, j=j: e.transpose(out=pv[:, j * 128:j * 128 + n], in_=hb[0:n, j * 128:(j + 1) * 128],
                                                      identity=ident_b[0:n, 0:n]), reads=[hb, ident_b], writes=[PB[bank]], signal=(j == 1))
            K.op('act', lambda e: e.activation(out=h2T[:, 2 * ct:2 * ct + 2, r0:r0 + n],
                                               in_=pv[:, 0:256].rearrange("p (a b) -> p a b", a=2)[:, :, 0:n], func=AF.Copy),
                 reads=[PB[bank]], writes=[h2T])

        for gi, (ct, s) in enumerate(groups):
            r0, n = BLOCKS[s]
            wt_ = wo[ct % 2]
            if s == 0 and ct >= 2:
                load_wo(ct)
            bank = s % 4
            cs = slice(ct * 256, (ct + 1) * 256)
            mm_tm(PB[bank][0:n, 0:256], bank, lambda k, r0=r0, n=n: mergedT[:, k, r0:r0 + n], lambda k, wt_=wt_: wt_[:, k, :], 16,
                  [mergedT, wt_])
            K.op('dve', lambda e, bank=bank, n=n, s=s, cs=cs: e.tensor_tensor(
                out=yb[s][0:n, cs], in0=PB[bank][0:n, 0:256], in1=yb[s][0:n, cs], op=ALU.add),
                reads=[PB[bank], yb[s]], writes=[yb[s]])
            K.op('act', lambda e, n=n, s=s, cs=cs, ct=ct: e.activation(out=sqj[0:n, :], in_=yb[s][0:n, cs], func=AF.Square,
                                                                       accum_out=ssp[0:n, s, ct:ct + 1]),
                 reads=[yb[s]], writes=[sqj, ssp])
            K.op('dve', lambda e, n=n, s=s, cs=cs, gi=gi: e.tensor_tensor(out=hbq[gi % 3][0:n, :], in0=yb[s][0:n, cs], in1=gbc[0:n, cs],
                                                                          op=ALU.mult), reads=[yb[s], gbc], writes=[hbq[gi % 3]])
            if gi >= 2:
                c4_tail(gi - 2)
        c4_tail(len(groups) - 2)
        c4_tail(len(groups) - 1)
        K.op('dve', lambda e: e.tensor_reduce(out=ssA[:], in_=ssp[:], axis=AX.X, op=ALU.add), reads=[ssp], writes=[ssA])
        K.op('act', lambda e: e.activation(out=ssA[:], in_=ssA[:], func=AF.Sqrt, bias=eps6[:], scale=1.0 / D), reads=[ssA, eps6], writes=[ssA])
        K.op('dve', lambda e: e.reciprocal(out=ssA[:], in_=ssA[:]), reads=[ssA], writes=[ssA])
        K.op('dve', lambda e: e.tensor_tensor(out=rstd2[:], in0=ssA[:], in1=ssA[:], op=ALU.mult), reads=[ssA], writes=[rstd2])
        K.barrier()
        A.at(96256)
        fT = A.alloc("fT", [128, 8, TOWN], BF16)
        wdn = [A.alloc(f"wdn{i}", [128, D], BF16) for i in range(8)]
        wup0 = A.alloc("wup0", [128, 16, 256], BF16)
        rl = [A.alloc(f"rl{i}", [128, 512]) for i in range(2)]
        assert A.off <= C3_T, A.off
        wup = [wup1, wup0]
        NT = 32

        def load_wup(T):
            K.dma(wup[T % 2][:], w_up[:, T * 256:(T + 1) * 256].rearrange("(k p) n -> p k n", p=128), writes=[wup[T % 2]], q='pool')

        def load_wdn(G):
            for hc in range(8):
                f_ = G * 8 + hc
                K.dma(wdn[hc][:], w_down[f_ * 128:(f_ + 1) * 128, :], writes=[wdn[hc]], q='pool')

        load_wup(1)
        load_wdn(0)
        ui = 0
        for G in range(8):
            for q4 in range(4):
                T = G * 4 + q4
                wt_ = wup[T % 2]
                for sub in range(2):
                    hc = 2 * q4 + sub
                    for (c0, w) in tchunks:
                        bank = ui % 4
                        r_ = rl[ui % 2]
                        ui += 1
                        mm_tm(PB[bank][:, 0:w], bank, lambda k, wt_=wt_, sub=sub: wt_[:, k, sub * 128:(sub + 1) * 128],
                              lambda k, c0=c0, w=w: h2T[:, k, c0:c0 + w], 16, [wt_, h2T])
                        K.op('act', lambda e, bank=bank, w=w, r_=r_: e.activation(out=r_[:, 0:w], in_=PB[bank][:, 0:w], func=AF.Relu),
                             reads=[PB[bank]], writes=[r_])
                        K.op('dve', lambda e, w=w, r_=r_, hc=hc, c0=c0: e.tensor_tensor(out=fT[:, hc, c0:c0 + w], in0=r_[:, 0:w], in1=r_[:, 0:w],
                                                                                       op=ALU.mult), reads=[r_], writes=[fT])
                if T + 2 < NT:
                    load_wup(T + 2)
            for s, (r0, n) in enumerate(BLOCKS):
                for c4 in range(4):
                    bank = 4 + c4
                    mm_tm(PB[bank][0:n, :], bank, lambda k, r0=r0, n=n: fT[:, k, r0:r0 + n], lambda k, c4=c4: wdn[k][:, c4 * 512:(c4 + 1) * 512], 8,
                          [fT] + wdn)
                    K.op('dve', lambda e, bank=bank, n=n, s=s, c4=c4: e.scalar_tensor_tensor(
                        out=yb[s][0:n, c4 * 512:(c4 + 1) * 512], in0=PB[bank][0:n, :], scalar=rstd2[0:n, s:s + 1],
                        in1=yb[s][0:n, c4 * 512:(c4 + 1) * 512], op0=ALU.mult, op1=ALU.add),
                        reads=[PB[bank], yb[s], rstd2], writes=[yb[s]])
                if G == 7:
                    K.dma(y_own[r0:r0 + n, :], yb[s][0:n, :], reads=[yb[s]], final=True)
            if G + 1 < 8:
                load_wdn(G + 1)
        K.finish()
        print("program built: ninst", K.ninst, "nsem", K.nsem, flush=True)
        return nc


_PROG = {}
POOL_WINDOWS = (2, 4, 8, 16)


def _rope_tables():
    half = 32
    inv = (1.0 / (np.float32(10000.0) ** (np.arange(half, dtype=np.float32) * np.float32(2.0 / 64)))).astype(np.float32)
    return inv


def _consts(j):
    inv = _rope_tables()
    c = {}
    c["ident"] = np.eye(128, dtype=np.float32)
    pos = (np.arange(NKB)[None, :] * 128 + np.arange(128)[:, None]).astype(np.float32)
    pos[:, 32] = 4096 + (np.arange(128) % 32)
    ang = pos[:, :, None] * inv[None, None, :]
    cs, sn = np.cos(ang).astype(np.float32), np.sin(ang).astype(np.float32)
    c["ktab_c"] = np.concatenate([cs, cs], axis=-1)
    c["ktab_s"] = np.concatenate([-sn, sn], axis=-1)
    qpos = np.zeros(TOWN, np.float32)
    for s in range(8):
        qpos[s * 128:(s + 1) * 128] = (4 * s + j) * 128 + np.arange(128)
    qpos[1024:1056] = 4096 + np.arange(32)
    qpos[1056:1088] = 4096 + np.arange(32)
    qa = qpos[None, :] * inv[:, None]
    qc, qs = np.cos(qa).astype(np.float32), np.sin(qa).astype(np.float32)
    c["qtab_c"] = np.concatenate([qc, qc], axis=0)
    c["qtab_s"] = np.concatenate([-qs, qs], axis=0)
    mcur = np.zeros((128, 8, 128), np.float32)
    for g, win in enumerate(POOL_WINDOWS):
        for t in range(128):
            for first in (True, False):
                cnt = min(t + 1, win) if first else win
                idx = g if first else 4 + g
                lo = max(0, t - win + 1)
                mcur[lo:t + 1, idx, t] += 1.0 / cnt
                mcur[t, idx, t] -= 1.0
    if j != 0:
        mcur[:, 0:4, :] = mcur[:, 4:8, :]
    c["mcur"] = mcur
    mprev = np.zeros((128, 32, 128), np.float32)
    for s in range(8):
        for g, win in enumerate(POOL_WINDOWS):
            for t in range(min(win - 1, 128)):
                for ps in range(t - win + 1, 0):
                    mprev[16 * s + 16 + ps, s * 4 + g, t] += 1.0 / win
    if j == 0:
        mprev[:, 0:4, :] = 0.0
    c["mprev"] = mprev
    mcs = np.zeros((64, 4, 64), np.float32)
    mps = np.zeros((32, 4, 64), np.float32)
    for bi in range(2):
        for g, win in enumerate(POOL_WINDOWS):
            for t in range(32):
                lo = max(0, t - win + 1)
                mcs[bi * 32 + lo:bi * 32 + t + 1, g, bi * 32 + t] += 1.0 / win
                mcs[bi * 32 + t, g, bi * 32 + t] -= 1.0
                for ps in range(t - win + 1, 0):
                    mps[bi * 16 + 16 + ps, g, bi * 32 + t] += 1.0 / win
    c["mcurS"] = mcs
    c["mprevS"] = mps
    km = np.zeros((16, SEQ), np.float32)
    qm = np.zeros((16, 1024), np.float32)
    for s in range(8):
        qm[2 * s, s * 128:s * 128 + 64] = -30000.0
        qm[2 * s + 1, s * 128 + 64:(s + 1) * 128] = -30000.0
        for d in range(4):
            kb = 4 * s + d
            if d > j:
                km[2 * s, kb * 128:(kb + 1) * 128] = 1.0
                km[2 * s + 1, kb * 128:(kb + 1) * 128] = 1.0
            elif d == j:
                km[2 * s, kb * 128 + 64:(kb + 1) * 128] = 1.0
    c["kmask"] = km
    c["qmask"] = qm
    return c


def kernel(x_prompt, mem_prompt, x_sample, cache_mla_latent, cache_mla_kpe, state_pool,
           cache_mem_k, cache_mem_v, g_mix, w_in, b_gate, w_pool, pool_scale, g_q_lat, w_qb,
           g_q_head, g_kv_lat, w_kb, w_vb, g_k_head, w_mla_o, g_mem, w_mem_kv, g_mem_q,
           g_mem_k, w_mem_o, w_out, g_ff, w_up, w_down):
    stage = int(os.environ.get("MK_STAGE", "99"))
    f = lambda a: np.ascontiguousarray(np.asarray(a, dtype=np.float32))
    x_prompt, mem_prompt, x_sample = f(x_prompt), f(mem_prompt), f(x_sample)
    cache_mla_latent, cache_mla_kpe, state_pool = f(cache_mla_latent), f(cache_mla_kpe), f(state_pool)
    cache_mem_k, cache_mem_v = f(cache_mem_k), f(cache_mem_v)
    if stage not in _PROG:
        _PROG[stage] = build_program(stage)
    nc = _PROG[stage]
    w_qb = f(w_qb)
    wq3 = w_qb.reshape(512, 16, 192)
    w_qbrot = np.ascontiguousarray(np.concatenate([wq3[:, :, 160:192], wq3[:, :, 128:160]], axis=-1).reshape(512, 1024))
    shared = dict(w_in=f(w_in), w_pool=f(w_pool), w_qb=w_qb, w_qbrot=w_qbrot, w_kb=f(w_kb), w_vb=f(w_vb),
                  w_mla_o=f(w_mla_o), w_mem_kv=f(w_mem_kv), w_mem_o=f(w_mem_o), w_out=f(w_out), w_up=f(w_up),
                  w_down=f(w_down), g_mix=f(g_mix), b_gate=f(b_gate), pool_scale=f(pool_scale), g_q_lat=f(g_q_lat),
                  g_q_head=f(g_q_head), g_kv_lat=f(g_kv_lat), g_k_head=f(g_k_head), g_mem=f(g_mem),
                  g_mem_q=f(g_mem_q), g_mem_k=f(g_mem_k), g_ff=f(g_ff))
    in_maps = []
    for c in range(8):
        b, j = c // 4, c % 4
        xs = x_prompt[b]
        xown = np.empty((TOWN, D), np.float32)
        xprev = np.zeros((128, D), np.float32)
        for s in range(8):
            i = 4 * s + j
            xown[s * 128:(s + 1) * 128] = xs[i * 128:(i + 1) * 128]
            if i > 0:
                xprev[16 * s:16 * s + 16] = xs[i * 128 - 16:i * 128]
        xown[1024:1056] = x_sample[2 * c]
        xown[1056:1088] = x_sample[2 * c + 1]
        sp = np.zeros((32, 1024), np.float32)
        sp[1:16] = state_pool[2 * c]
        sp[17:32] = state_pool[2 * c + 1]
        m = dict(xseq=xs, xown=xown, xprev=xprev,
                 latc=cache_mla_latent[2 * c:2 * c + 2], kpec=cache_mla_kpe[2 * c:2 * c + 2], spool=sp,
                 memp=mem_prompt[b], cmk=cache_mem_k[2 * c:2 * c + 2].reshape(2, 256, 1024),
                 cmv=cache_mem_v[2 * c:2 * c + 2].reshape(2, 256, 1024))
        m.update(shared)
        m.update(_consts(j))
        in_maps.append({k: np.ascontiguousarray(v) for k, v in m.items()})
    res = run_bass_kernel_spmd(nc, in_maps, core_ids=list(range(8)))
    R = res.results
    y_p = np.zeros((2, SEQ, D), np.float32)
    lat_p = np.zeros((2, SEQ, 512), np.float32)
    kpe_p = np.zeros((2, SEQ, 64), np.float32)
    y_s = np.zeros((16, 32, D), np.float32)
    lat_s = np.zeros((16, 32, 512), np.float32)
    kpe_s = np.zeros((16, 32, 64), np.float32)
    pool_p = np.zeros((2, 15, 1024), np.float32)
    pool_s = np.zeros((16, 15, 1024), np.float32)
    mem_k_p = np.zeros((2, 256, 4, 256), np.float32)
    mem_v_p = np.zeros((2, 256, 4, 256), np.float32)
    for c in range(8):
        b, j = c // 4, c % 4
        r = R[c]
        for s in range(8):
            i = 4 * s + j
            y_p[b, i * 128:(i + 1) * 128] = r["y_own"][s * 128:(s + 1) * 128]
            lat_p[b, i * 128:(i + 1) * 128] = r["lat_own"][s * 128:(s + 1) * 128]
            kpe_p[b, i * 128:(i + 1) * 128] = r["kpe_own"][s * 128:(s + 1) * 128]
        for bi in range(2):
            y_s[2 * c + bi] = r["y_own"][1024 + 32 * bi:1056 + 32 * bi]
            lat_s[2 * c + bi] = r["lat_own"][1024 + 32 * bi:1056 + 32 * bi]
            kpe_s[2 * c + bi] = r["kpe_own"][1024 + 32 * bi:1056 + 32 * bi]
            pool_s[2 * c + bi] = r["pools_out"][bi]
        if j == 3:
            pool_p[b] = r["poolp_out"]
        if j == 0:
            mem_k_p[b] = r["memk_out"].reshape(256, 4, 256)
            mem_v_p[b] = r["memv_out"].reshape(256, 4, 256)
    return (y_p, y_s, lat_p, kpe_p, pool_p, mem_k_p, mem_v_p, lat_s, kpe_s, pool_s)
```
